# Optimizing a Trainium2 kernel written in Bass

```python
import math
import jax, jax.numpy as jnp
from jax import lax
import numpy as np

D_MODEL = 1024
BATCH = 2
SEQ = 8192
DEPTH = 2
DEC_BATCH = 8
DEC_SEQ = 32
PAST_LEN = 2048

CHUNK = 64
Q_BLOCK = 128
POOL_WINDOWS = (2, 4, 8, 16)
POOL_GROUPS = 4
POOL_GROUP_DIM = 64
POOL_DIM = POOL_GROUPS * POOL_GROUP_DIM
POOL_STATE = max(POOL_WINDOWS) - 1
SB_HEADS = 4
SB_HEAD_DIM = 64
SB_DIM = SB_HEADS * SB_HEAD_DIM
DIFF_HEADS = 4
DIFF_HALF_DIM = 64
DIFF_HEAD_DIM = 2 * DIFF_HALF_DIM
DIFF_DIM = DIFF_HEADS * DIFF_HEAD_DIM
MIX_DIM = POOL_DIM + SB_DIM + DIFF_DIM
IN_DIM = POOL_DIM + 3 * SB_DIM + 3 * DIFF_DIM
IN_SPLITS = [POOL_DIM,
             POOL_DIM + SB_DIM,
             POOL_DIM + 2 * SB_DIM,
             POOL_DIM + 3 * SB_DIM,
             POOL_DIM + 3 * SB_DIM + DIFF_DIM,
             POOL_DIM + 3 * SB_DIM + 2 * DIFF_DIM]
MEM_LEN = 256
MEM_HEADS = 4
MEM_HEAD_DIM = D_MODEL // MEM_HEADS
D_FF = ((8 * D_MODEL // 3 + 255) // 256) * 256
EPS = 1e-6
NEG_INF = -1e30

kernel_name = "hybrid_pool_stickbreak_diffattn_stream_step"


def _rmsnorm(x, g):
    xf = x.astype(jnp.float32)
    y = xf * lax.rsqrt(jnp.mean(xf * xf, axis=-1, keepdims=True) + EPS)
    return (y * g.astype(jnp.float32)).astype(x.dtype)


def _pool_mix(u, buf, start_pos, pool_w, pool_scale):
    B, T, _ = u.shape
    full = jnp.concatenate([buf, u], axis=1).astype(jnp.float32)
    cs = jnp.concatenate([jnp.zeros_like(full[:, :1]), jnp.cumsum(full, axis=1)], axis=1)
    pos = start_pos + jnp.arange(T)
    means = []
    for g, w in enumerate(POOL_WINDOWS):
        sl = slice(g * POOL_GROUP_DIM, (g + 1) * POOL_GROUP_DIM)
        win = cs[:, POOL_STATE + 1:, sl] - cs[:, POOL_STATE + 1 - w:POOL_STATE + 1 - w + T, sl]
        cnt = jnp.minimum(w, pos + 1).astype(jnp.float32)[None, :, None]
        means.append(win / cnt)
    d = (jnp.concatenate(means, axis=-1) - u.astype(jnp.float32)).astype(u.dtype)
    d = d.reshape(B, T, POOL_GROUPS, POOL_GROUP_DIM)
    y = jnp.einsum('btgc,gcd->btgd', d, pool_w).reshape(B, T, POOL_DIM)
    return y * pool_scale


def _stick_breaking(q, k, v, q_pos, k_pos):
    z = jnp.einsum('bqhd,bkhd->bhqk', q.astype(jnp.float32), k.astype(jnp.float32)) * (SB_HEAD_DIM ** -0.5)
    mask = k_pos[None, :] < q_pos[:, None]
    log_keep = jnp.where(mask, jax.nn.log_sigmoid(-z), 0.0)
    suffix = lax.cumsum(log_keep, axis=3, reverse=True) - log_keep
    a = jnp.where(mask, jnp.exp(jax.nn.log_sigmoid(z) + suffix), 0.0)
    return jnp.einsum('bhqk,bkhd->bqhd', a, v.astype(jnp.float32))


def _diff_attention(q, k, v, q_pos, k_pos, lam):
    B, Tk = k.shape[:2]
    k = k.reshape(B, Tk, DIFF_HEADS, 2, DIFF_HALF_DIM)
    s = jnp.einsum('bqhid,bkhid->bihqk', q.astype(jnp.float32), k.astype(jnp.float32)) * (DIFF_HALF_DIM ** -0.5)
    mask = (k_pos[None, :] // CHUNK) <= (q_pos[:, None] // CHUNK)
    p = jax.nn.softmax(jnp.where(mask, s, NEG_INF), axis=-1)
    a = p[:, 0] - lam * p[:, 1]
    return jnp.einsum('bhqk,bkhe->bqhe', a, v.astype(jnp.float32))


def _sweep(fn, q, q_pos):
    B, S = q.shape[:2]
    nb = S // Q_BLOCK
    qb = q.reshape((B, nb, Q_BLOCK) + q.shape[2:]).swapaxes(0, 1)
    pb = q_pos.reshape(nb, Q_BLOCK)
    out = lax.map(lambda a: fn(a[0], a[1]), (qb, pb))
    return out.swapaxes(0, 1).reshape((B, S) + out.shape[3:])


def _mem_kv(mem, g, wk, wv):
    B, M, _ = mem.shape
    hm = _rmsnorm(mem, g)
    return ((hm @ wk).reshape(B, M, MEM_HEADS, MEM_HEAD_DIM),
            (hm @ wv).reshape(B, M, MEM_HEADS, MEM_HEAD_DIM))


def _mem_attend(h, mk, mv, wq, wo):
    B, T, _ = h.shape
    q = (h @ wq).reshape(B, T, MEM_HEADS, MEM_HEAD_DIM)
    s = jnp.einsum('bqhd,bkhd->bhqk', q.astype(jnp.float32), mk.astype(jnp.float32)) * (MEM_HEAD_DIM ** -0.5)
    p = jax.nn.softmax(s, axis=-1)
    o = jnp.einsum('bhqk,bkhd->bqhd', p, mv.astype(jnp.float32)).astype(h.dtype)
    return o.reshape(B, T, D_MODEL) @ wo


def _layer(x, pool_buf, sb_k_past, sb_v_past, diff_k_past, diff_v_past, mem_k, mem_v, lw, lam_init):
    (g_pre, g_post, w_in, w_out, pool_w, pool_scale, lam_q1, lam_k1, lam_q2, lam_k2,
     diff_g, wq_m, wo_m, w_gate, w_up, w_down) = lw
    B, T, _ = x.shape
    past = 0 if sb_k_past is None else sb_k_past.shape[1]
    q_pos = past + jnp.arange(T)
    k_pos = jnp.arange(past + T)

    h = _rmsnorm(x, g_pre[0])
    u, sq, sk, sv, dq, dk, dv = jnp.split(h @ w_in, IN_SPLITS, axis=-1)
    sq = sq.reshape(B, T, SB_HEADS, SB_HEAD_DIM)
    sk = sk.reshape(B, T, SB_HEADS, SB_HEAD_DIM)
    sv = sv.reshape(B, T, SB_HEADS, SB_HEAD_DIM)
    dq = dq.reshape(B, T, DIFF_HEADS, 2, DIFF_HALF_DIM)
    dk = dk.reshape(B, T, DIFF_HEADS, DIFF_HEAD_DIM)
    dv = dv.reshape(B, T, DIFF_HEADS, DIFF_HEAD_DIM)

    pool_out = _pool_mix(u, pool_buf, past, pool_w, pool_scale)
    new_pool = jnp.concatenate([pool_buf, u], axis=1)[:, -POOL_STATE:]

    if past:
        sk_all = jnp.concatenate([sb_k_past, sk], axis=1)
        sv_all = jnp.concatenate([sb_v_past, sv], axis=1)
        dk_all = jnp.concatenate([diff_k_past, dk], axis=1)
        dv_all = jnp.concatenate([diff_v_past, dv], axis=1)
    else:
        sk_all, sv_all, dk_all, dv_all = sk, sv, dk, dv

    lam = (jnp.exp(jnp.sum(lam_q1.astype(jnp.float32) * lam_k1.astype(jnp.float32)))
           - jnp.exp(jnp.sum(lam_q2.astype(jnp.float32) * lam_k2.astype(jnp.float32))) + lam_init)
    sb_fn = lambda qb, pb: _stick_breaking(qb, sk_all, sv_all, pb, k_pos)
    diff_fn = lambda qb, pb: _diff_attention(qb, dk_all, dv_all, pb, k_pos, lam)
    if T % Q_BLOCK == 0:
        sb_o = _sweep(sb_fn, sq, q_pos)
        diff_o = _sweep(diff_fn, dq, q_pos)
    else:
        sb_o = sb_fn(sq, q_pos)
        diff_o = diff_fn(dq, q_pos)
    diff_o = _rmsnorm(diff_o, diff_g) * (1.0 - lam_init)

    mix = jnp.concatenate([pool_out,
                           sb_o.reshape(B, T, SB_DIM).astype(x.dtype),
                           diff_o.reshape(B, T, DIFF_DIM).astype(x.dtype)], axis=-1)
    x = x + _rmsnorm(mix @ w_out, g_post[0])

    h = _rmsnorm(x, g_pre[1])
    x = x + _rmsnorm(_mem_attend(h, mem_k, mem_v, wq_m, wo_m), g_post[1])

    h = _rmsnorm(x, g_pre[2])
    f = (jax.nn.silu(h @ w_gate) * (h @ w_up)) @ w_down
    x = x + _rmsnorm(f, g_post[2])
    return x, sk, sv, dk, dv, new_pool


def setup_inputs(seed: int = 0) -> dict:
    key = jax.random.key(seed)
    ks = jax.random.split(key, 32)
    nrm = lambda k, shape, s: jax.random.normal(k, shape, jnp.float32) * s
    return {
        "x_prompt": nrm(ks[0], (BATCH, SEQ, D_MODEL), 1.0),
        "x_sample": nrm(ks[1], (DEC_BATCH, DEC_SEQ, D_MODEL), 1.0),
        "cache_sb_k": nrm(ks[2], (DEPTH, DEC_BATCH, PAST_LEN, SB_HEADS, SB_HEAD_DIM), 1.0),
        "cache_sb_v": nrm(ks[3], (DEPTH, DEC_BATCH, PAST_LEN, SB_HEADS, SB_HEAD_DIM), 1.0),
        "cache_diff_k": nrm(ks[4], (DEPTH, DEC_BATCH, PAST_LEN, DIFF_HEADS, DIFF_HEAD_DIM), 1.0),
        "cache_diff_v": nrm(ks[5], (DEPTH, DEC_BATCH, PAST_LEN, DIFF_HEADS, DIFF_HEAD_DIM), 1.0),
        "cache_mem_k": nrm(ks[6], (DEPTH, DEC_BATCH, MEM_LEN, MEM_HEADS, MEM_HEAD_DIM), 1.0),
        "cache_mem_v": nrm(ks[7], (DEPTH, DEC_BATCH, MEM_LEN, MEM_HEADS, MEM_HEAD_DIM), 1.0),
        "state_pool": nrm(ks[8], (DEPTH, DEC_BATCH, POOL_STATE, POOL_DIM), 1.0),
        "mem_prompt": nrm(ks[9], (BATCH, MEM_LEN, D_MODEL), 1.0),
        "g_pre": 1.0 + nrm(ks[10], (DEPTH, 3, D_MODEL), 0.02),
        "g_post": 1.0 + nrm(ks[11], (DEPTH, 3, D_MODEL), 0.02),
        "g_mem": 1.0 + nrm(ks[12], (DEPTH, D_MODEL), 0.02),
        "w_in": nrm(ks[13], (DEPTH, D_MODEL, IN_DIM), D_MODEL ** -0.5),
        "w_out": nrm(ks[14], (DEPTH, MIX_DIM, D_MODEL), MIX_DIM ** -0.5),
        "pool_w": nrm(ks[15], (DEPTH, POOL_GROUPS, POOL_GROUP_DIM, POOL_GROUP_DIM), POOL_GROUP_DIM ** -0.5),
        "pool_scale": 1.0 + nrm(ks[16], (DEPTH, POOL_DIM), 0.1),
        "lam_q1": nrm(ks[17], (DEPTH, DIFF_HALF_DIM), 0.1),
        "lam_k1": nrm(ks[18], (DEPTH, DIFF_HALF_DIM), 0.1),
        "lam_q2": nrm(ks[19], (DEPTH, DIFF_HALF_DIM), 0.1),
        "lam_k2": nrm(ks[20], (DEPTH, DIFF_HALF_DIM), 0.1),
        "diff_g": 1.0 + nrm(ks[21], (DEPTH, DIFF_HEAD_DIM), 0.02),
        "wq_m": nrm(ks[22], (DEPTH, D_MODEL, D_MODEL), D_MODEL ** -0.5),
        "wk_m": nrm(ks[23], (DEPTH, D_MODEL, D_MODEL), D_MODEL ** -0.5),
        "wv_m": nrm(ks[24], (DEPTH, D_MODEL, D_MODEL), D_MODEL ** -0.5),
        "wo_m": nrm(ks[25], (DEPTH, D_MODEL, D_MODEL), D_MODEL ** -0.5),
        "w_gate": nrm(ks[26], (DEPTH, D_MODEL, D_FF), D_MODEL ** -0.5),
        "w_up": nrm(ks[27], (DEPTH, D_MODEL, D_FF), D_MODEL ** -0.5),
        "w_down": nrm(ks[28], (DEPTH, D_FF, D_MODEL), D_FF ** -0.5),
    }


def reference(x_prompt, x_sample, cache_sb_k, cache_sb_v, cache_diff_k, cache_diff_v,
              cache_mem_k, cache_mem_v, state_pool, mem_prompt,
              g_pre, g_post, g_mem, w_in, w_out, pool_w, pool_scale,
              lam_q1, lam_k1, lam_q2, lam_k2, diff_g, wq_m, wk_m, wv_m, wo_m,
              w_gate, w_up, w_down):
    xp, xs = x_prompt, x_sample
    p_sbk, p_sbv, p_dk, p_dv, p_pool, p_mk, p_mv = [], [], [], [], [], [], []
    s_sbk, s_sbv, s_dk, s_dv, s_pool = [], [], [], [], []
    for li in range(DEPTH):
        lam_init = 0.8 - 0.6 * math.exp(-0.3 * li)
        lw = (g_pre[li], g_post[li], w_in[li], w_out[li], pool_w[li], pool_scale[li],
              lam_q1[li], lam_k1[li], lam_q2[li], lam_k2[li], diff_g[li],
              wq_m[li], wo_m[li], w_gate[li], w_up[li], w_down[li])
        mk, mv = _mem_kv(mem_prompt, g_mem[li], wk_m[li], wv_m[li])
        zero_buf = jnp.zeros((xp.shape[0], POOL_STATE, POOL_DIM), xp.dtype)
        xp, sk, sv, dk, dv, npool = _layer(xp, zero_buf, None, None, None, None, mk, mv, lw, lam_init)
        p_sbk.append(sk); p_sbv.append(sv); p_dk.append(dk); p_dv.append(dv)
        p_pool.append(npool); p_mk.append(mk); p_mv.append(mv)
        xs, sk, sv, dk, dv, npool = _layer(xs, state_pool[li], cache_sb_k[li], cache_sb_v[li],
                                           cache_diff_k[li], cache_diff_v[li],
                                           cache_mem_k[li], cache_mem_v[li], lw, lam_init)
        s_sbk.append(sk); s_sbv.append(sv); s_dk.append(dk); s_dv.append(dv); s_pool.append(npool)
    return (xp, xs,
            jnp.stack(p_sbk), jnp.stack(p_sbv), jnp.stack(p_dk), jnp.stack(p_dv),
            jnp.stack(p_pool), jnp.stack(p_mk), jnp.stack(p_mv),
            jnp.stack(s_sbk), jnp.stack(s_sbv), jnp.stack(s_dk), jnp.stack(s_dv),
            jnp.stack(s_pool))
```

```python
import math
import contextlib
import numpy as np
import concourse.bass as bass
import concourse.mybir as mybir
from concourse.bass_utils import run_bass_kernel_spmd

F32 = mybir.dt.float32
BF16 = mybir.dt.bfloat16
AF = mybir.ActivationFunctionType
ALU = mybir.AluOpType
AX = mybir.AxisListType

L = 2
D = 1024
KC = 8
SEQ = 8192
NS = 32
PAST = 2048
TOKP = 2048
NT = TOKP + NS
HC = SEQ + 4 * NS
DFF = 2816
FC = 22
EPS = 1e-6
WINDOWS = (2, 4, 8, 16)
GROUPS = [[0, 1, 2, 3], [4, 5, 6, 7]]
NG = 98
WSLOT = 4096
STOP_AFTER = None
DBG = ''


class _Stop(Exception):
    pass


class Buf:
    __slots__ = ("last_w", "readers")

    def __init__(self):
        self.last_w = None
        self.readers = {}


class Sched:
    def __init__(self, nc):
        self.nc = nc
        self.E = {"pe": nc.tensor, "act": nc.scalar, "dve": nc.vector, "pool": nc.gpsimd, "sp": nc.sync}
        self.esem = {}
        self.ecnt = {}
        for e in ("pe", "act", "dve", "pool"):
            self.esem[e] = nc.alloc_semaphore("se_" + e)
            self.ecnt[e] = 0
        self.csem = {}
        self.ccnt = {}
        self.known = {e: {} for e in self.E}
        self.pending = {e: None for e in self.E}
        self.nops = 0

    def buf(self):
        return Buf()

    def _waits(self, eng, reads, writes, skip_waw=False):
        need = {}

        def add(tok, raw):
            if tok is None:
                return
            if tok[0] == "e" and tok[1] == eng:
                if eng == "pe":
                    return
            k = (tok[0], tok[1])
            if need.get(k, 0) < tok[2]:
                need[k] = tok[2]

        for b in reads:
            add(b.last_w, True)
        for b in writes:
            if not skip_waw:
                add(b.last_w, False)
            for k, v in b.readers.items():
                add((k[0], k[1], v), False)
        if self.pending[eng] is not None:
            for tok in self.pending[eng]:
                if tok[0] == "e" and tok[1] == eng:
                    continue
                k = (tok[0], tok[1])
                if need.get(k, 0) < tok[2]:
                    need[k] = tok[2]
            self.pending[eng] = None
        kn = self.known[eng]
        for k, v in need.items():
            if kn.get(k, 0) >= v:
                continue
            kn[k] = v
            sem = self.esem[k[1]] if k[0] == "e" else self.csem[k[1]]
            self.E[eng].wait_ge(sem, v)

    def _record(self, tok, reads, writes):
        k = (tok[0], tok[1])
        for b in reads:
            if b.readers.get(k, 0) < tok[2]:
                b.readers[k] = tok[2]
        for b in writes:
            b.last_w = tok
            b.readers = {}

    def op(self, eng, fn, reads=(), writes=()):
        self._waits(eng, reads, writes)
        ins = fn(self.E[eng])
        self.ecnt[eng] += 1
        ins.then_inc(self.esem[eng], 1)
        self._record(("e", eng, self.ecnt[eng]), reads, writes)
        self.nops += 1

    def dma(self, q, chan, fn, reads=(), writes=(), skip_waw=False, inc=16):
        if chan not in self.csem:
            self.csem[chan] = self.nc.alloc_semaphore("sc_" + chan)
            self.ccnt[chan] = 0
        self._waits(q, reads, writes, skip_waw)
        ins = fn(self.E[q])
        self.ccnt[chan] += inc
        ins.then_inc(self.csem[chan], inc)
        self._record(("c", chan, self.ccnt[chan]), reads, writes)
        self.nops += 1

    def barrier(self, skip_prefix=None):
        toks = [("e", e, n) for e, n in self.ecnt.items() if n > 0]
        toks += [("c", c, n) for c, n in self.ccnt.items() if n > 0 and not (skip_prefix and c.startswith(skip_prefix))]
        for e in self.E:
            self.pending[e] = list(toks)

    def finish(self):
        self.barrier()
        self._waits("sp", (), ())


_SLOT_UID = [0]


class Slots:
    def __init__(self, nc, S, stack, name, n, shape, dt):
        _SLOT_UID[0] += 1
        self.t = [stack.enter_context(nc.sbuf_tensor(f"sl_{name}_{_SLOT_UID[0]}_{i}", list(shape), dt)) for i in range(n)]
        self.b = [S.buf() for _ in range(n)]
        self.i = 0
        self.n = n

    def next(self):
        i = self.i
        self.i = (i + 1) % self.n
        return self.t[i], self.b[i], i


class Banks:
    def __init__(self, nc, S, stack):
        self.t = [stack.enter_context(nc.psum_tensor(f"ps{i}", [128, 512], F32)) for i in range(8)]
        self.b = [S.buf() for _ in range(8)]
        self.pinned = set()
        self.rr = 0

    def get(self, pin=False):
        for _ in range(16):
            i = self.rr
            self.rr = (self.rr + 1) % 8
            if i not in self.pinned:
                if pin:
                    self.pinned.add(i)
                return i
        raise RuntimeError("no psum bank")

    def unpin(self, i):
        self.pinned.discard(i)


def build_program():
    nc = bass.Bass("TRN2", target_bir_lowering=False)
    S = Sched(nc)

    small = STOP_AFTER is not None and (STOP_AFTER >= 30 or STOP_AFTER in (1, 2, 3))
    BIG = ("w_out_p", "wq", "wk", "wv", "wo", "w_gate", "w_up", "w_down", "c_memk", "c_memv", "memp", "gmem_b")

    def din(name, shape):
        if small and name in BIG:
            shape = [1, 1]
        return nc.dram_tensor(name, list(shape), F32, kind="ExternalInput").ap()

    def dout(name, shape):
        return nc.dram_tensor(name, list(shape), F32, kind="ExternalOutput").ap()

    def dint(name, shape):
        return nc.dram_tensor(name, list(shape), F32).ap()

    xT_in = din("xT", [D, NT])
    xT_full = din("xT_full", [D, HC])
    w_in_h = din("w_in_h", [L * D, 640])
    pool_w_h = din("pool_w_h", [L * 64, 64])
    pvec_d = din("pvec", [64, 32])
    w_out_p = din("w_out_p", [L * D, D])
    wq_d = din("wq", [L * D, D])
    wk_d = din("wk", [L * D, D])
    wv_d = din("wv", [L * D, D])
    wo_d = din("wo", [L * D, D])
    wg_d = din("w_gate", [L * D, DFF])
    wu_d = din("w_up", [L * D, DFF])
    wd_d = din("w_down", [L * DFF, D])
    gvec_d = din("gvec", [128, NG])
    gmem_d = din("gmem_b", [L * 128, D])
    lamv_d = din("lamv", [64, 4 * L])
    c_sbkT = din("c_sbkT", [L * 4 * 64, PAST])
    c_sbv = din("c_sbv", [L * 4 * PAST, 64])
    c_dkT = din("c_dkT", [L * 4 * 128, PAST])
    c_dv = din("c_dv", [L * 4 * PAST, 128])
    spoolT = din("spoolT", [L * 4 * 64, 16])
    c_memk = din("c_memk", [L * 256, D])
    c_memv = din("c_memv", [L * 256, D])
    memp_d = din("memp", [256, D])
    ones_d = din("ones_c", [128, 128])
    tri_d = din("tri_c", [128, 128])
    ident_d = din("ident_c", [128, 128])
    ms_d = din("ms_c", [128, 2048])
    md_d = din("md_c", [128, 2048])
    yT = dout("yT", [D, NT])
    skT_o = dout("skT_o", [L * 64, HC])
    dkT_o = dout("dkT_o", [L * 128, HC])
    sv_o = dout("sv_o", [L * HC, 64])
    dv_o = dout("dv_o", [L * HC, 128])
    pool_o = dout("pool_o", [L * 64, 75])
    memk_o = dout("memk_o", [L * 256, D])
    memv_o = dout("memv_o", [L * 256, D])
    hT_src = [(dint(f"hT_src{l}", [8 * D, 256]), dint(f"hT_srcs{l}", [D, NS])) for l in range(L)]
    hT_all = [(dint(f"hT_all{l}", [8 * 4 * D, 256]), dint(f"hT_alls{l}", [4 * D, NS])) for l in range(L)]
    mixT_src = [(dint(f"mixT_src{l}", [8 * 256, 1024]), dint(f"mixT_srcs{l}", [256, 4 * NS])) for l in range(L)]
    mixT_all = [(dint(f"mixT_all{l}", [8 * 1024, 1024]), dint(f"mixT_alls{l}", [1024, 4 * NS])) for l in range(L)]
    xT_scr = dint("xT_scr", [D, NT])

    TILES = [(0, 512, "p"), (512, 512, "p"), (1024, 512, "p"), (1536, 512, "p"), (2048, NS, "s")]

    with contextlib.ExitStack() as g:
        uid = [0]

        def sbt(stack, name, shape, dt=F32):
            uid[0] += 1
            return stack.enter_context(nc.sbuf_tensor(f"sb_{name}_{uid[0]}", list(shape), dt))

        banks = Banks(nc, S, g)
        ps = banks.t
        pb = banks.b

        ones = sbt(g, "ones", [128, 128], BF16)
        ones_f = sbt(g, "ones_f", [128, 128], F32)
        triT = sbt(g, "triT", [128, 128], BF16)
        ident_f = sbt(g, "ident_f", [128, 128], F32)
        ms_sb = sbt(g, "ms_sb", [128, 2048], BF16)
        md_sb = sbt(g, "md_sb", [128, 2048], BF16)
        gvec = sbt(g, "gvec", [128, NG], F32)
        pvec_sb = sbt(g, "pvec_sb", [64, 32], F32)
        lamv = sbt(g, "lamv", [64, 4 * L], F32)
        eps_t = sbt(g, "eps_t", [128, 1], F32)
        one_t = sbt(g, "one_t", [128, 1], F32)
        gd_t = sbt(g, "gd_t", [128, L], F32)
        nlam_t = sbt(g, "nlam_t", [128, L], F32)
        Bc = S.buf()
        S.dma("pool", "const", lambda e: e.dma_start(out=ones[:], in_=ones_d), writes=[Bc], skip_waw=True)
        S.dma("pool", "const", lambda e: e.dma_start(out=triT[:], in_=tri_d), writes=[Bc], skip_waw=True)
        S.dma("pool", "const", lambda e: e.dma_start(out=ms_sb[:], in_=ms_d), writes=[Bc], skip_waw=True)
        S.dma("pool", "const", lambda e: e.dma_start(out=md_sb[:], in_=md_d), writes=[Bc], skip_waw=True)
        S.dma("sp", "const2", lambda e: e.dma_start(out=ones_f[:], in_=ones_d), writes=[Bc], skip_waw=True)
        S.dma("sp", "const2", lambda e: e.dma_start(out=ident_f[:], in_=ident_d), writes=[Bc], skip_waw=True)
        S.dma("sp", "const2", lambda e: e.dma_start(out=gvec[:], in_=gvec_d), writes=[Bc], skip_waw=True)
        S.dma("sp", "const2", lambda e: e.dma_start(out=pvec_sb[:], in_=pvec_d), writes=[Bc], skip_waw=True)
        S.dma("sp", "const2", lambda e: e.dma_start(out=lamv[:], in_=lamv_d), writes=[Bc], skip_waw=True)
        S.op("dve", lambda e: e.memset(eps_t[:], EPS), writes=[Bc])
        S.op("dve", lambda e: e.memset(one_t[:], 1.0), writes=[Bc])
        S.barrier()
        with contextlib.ExitStack() as st0:
            prods = sbt(st0, "prods", [64, 2 * L], F32)
            ev = sbt(st0, "ev", [128, 2 * L], F32)
            Bp = S.buf()
            for l in range(L):
                S.op("dve", lambda e: e.tensor_tensor(out=prods[:, 2 * l:2 * l + 1], in0=lamv[:, 4 * l:4 * l + 1],
                                                      in1=lamv[:, 4 * l + 1:4 * l + 2], op=ALU.mult), writes=[Bp])
                S.op("dve", lambda e: e.tensor_tensor(out=prods[:, 2 * l + 1:2 * l + 2], in0=lamv[:, 4 * l + 2:4 * l + 3],
                                                      in1=lamv[:, 4 * l + 3:4 * l + 4], op=ALU.mult), writes=[Bp])
            bi = banks.get()
            S.op("pe", lambda e: e.matmul(ps[bi][:, 0:2 * L], lhsT=ones_f[0:64, :], rhs=prods[:, :], start=True, stop=True),
                 reads=[Bp], writes=[pb[bi]])
            Be = S.buf()
            S.op("act", lambda e: e.activation(out=ev[:], in_=ps[bi][:, 0:2 * L], func=AF.Exp), reads=[pb[bi]], writes=[Be])
            for l in range(L):
                lam_init = 0.8 - 0.6 * math.exp(-0.3 * l)
                S.op("dve", lambda e: e.tensor_tensor(out=nlam_t[:, l:l + 1], in0=ev[:, 2 * l + 1:2 * l + 2],
                                                      in1=ev[:, 2 * l:2 * l + 1], op=ALU.subtract), reads=[Be], writes=[Bc])
                S.op("dve", lambda e: e.tensor_scalar(out=nlam_t[:, l:l + 1], in0=nlam_t[:, l:l + 1], scalar1=-lam_init,
                                                      scalar2=None, op0=ALU.add), reads=[Bc], writes=[Bc])
                S.op("dve", lambda e: e.tensor_scalar(out=gd_t[:, l:l + 1], in0=gvec[:, 96 + l:97 + l], scalar1=1.0 - lam_init,
                                                      scalar2=None, op0=ALU.mult), writes=[Bc])
            S.barrier()

        pid4 = nc.gpsimd.partition_id() % 4
        dyn_off = {}
        for c0_ in (0, 1024):
            dyn_off[c0_] = nc.gpsimd.snap(pid4 * 2048 + c0_, min_val=c0_, max_val=3 * 2048 + c0_)
        dyn_off[2048] = nc.gpsimd.snap(pid4 * NS, min_val=0, max_val=3 * NS)

        def gpre(l, i):
            return (l * 3 + i) * 8

        def gpost(l, i):
            return 48 + (l * 3 + i) * 8

        ag_bufs = {}
        ag_byq = {}

        def ag_chunk(chan, src_ap, dst_ap, reads):
            b_ = S.buf()
            ag_bufs.setdefault(chan.split("_")[0], []).append(b_)
            ag_byq[chan] = b_
            S.dma("pool", chan, lambda e: e.collective_compute("AllGather", ALU.bypass, replica_groups=GROUPS, ins=[src_ap], outs=[dst_ap]),
                  reads=reads, writes=[b_], inc=1)

        def token_phase(l, x_src, x_dst, do_sub, next_g, hT_dst, mix_all, hT_gat=None, ag_l=0):
            Bhs = S.buf()
            with contextlib.ExitStack() as sc:
                wsl = Slots(nc, S, sc, "wsl", 4, [128, WSLOT], BF16)
                stat = {"bank": None}
                if do_sub:
                    mkT_p = sbt(sc, "mkT_p", [128, 8, 256], BF16)
                    mv_p = sbt(sc, "mv_p", [128, 2, D], BF16)
                    mkT_s = sbt(sc, "mkT_s", [128, 8, 256], BF16)
                    mv_s = sbt(sc, "mv_s", [128, 2, D], BF16)
                    Bmk = S.buf()

                def wload(W, r0, kcn, c0, ncols):
                    t, b, i = wsl.next()
                    view = t[:, 0:kcn * ncols].rearrange("p (k n) -> p k n", k=kcn)
                    src = W[r0:r0 + kcn * 128, c0:c0 + ncols].rearrange("(k p) n -> p k n", p=128)
                    S.dma("pool", f"w{i}", lambda e: e.dma_start(out=view, in_=src), writes=[b])
                    return view, b

                deferred = []

                def run_blocks(blocks, prefetch=3):
                    loaded = []
                    nxt = 0
                    for i, blk in enumerate(blocks):
                        while nxt < len(blocks) and nxt < i + prefetch:
                            W, r0, kcn, c0, ncols, _ = blocks[nxt]
                            loaded.append(wload(W, r0, kcn, c0, ncols))
                            nxt += 1
                        view, b = loaded[i]
                        blk[5](view, b)
                        for d_ in list(deferred):
                            d_[0] -= 1
                            if d_[0] <= 0:
                                deferred.remove(d_)
                                d_[1]()
                    for d_ in list(deferred):
                        deferred.remove(d_)
                        d_[1]()

                if do_sub:
                    with contextlib.ExitStack() as sm:
                        mp = sbt(sm, "mp", [128, 2, D], F32)
                        Bmp = S.buf()
                        sqt = sbt(sm, "sqt", [128, D], F32)
                        ss = sbt(sm, "ss", [128, 2], F32)
                        rm = sbt(sm, "rm", [128, 2], F32)
                        gm = sbt(sm, "gm", [128, D], F32)
                        hm = sbt(sm, "hm", [128, 2, D], F32)
                        Bhm = S.buf()
                        hmT = sbt(sm, "hmT", [128, 8, 256], BF16)
                        BhmT = S.buf()
                        mkn = sbt(sm, "mkn", [128, 2, D], F32)
                        Bmkn = S.buf()
                        mvn = sbt(sm, "mvn", [128, 2, D], F32)
                        Bmvn = S.buf()
                        ck = sbt(sm, "ck", [128, 2, D], F32)
                        Bck = S.buf()
                        S.dma("sp", "me", lambda e: e.dma_start(out=mp[:], in_=memp_d.rearrange("(i p) d -> p i d", p=128)), writes=[Bmp])
                        Bgm = S.buf()
                        S.dma("sp", "me2", lambda e: e.dma_start(out=gm[:], in_=gmem_d[l * 128:(l + 1) * 128, :]), writes=[Bgm])
                        S.dma("sp", "ck", lambda e: e.dma_start(out=ck[:], in_=c_memk[l * 256:(l + 1) * 256, :].rearrange("(i p) d -> p i d", p=128)),
                              writes=[Bck])
                        S.dma("pool", "cv", lambda e: e.dma_start(out=mv_s[:], in_=c_memv[l * 256:(l + 1) * 256, :].rearrange("(i p) d -> p i d", p=128)),
                              writes=[S.buf()])
                        Bss = S.buf()
                        for i in range(2):
                            S.op("dve", lambda e: e.tensor_tensor(out=sqt[:], in0=mp[:, i, :], in1=mp[:, i, :], op=ALU.mult), reads=[Bmp], writes=[Bss])
                            S.op("dve", lambda e: e.reduce_sum(out=ss[:, i:i + 1], in_=sqt[:], axis=AX.X), reads=[Bss], writes=[Bss])
                        S.op("act", lambda e: e.activation(out=rm[:], in_=ss[:], func=AF.Ln, scale=1.0 / D, bias=eps_t[:, 0:1]), reads=[Bss], writes=[Bss])
                        S.op("act", lambda e: e.activation(out=rm[:], in_=rm[:], func=AF.Exp, scale=-0.5), reads=[Bss], writes=[Bss])
                        for i in range(2):
                            S.op("dve", lambda e: e.scalar_tensor_tensor(out=hm[:, i, :], in0=mp[:, i, :], scalar=rm[:, i:i + 1], in1=gm[:],
                                                                         op0=ALU.mult, op1=ALU.mult), reads=[Bmp, Bss, Bgm], writes=[Bhm])

                        def transpose_to(src, srcb, dst, dstb):
                            for c8 in range(8):
                                bi = banks.get()
                                for i in range(2):
                                    S.op("pe", lambda e: e.transpose(ps[bi][:, i * 128:(i + 1) * 128], src[:, i, c8 * 128:(c8 + 1) * 128], ident_f[:]),
                                         reads=[srcb], writes=[pb[bi]])
                                S.op("act", lambda e: e.activation(out=dst[:, c8, :], in_=ps[bi][:, 0:256], func=AF.Copy), reads=[pb[bi]], writes=[dstb])

                        transpose_to(hm, Bhm, hmT, BhmT)
                        transpose_to(ck, Bck, mkT_s, Bmk)
                        blocks = []

                        def mk_block(dstn, dstb, cb):
                            def f(view, wb):
                                for i in range(2):
                                    bi = banks.get()
                                    for kc in range(8):
                                        S.op("pe", lambda e: e.matmul(ps[bi][:, :], lhsT=hmT[:, kc, i * 128:(i + 1) * 128], rhs=view[:, kc, :],
                                                                      start=(kc == 0), stop=(kc == 7)), reads=[BhmT, wb], writes=[pb[bi]])
                                    S.op("act", lambda e: e.activation(out=dstn[:, i, cb * 512:(cb + 1) * 512], in_=ps[bi][:, :], func=AF.Copy),
                                         reads=[pb[bi]], writes=[dstb])
                            return f
                        for cb in range(2):
                            blocks.append((wk_d, l * D, 8, cb * 512, 512, mk_block(mkn, Bmkn, cb)))
                        for cb in range(2):
                            blocks.append((wv_d, l * D, 8, cb * 512, 512, mk_block(mvn, Bmvn, cb)))
                        run_blocks(blocks)
                        S.dma("sp", "mo0", lambda e: e.dma_start(out=memk_o[l * 256:(l + 1) * 256, :].rearrange("(i p) d -> p i d", p=128), in_=mkn[:]),
                              reads=[Bmkn])
                        S.dma("sp", "mo1", lambda e: e.dma_start(out=memv_o[l * 256:(l + 1) * 256, :].rearrange("(i p) d -> p i d", p=128), in_=mvn[:]),
                              reads=[Bmvn])
                        transpose_to(mkn, Bmkn, mkT_p, Bmk)
                        for i in range(2):
                            S.op("dve", lambda e: e.tensor_copy(out=mv_p[:, i, :], in_=mvn[:, i, :]), reads=[Bmvn], writes=[Bmk])
                        S.barrier(skip_prefix="ag")

                def make_ctx(W, tiles_, tag):
                    stat = {"bank": None}
                    sg_bufs = {}
                    Bhs = S.buf()
                    xT = sbt(sc, "xTt", [128, 8, W], F32)
                    Bx = [S.buf() for _ in range(8)]
                    osb = sbt(sc, "osb", [128, 8, W], F32)
                    Bo = [S.buf() for _ in range(8)]
                    hT = sbt(sc, "hTt", [128, 8, W], BF16)
                    Bh = [S.buf() for _ in range(8)]
                    sqs = Slots(nc, S, sc, "sqs", 2, [128, W], BF16)
                    lnt = sbt(sc, "lnt", [128, W], F32)
                    Bln = S.buf()
                    rstd = sbt(sc, "rstd", [128, W], F32)
                    Brs = S.buf()
                    if do_sub:
                        mixT = sbt(sc, "mixTt", [128, 8, W], BF16)
                        Bm = S.buf()
                        qT = sbt(sc, "qTt", [128, 8, W], BF16)
                        Bq = [S.buf() for _ in range(8)]
                        oT = sbt(sc, "oTt", [128, 8, W], BF16)
                        Boo = [S.buf() for _ in range(8)]
                        actT = sbt(sc, "actT", [128, FC, W], BF16)
                        Ba = [S.buf() for _ in range(FC)]
                        Psl = Slots(nc, S, sc, "Psl", 2, [128, 2, W], BF16)
                        rdn = Slots(nc, S, sc, "rdn", 2, [128, W], F32)
                        sgs = Slots(nc, S, sc, "sgs", 2, [128, 4, W], F32)
                    def stats_add(src_ap, tt, n, src_bufs):
                        sq, sqb, _ = sqs.next()
                        S.op("act", lambda e: e.activation(out=sq[:, :tt], in_=src_ap, func=AF.Square), reads=src_bufs, writes=[sqb])
                        if n == 0:
                            stat["bank"] = banks.get(pin=True)
                        bi = stat["bank"]
                        S.op("pe", lambda e: e.matmul(ps[bi][:, :tt], lhsT=ones[:, :], rhs=sq[:, :tt], start=(n == 0), stop=(n == 7)),
                             reads=[sqb], writes=[pb[bi]])

                    def rstd_compute(tt):
                        bi = stat["bank"]
                        S.op("act", lambda e: e.activation(out=lnt[:, :tt], in_=ps[bi][:, :tt], func=AF.Ln, scale=1.0 / D, bias=eps_t[:, 0:1]),
                             reads=[pb[bi]], writes=[Bln])
                        S.op("act", lambda e: e.activation(out=rstd[:, :tt], in_=lnt[:, :tt], func=AF.Exp, scale=-0.5),
                             reads=[Bln], writes=[Brs])
                        banks.unpin(bi)
                        stat["bank"] = None

                    def sub_out(n, bi, tt):
                        S.op("act", lambda e: e.activation(out=osb[:, n, :tt], in_=ps[bi][:, :tt], func=AF.Copy), reads=[pb[bi]], writes=[Bo[n]])
                        stats_add(ps[bi][:, :tt], tt, n, [pb[bi]])

                    def post_norm(gb, tt):
                        rstd_compute(tt)
                        for n in range(8):
                            S.op("dve", lambda e: e.tensor_tensor(out=osb[:, n, :tt], in0=osb[:, n, :tt], in1=rstd[:, :tt], op=ALU.mult),
                                 reads=[Bo[n], Brs], writes=[Bo[n]])
                        for n in range(8):
                            S.op("dve", lambda e: e.scalar_tensor_tensor(out=xT[:, n, :tt], in0=osb[:, n, :tt], scalar=gvec[:, gb + n:gb + n + 1],
                                                                         in1=xT[:, n, :tt], op0=ALU.mult, op1=ALU.add),
                                 reads=[Bo[n], Bx[n]], writes=[Bx[n]])

                    def pre_norm(gb, tt, dst, dstb):
                        for n in range(8):
                            stats_add(xT[:, n, :tt], tt, n, [Bx[n]])
                        rstd_compute(tt)
                        for n in range(8):
                            S.op("dve", lambda e: e.scalar_tensor_tensor(out=dst[:, n, :tt], in0=xT[:, n, :tt], scalar=gvec[:, gb + n:gb + n + 1],
                                                                         in1=rstd[:, :tt], op0=ALU.mult, op1=ALU.mult),
                                 reads=[Bx[n], Brs], writes=[dstb[n]])

                    blocks = []
                    for (c0, tt, kind) in tiles_:
                        def load_tile(c0=c0, tt=tt, kind=kind):
                            S.dma("sp", "xT" + tag, lambda e: e.dma_start(out=xT[:, :, :tt], in_=x_src[:, c0:c0 + tt].rearrange("(k p) n -> p k n", p=128)),
                                  writes=Bx)
                            if do_sub:
                                if kind == "p":
                                    off = dyn_off[(c0 // 1024) * 1024]
                                    cc = c0 % 1024
                                    src = mix_all[0][bass.ds(off, 1024), cc:cc + tt].rearrange("(c p) n -> p c n", p=128)
                                else:
                                    src = mix_all[1].rearrange("(c p) n -> p c n", p=128)[:, :, bass.ds(dyn_off[2048], tt)]
                                S.dma("pool", "mx" + tag, lambda e: e.dma_start(out=mixT[:, :, :tt], in_=src), reads=ag_bufs.get(f"agm{l}", []), writes=[Bm])

                        def end_tile(c0=c0, tt=tt):
                            if next_g is not None:
                                pre_norm(next_g, tt, osb, Bo)
                                if tt == 512:
                                    for hf in range(2):
                                        q = c0 // 256 + hf
                                        S.dma("sp", "hs" + tag, lambda e: e.dma_start(out=hT_dst[0][q * D:(q + 1) * D, :].rearrange("(k p) n -> p k n", p=128),
                                                                                in_=osb[:, :, 256 * hf:256 * (hf + 1)]), reads=Bo, writes=[Bhs], skip_waw=True)
                                    for hf in range(2):
                                        q = c0 // 256 + hf
                                        deferred.append([3, lambda q=q: ag_chunk(f"agh{ag_l}_{q}", hT_dst[0][q * D:(q + 1) * D, :], hT_gat[0][q * 4 * D:(q + 1) * 4 * D, :], [Bhs])])
                                else:
                                    S.dma("sp", "hs" + tag, lambda e: e.dma_start(out=hT_dst[1].rearrange("(k p) n -> p k n", p=128), in_=osb[:, :, :tt]),
                                          reads=Bo, writes=[Bhs], skip_waw=True)
                                    deferred.append([3, lambda: ag_chunk(f"agh{ag_l}_8", hT_dst[1], hT_gat[1], [Bhs])])
                            if x_dst is not None:
                                S.dma("sp", "xs" + tag, lambda e: e.dma_start(out=x_dst[:, c0:c0 + tt].rearrange("(k p) n -> p k n", p=128), in_=xT[:, :, :tt]),
                                      reads=Bx)

                        if not do_sub:
                            load_tile()
                            end_tile()
                            continue

                        first = [True]

                        def proj_block(rhsT, rhsb, cb, tt, consume, after=None, pre=None):
                            def f(view, wb):
                                if pre is not None:
                                    pre()
                                for nl in range(4):
                                    n = cb * 4 + nl
                                    bi = banks.get()
                                    for kc in range(8):
                                        S.op("pe", lambda e: e.matmul(ps[bi][:, :tt], lhsT=view[:, kc, nl * 128:(nl + 1) * 128], rhs=rhsT[:, kc, :tt],
                                                                      start=(kc == 0), stop=(kc == 7)), reads=[rhsb[kc], wb], writes=[pb[bi]])
                                    consume(n, bi)
                                if after is not None:
                                    after()
                            return f

                        def attn_core(tt=tt, kind=kind):
                            mkT = mkT_p if kind == "p" else mkT_s
                            mvv = mv_p if kind == "p" else mv_s
                            for hm_ in range(4):
                                sc_ = [banks.get(), banks.get()]
                                for mc in range(2):
                                    for dc in range(2):
                                        S.op("pe", lambda e: e.matmul(ps[sc_[mc]][:, :tt], lhsT=mkT[:, 2 * hm_ + dc, mc * 128:(mc + 1) * 128],
                                                                      rhs=qT[:, 2 * hm_ + dc, :tt], start=(dc == 0), stop=(dc == 1)),
                                             reads=[Bmk, Bq[2 * hm_ + dc]], writes=[pb[sc_[mc]]])
                                P, Pb, _ = Psl.next()
                                for mc in range(2):
                                    S.op("act", lambda e: e.activation(out=P[:, mc, :tt], in_=ps[sc_[mc]][:, :tt], func=AF.Exp, scale=1.0 / 16.0),
                                         reads=[pb[sc_[mc]]], writes=[Pb])
                                dn = banks.get()
                                for mc in range(2):
                                    S.op("pe", lambda e: e.matmul(ps[dn][:, :tt], lhsT=ones[:, :], rhs=P[:, mc, :tt], start=(mc == 0), stop=(mc == 1)),
                                         reads=[Pb], writes=[pb[dn]])
                                ob = [banks.get(), banks.get()]
                                for dc in range(2):
                                    for mc in range(2):
                                        S.op("pe", lambda e: e.matmul(ps[ob[dc]][:, :tt], lhsT=mvv[:, mc, hm_ * 256 + dc * 128:hm_ * 256 + (dc + 1) * 128],
                                                                      rhs=P[:, mc, :tt], start=(mc == 0), stop=(mc == 1)),
                                             reads=[Bmk, Pb], writes=[pb[ob[dc]]])
                                rd, rdb, _ = rdn.next()
                                S.op("dve", lambda e: e.reciprocal(out=rd[:, :tt], in_=ps[dn][:, :tt]), reads=[pb[dn]], writes=[rdb])
                                for dc in range(2):
                                    S.op("dve", lambda e: e.tensor_tensor(out=oT[:, 2 * hm_ + dc, :tt], in0=ps[ob[dc]][:, :tt], in1=rd[:, :tt], op=ALU.mult),
                                         reads=[pb[ob[dc]], rdb], writes=[Boo[2 * hm_ + dc]])

                        for cb in range(2):
                            blocks.append((w_out_p, l * D, 8, cb * 512, 512,
                                           proj_block(mixT, [Bm] * 8, cb, tt, lambda n, bi, tt=tt: sub_out(n, bi, tt),
                                                      after=(lambda tt=tt: (post_norm(gpost(l, 0), tt), pre_norm(gpre(l, 1), tt, hT, Bh))) if cb == 1 else None,
                                                      pre=load_tile if cb == 0 else None)))
                        def q_consume(n, bi, tt=tt):
                            S.op("act", lambda e: e.activation(out=qT[:, n, :tt], in_=ps[bi][:, :tt], func=AF.Copy), reads=[pb[bi]], writes=[Bq[n]])
                        for cb in range(2):
                            blocks.append((wq_d, l * D, 8, cb * 512, 512,
                                           proj_block(hT, Bh, cb, tt, q_consume, after=attn_core if cb == 1 else None)))
                        for cb in range(2):
                            blocks.append((wo_d, l * D, 8, cb * 512, 512,
                                           proj_block(oT, Boo, cb, tt, lambda n, bi, tt=tt: sub_out(n, bi, tt),
                                                      after=(lambda tt=tt: (post_norm(gpost(l, 1), tt), pre_norm(gpre(l, 2), tt, hT, Bh))) if cb == 1 else None)))
                        gslot = {}
                        for cb in range(6):
                            ncols = 512 if cb < 5 else 256

                            def gate_f(view, wb, cb=cb, ncols=ncols, tt=tt):
                                sg, _sgb0, si_ = sgs.next()
                                sgb = sg_bufs.setdefault(si_, [S.buf() for _ in range(4)])
                                gslot[cb] = (sg, sgb)
                                for jl in range(ncols // 128):
                                    bi = banks.get()
                                    for kc in range(8):
                                        S.op("pe", lambda e: e.matmul(ps[bi][:, :tt], lhsT=view[:, kc, jl * 128:(jl + 1) * 128], rhs=hT[:, kc, :tt],
                                                                      start=(kc == 0), stop=(kc == 7)), reads=[Bh[kc], wb], writes=[pb[bi]])
                                    S.op("act", lambda e: e.activation(out=sg[:, jl, :tt], in_=ps[bi][:, :tt], func=AF.Silu), reads=[pb[bi]], writes=[sgb[jl]])

                            def up_f(view, wb, cb=cb, ncols=ncols, tt=tt):
                                sg, sgb = gslot[cb]
                                for jl in range(ncols // 128):
                                    j = cb * 4 + jl
                                    bi = banks.get()
                                    for kc in range(8):
                                        S.op("pe", lambda e: e.matmul(ps[bi][:, :tt], lhsT=view[:, kc, jl * 128:(jl + 1) * 128], rhs=hT[:, kc, :tt],
                                                                      start=(kc == 0), stop=(kc == 7)), reads=[Bh[kc], wb], writes=[pb[bi]])
                                    S.op("dve", lambda e: e.tensor_tensor(out=actT[:, j, :tt], in0=ps[bi][:, :tt], in1=sg[:, jl, :tt], op=ALU.mult),
                                         reads=[pb[bi], sgb[jl]], writes=[Ba[j]])
                            blocks.append((wg_d, l * D, 8, cb * 512, ncols, gate_f))
                            blocks.append((wu_d, l * D, 8, cb * 512, ncols, up_f))
                        for n in range(8):
                            def down_f(view, wb, n=n, tt=tt, end_tile=end_tile):
                                bi = banks.get()
                                for j in range(FC):
                                    S.op("pe", lambda e: e.matmul(ps[bi][:, :tt], lhsT=view[:, j, :], rhs=actT[:, j, :tt],
                                                                  start=(j == 0), stop=(j == FC - 1)), reads=[Ba[j], wb], writes=[pb[bi]])
                                sub_out(n, bi, tt)
                                if n == 7:
                                    post_norm(gpost(l, 2), tt)
                                    end_tile()
                            blocks.append((wd_d, l * DFF, FC, n * 128, 128, down_f))
                    return blocks

                if do_sub:
                    bm = make_ctx(512, TILES[:4], "m")
                    bs = make_ctx(NS, TILES[4:], "s")
                    nz = len(bs)
                    merged = bm[:len(bm) - nz]
                    for k_ in range(nz):
                        a_, b_ = bm[len(bm) - nz + k_], bs[k_]
                        assert a_[:5] == b_[:5]
                        merged.append(a_[:5] + ((lambda view, wb, fa=a_[5], fb=b_[5]: (fa(view, wb), fb(view, wb))),))
                    run_blocks(merged)
                else:
                    make_ctx(512, TILES, "m")
                S.barrier(skip_prefix="ag")

        def head_phase(l):
            hall = hT_all[l]

            def mix_dst(row0, nrows, col, n):
                if col < SEQ:
                    q, cc = col // 1024, col % 1024
                    return mixT_src[l][0][q * 256 + row0:q * 256 + row0 + nrows, cc:cc + n]
                return mixT_src[l][1][row0:row0 + nrows, col - SEQ:col - SEQ + n]
            with contextlib.ExitStack() as sc:
                sqT = sbt(sc, "sqT", [64, HC], BF16)
                skTn = sbt(sc, "skTn", [64, HC], BF16)
                dqT = sbt(sc, "dqT", [128, HC], BF16)
                dkT = sbt(sc, "dkT", [128, HC], BF16)
                sv = sbt(sc, "svr", [128, 68, 64], BF16)
                dv = sbt(sc, "dvr", [128, 68, 128], BF16)
                with contextlib.ExitStack() as sp:
                    w_sb = sbt(sp, "w_sb", [128, 8, 640], BF16)
                    Bw = S.buf()
                    pw_sb = sbt(sp, "pw_sb", [64, 64], BF16)
                    hTs = Slots(nc, S, sp, "hTs", 2, [128, 8, 512], BF16)
                    kst = Slots(nc, S, sp, "kst", 2, [128, 512], F32)
                    vst = Slots(nc, S, sp, "vst", 2, [128, 4, 192], F32)
                    uts = Slots(nc, S, sp, "uts", 2, [64, 528], F32)
                    us = sbt(sp, "us", [64, 4, 48], F32)
                    Bus = S.buf()
                    s1 = sbt(sp, "s1", [64, 528], F32)
                    s2 = sbt(sp, "s2", [64, 528], F32)
                    s3 = sbt(sp, "s3", [64, 528], F32)
                    s4 = sbt(sp, "s4", [64, 528], F32)
                    acc = sbt(sp, "acc", [64, 512], F32)
                    dT = sbt(sp, "dT", [64, 512], BF16)
                    Bsc = S.buf()
                    mps = Slots(nc, S, sp, "mps", 2, [64, 512], F32)
                    S.dma("pool", "wi", lambda e: e.dma_start(out=w_sb[:], in_=w_in_h[l * D:(l + 1) * D, :].rearrange("(k p) n -> p k n", p=128)),
                          writes=[Bw])
                    Bpw = S.buf()
                    S.dma("pool", "wi2", lambda e: e.dma_start(out=pw_sb[:], in_=pool_w_h[l * 64:(l + 1) * 64, :]), writes=[Bpw])
                    S.op("dve", lambda e: e.memset(us[:], 0.0), writes=[Bus])
                    if 'c' not in DBG:
                      S.dma("sp", "spl", lambda e: e.dma_start(out=us[:, :, 1:16],
                                                             in_=spoolT[l * 256:(l + 1) * 256, 0:15].rearrange("(r p) n -> p r n", p=64)),
                            writes=[Bus])

                    if l == 0:
                        xfs = Slots(nc, S, sp, "xfs", 2, [128, 8, 512], F32)
                        nsq = Slots(nc, S, sp, "nsq", 2, [128, 512], BF16)
                        nln = sbt(sp, "nln", [128, 512], F32)
                        nrs = sbt(sp, "nrs", [128, 512], F32)
                        Bnr = S.buf()

                    def load_x(i):
                        xf, xb_, si = xfs.next()
                        nt_ = 512 if i < 16 else 128
                        S.dma("sp", f"xf{si}", lambda e: e.dma_start(out=xf[:, :, :nt_], in_=xT_full[:, 512 * i:512 * i + nt_].rearrange("(k p) n -> p k n", p=128)),
                              writes=[xb_])
                        return xf, xb_, nt_

                    def norm_x(xl):
                        xf, xb_, nt_ = xl
                        t, b, si = hTs.next()
                        bi = banks.get(pin=True)
                        for n in range(8):
                            sq, sqb, _ = nsq.next()
                            S.op("act", lambda e: e.activation(out=sq[:, :nt_], in_=xf[:, n, :nt_], func=AF.Square), reads=[xb_], writes=[sqb])
                            S.op("pe", lambda e: e.matmul(ps[bi][:, :nt_], lhsT=ones[:, :], rhs=sq[:, :nt_], start=(n == 0), stop=(n == 7)),
                                 reads=[sqb], writes=[pb[bi]])
                        S.op("act", lambda e: e.activation(out=nln[:, :nt_], in_=ps[bi][:, :nt_], func=AF.Ln, scale=1.0 / D, bias=eps_t[:, 0:1]),
                             reads=[pb[bi]], writes=[Bnr])
                        banks.unpin(bi)
                        S.op("act", lambda e: e.activation(out=nrs[:, :nt_], in_=nln[:, :nt_], func=AF.Exp, scale=-0.5), reads=[Bnr], writes=[Bnr])
                        gb = gpre(0, 0)
                        for n in range(8):
                            S.op("dve", lambda e: e.scalar_tensor_tensor(out=t[:, n, :nt_], in0=xf[:, n, :nt_], scalar=gvec[:, gb + n:gb + n + 1],
                                                                         in1=nrs[:, :nt_], op0=ALU.mult, op1=ALU.mult),
                                 reads=[xb_, Bnr], writes=[b])
                        return t, b

                    def load_h(i):
                        t, b, si = hTs.next()
                        if i < 16:
                            r, c0 = i // 4, (i % 4) * 512
                            for hf in range(2):
                                q = c0 // 256 + hf
                                row = q * 4 * D + r * D
                                S.dma("pool", f"hT{si}", lambda e: e.dma_start(
                                    out=t[:, :, 256 * hf:256 * (hf + 1)], in_=hall[0][row:row + D, :].rearrange("(k p) n -> p k n", p=128)),
                                    reads=[ag_byq[f"agh{l}_{q}"]], writes=[b], skip_waw=True)
                        else:
                            for r in range(4):
                                S.dma("pool", f"hT{si}", lambda e: e.dma_start(
                                    out=t[:, :, 32 * r:32 * r + 32], in_=hall[1][r * D:(r + 1) * D, :].rearrange("(k p) n -> p k n", p=128)),
                                    reads=[ag_byq[f"agh{l}_8"]], writes=[b], skip_waw=True)
                        return t, b

                    def pool_scan(u_ap, nt, first, out_cols, ub):
                        n = 16 + nt
                        S.op("dve", lambda e: e.tensor_tensor(out=s1[:, 1:n], in0=u_ap[:, 1:n], in1=u_ap[:, 0:n - 1], op=ALU.add), reads=[ub], writes=[Bsc])
                        S.op("dve", lambda e: e.tensor_tensor(out=s2[:, 3:n], in0=s1[:, 3:n], in1=s1[:, 1:n - 2], op=ALU.add), reads=[Bsc], writes=[Bsc])
                        S.op("dve", lambda e: e.tensor_tensor(out=s3[:, 7:n], in0=s2[:, 7:n], in1=s2[:, 3:n - 4], op=ALU.add), reads=[Bsc], writes=[Bsc])
                        S.op("dve", lambda e: e.tensor_tensor(out=s4[:, 15:n], in0=s3[:, 15:n], in1=s3[:, 7:n - 8], op=ALU.add), reads=[Bsc], writes=[Bsc])
                        S.op("dve", lambda e: e.tensor_scalar(out=acc[:, :nt], in0=s1[:, 16:n], scalar1=pvec_sb[:, 2:3], scalar2=None, op0=ALU.mult),
                             reads=[Bsc], writes=[Bsc])
                        for k, sk_ in enumerate((s2, s3, s4)):
                            S.op("dve", lambda e: e.scalar_tensor_tensor(out=acc[:, :nt], in0=sk_[:, 16:n], scalar=pvec_sb[:, 3 + k:4 + k], in1=acc[:, :nt],
                                                                         op0=ALU.mult, op1=ALU.add), reads=[Bsc], writes=[Bsc])
                        if first:
                            S.op("dve", lambda e: e.tensor_tensor(out=acc[:, 0:16], in0=acc[:, 0:16], in1=pvec_sb[:, 8:24], op=ALU.mult), reads=[Bsc], writes=[Bsc])
                        S.op("dve", lambda e: e.tensor_tensor(out=dT[:, :nt], in0=acc[:, :nt], in1=u_ap[:, 16:n], op=ALU.subtract), reads=[Bsc, ub], writes=[Bsc])
                        bi = banks.get()
                        S.op("pe", lambda e: e.matmul(ps[bi][0:64, :nt], lhsT=pw_sb[:, :], rhs=dT[:, :nt], start=True, stop=True),
                             reads=[Bsc, Bpw], writes=[pb[bi]])
                        m, mb, mi = mps.next()
                        S.op("dve", lambda e: e.tensor_scalar(out=m[:, :nt], in0=ps[bi][0:64, :nt], scalar1=pvec_sb[:, l:l + 1], scalar2=None, op0=ALU.mult),
                             reads=[pb[bi]], writes=[mb])
                        S.dma("sp", f"mp{mi}", lambda e: e.dma_start(out=mix_dst(0, 64, out_cols, nt), in_=m[:, :nt]), reads=[mb])

                    prev_u = None
                    if l == 0:
                        xq = [load_x(0), load_x(1)]
                        hcur = norm_x(xq[0])
                    else:
                        nxt = load_h(0)
                    for i in range(17):
                        if l == 0:
                            hT, hb = hcur
                            if i < 16:
                                hcur = norm_x(xq[(i + 1) % 2])
                                if i + 2 <= 16:
                                    xq[i % 2] = load_x(i + 2)
                        else:
                            hT, hb = nxt
                            if i < 16:
                                nxt = load_h(i + 1)
                        nt = 512 if i < 16 else 128
                        t0 = 512 * i
                        u_t, u_b, _ = uts.next()
                        for (nm, wc0, M) in (("sq", 0, 64), ("sk", 64, 64), ("u", 128, 64), ("dq", 192, 128), ("dk", 320, 128)):
                            if 'e' in DBG:
                                continue
                            if 'd' in DBG and nm in ("sk", "dk"):
                                continue
                            bi = banks.get()
                            for kc in range(8):
                                S.op("pe", lambda e: e.matmul(ps[bi][:M, :nt], lhsT=w_sb[:, kc, wc0:wc0 + M], rhs=hT[:, kc, :nt],
                                                              start=(kc == 0), stop=(kc == 7)), reads=[Bw, hb], writes=[pb[bi]])
                            if nm == "sq":
                                S.op("act", lambda e: e.activation(out=sqT[:, t0:t0 + nt], in_=ps[bi][:64, :nt], func=AF.Copy), reads=[pb[bi]])
                            elif nm == "dq":
                                S.op("act", lambda e: e.activation(out=dqT[:, t0:t0 + nt], in_=ps[bi][:, :nt], func=AF.Copy), reads=[pb[bi]])
                            elif nm == "sk":
                                k_, kb_, ki = kst.next()
                                S.op("act", lambda e: e.activation(out=k_[:64, :nt], in_=ps[bi][:64, :nt], func=AF.Copy), reads=[pb[bi]], writes=[kb_])
                                S.op("dve", lambda e: e.tensor_scalar(out=skTn[:, t0:t0 + nt], in0=k_[:64, :nt], scalar1=-0.125, scalar2=None, op0=ALU.mult),
                                     reads=[kb_])
                                if 'h' not in DBG:
                                    S.dma("sp", f"ks{ki}", lambda e: e.dma_start(out=skT_o[l * 64:(l + 1) * 64, t0:t0 + nt], in_=k_[:64, :nt]), reads=[kb_])
                            elif nm == "dk":
                                k_, kb_, ki = kst.next()
                                S.op("act", lambda e: e.activation(out=k_[:, :nt], in_=ps[bi][:, :nt], func=AF.Copy), reads=[pb[bi]], writes=[kb_])
                                S.op("dve", lambda e: e.tensor_copy(out=dkT[:, t0:t0 + nt], in_=k_[:, :nt]), reads=[kb_])
                                if 'h' not in DBG:
                                    S.dma("sp", f"ks{ki}", lambda e: e.dma_start(out=dkT_o[l * 128:(l + 1) * 128, t0:t0 + nt], in_=k_[:, :nt]), reads=[kb_])
                            else:
                                if i < 16:
                                    S.op("act", lambda e: e.activation(out=u_t[:, 16:16 + nt], in_=ps[bi][:64, :nt], func=AF.Copy), reads=[pb[bi]], writes=[u_b])
                                else:
                                    for r in range(4):
                                        S.op("act", lambda e: e.activation(out=us[:, r, 16:48], in_=ps[bi][:64, 32 * r:32 * r + 32], func=AF.Copy),
                                             reads=[pb[bi]], writes=[Bus])
                        v_, vb_, vi = vst.next()
                        if 'b' in DBG:
                            pass
                        elif i < 16:
                            for half in range(2):
                                bi = banks.get()
                                for s_ in range(2):
                                    sub = half * 2 + s_
                                    for kc in range(8):
                                        S.op("pe", lambda e: e.matmul(ps[bi][:, s_ * 192:(s_ + 1) * 192], lhsT=hT[:, kc, sub * 128:(sub + 1) * 128],
                                                                      rhs=w_sb[:, kc, 448:640], start=(kc == 0), stop=(kc == 7)), reads=[Bw, hb], writes=[pb[bi]])
                                for s_ in range(2):
                                    sub = half * 2 + s_
                                    blk = 4 * i + sub
                                    S.op("dve", lambda e: e.tensor_copy(out=v_[:, sub, :], in_=ps[bi][:, s_ * 192:(s_ + 1) * 192]), reads=[pb[bi]], writes=[vb_])
                                    S.op("pool", lambda e: e.tensor_copy(out=sv[:, blk, :], in_=v_[:, sub, 0:64]), reads=[vb_])
                                    S.op("pool", lambda e: e.tensor_copy(out=dv[:, blk, :], in_=v_[:, sub, 64:192]), reads=[vb_])
                            S.dma("sp", f"vs{vi}", lambda e: e.dma_start(
                                out=sv_o[l * HC + t0:l * HC + t0 + 512, :].rearrange("(s p) d -> p s d", p=128), in_=v_[:, :, 0:64]), reads=[vb_])
                            S.dma("sp", f"vs{vi}", lambda e: e.dma_start(
                                out=dv_o[l * HC + t0:l * HC + t0 + 512, :].rearrange("(s p) d -> p s d", p=128), in_=v_[:, :, 64:192]), reads=[vb_])
                        else:
                            for r in range(4):
                                bi = banks.get()
                                for kc in range(8):
                                    S.op("pe", lambda e: e.matmul(ps[bi][:32, 0:192], lhsT=hT[:, kc, 32 * r:32 * r + 32], rhs=w_sb[:, kc, 448:640],
                                                                  start=(kc == 0), stop=(kc == 7)), reads=[Bw, hb], writes=[pb[bi]])
                                S.op("dve", lambda e: e.tensor_copy(out=v_[:32, r, :], in_=ps[bi][:32, 0:192]), reads=[pb[bi]], writes=[vb_])
                                S.op("pool", lambda e: e.tensor_copy(out=sv[:32, 64 + r, :], in_=v_[:32, r, 0:64]), reads=[vb_])
                                S.op("pool", lambda e: e.tensor_copy(out=dv[:32, 64 + r, :], in_=v_[:32, r, 64:192]), reads=[vb_])
                            S.dma("sp", f"vs{vi}", lambda e: e.dma_start(
                                out=sv_o[l * HC + SEQ:l * HC + SEQ + 128, :].rearrange("(r p) d -> p r d", p=32), in_=v_[:32, :, 0:64]), reads=[vb_])
                            S.dma("sp", f"vs{vi}", lambda e: e.dma_start(
                                out=dv_o[l * HC + SEQ:l * HC + SEQ + 128, :].rearrange("(r p) d -> p r d", p=32), in_=v_[:32, :, 64:192]), reads=[vb_])
                        if 'a' in DBG:
                            pass
                        elif i < 16:
                            if i == 0:
                                S.op("dve", lambda e: e.memset(u_t[:, 0:16], 0.0), writes=[u_b])
                            else:
                                pu, pub = prev_u
                                S.op("dve", lambda e: e.tensor_copy(out=u_t[:, 0:16], in_=pu[:, 512:528]), reads=[pub], writes=[u_b])
                            pool_scan(u_t, 512, i == 0, t0, u_b)
                            if i == 15:
                                S.dma("sp", "po", lambda e: e.dma_start(out=pool_o[l * 64:(l + 1) * 64, 0:15], in_=u_t[:, 16 + 497:16 + 512]), reads=[u_b])
                            prev_u = (u_t, u_b)
                        else:
                            for r in range(4):
                                pool_scan(us[:, r, :], 32, False, SEQ + 32 * r, Bus)
                                S.dma("sp", "po", lambda e: e.dma_start(out=pool_o[l * 64:(l + 1) * 64, 15 + 15 * r:30 + 15 * r], in_=us[:, r, 33:48]), reads=[Bus])
                        if STOP_AFTER == 31 and i == 0:
                            raise _Stop()
                    S.barrier(skip_prefix="ag")
                    if STOP_AFTER == 32:
                        raise _Stop()
                with contextlib.ExitStack() as sa:
                    Es = Slots(nc, S, sa, "Es", 3, [128, 512], F32)
                    Zs = Slots(nc, S, sa, "Zs", 6, [128, 512], F32)
                    Ars = Slots(nc, S, sa, "Ars", 3, [128, 512], F32)
                    Ls = Slots(nc, S, sa, "Ls", 4, [128, 512], BF16)
                    As = Slots(nc, S, sa, "As", 3, [128, 512], BF16)
                    P1s = Slots(nc, S, sa, "P1s", 3, [128, 512], BF16)
                    P2s = Slots(nc, S, sa, "P2s", 3, [128, 512], BF16)
                    Lsum = sbt(sa, "Lsum", [128, 512], BF16)
                    BLs = S.buf()
                    pac = sbt(sa, "pac", [128, 1024], BF16)
                    pa1 = pac[:, 0:512]
                    pa2 = pac[:, 512:1024]
                    Pc = Slots(nc, S, sa, "Pc", 3, [128, 1024], BF16)
                    P2bufs = [S.buf() for _ in range(3)]
                    Bpa = S.buf()
                    msb = Slots(nc, S, sa, "msb", 2, [64, 512], F32)
                    mdf = Slots(nc, S, sa, "mdf", 2, [128, 512], F32)
                    f1 = sbt(sa, "f1", [128, 512], F32)
                    f2 = sbt(sa, "f2", [128, 512], F32)
                    f3 = sbt(sa, "f3", [128, 512], F32)
                    f4 = sbt(sa, "f4", [128, 512], F32)
                    fsq = sbt(sa, "fsq", [128, 512], BF16)
                    Bf = S.buf()
                    skTp = sbt(sa, "skTp", [64, PAST], BF16)
                    svp = sbt(sa, "svp", [128, 16, 64], BF16)
                    dkTp = sbt(sa, "dkTp", [128, PAST], BF16)
                    dvp = sbt(sa, "dvp", [128, 16, 128], BF16)
                    Bca = [S.buf() for _ in range(4)]
                    Bmsw = [S.buf(), S.buf()]
                    Bmdw = [S.buf(), S.buf()]

                    def run_pipelined(gens, width):
                        active = []
                        it = iter(gens)
                        more = True
                        while True:
                            if more and len(active) < width:
                                try:
                                    active.append(next(it))
                                except StopIteration:
                                    more = False
                            if not active:
                                break
                            for g_ in list(active):
                                try:
                                    next(g_)
                                except StopIteration:
                                    active.remove(g_)

                    def sb_attend(q_ap, nq, blocks, out_cols):
                        Oi = banks.get(pin=True)
                        nb = len(blocks)

                        def unit(idx, blk):
                            kT, vv, hs, mask, rb = blk
                            zi = banks.get()
                            S.op("pe", lambda e: e.matmul(ps[zi][:hs, :nq], lhsT=kT, rhs=q_ap, start=True, stop=True), reads=rb, writes=[pb[zi]])
                            yield
                            zs, zsb, _ = Zs.next()
                            S.op("dve", lambda e: e.tensor_copy(out=zs[:hs, :nq], in_=ps[zi][:hs, :nq]), reads=[pb[zi]], writes=[zsb])
                            yield
                            E, Eb, _ = Es.next()
                            S.op("act", lambda e: e.activation(out=E[:hs, :nq], in_=zs[:hs, :nq], func=AF.Exp, scale=-1.0), reads=[zsb], writes=[Eb])
                            yield
                            Lp, Lb, _ = Ls.next()
                            S.op("act", lambda e: e.activation(out=Lp[:hs, :nq], in_=E[:hs, :nq], func=AF.Ln, bias=one_t[:hs, 0:1]), reads=[Eb], writes=[Lb])
                            if mask is not None:
                                S.op("pool", lambda e: e.tensor_tensor(out=Lp[:hs, :nq], in0=Lp[:hs, :nq], in1=mask, op=ALU.mult), reads=[Lb], writes=[Lb])
                            yield
                            si = banks.get()
                            S.op("pe", lambda e: e.matmul(ps[si][:hs, :nq], lhsT=triT[:hs, :hs], rhs=Lp[:hs, :nq], start=True, stop=(idx == 0)),
                                 reads=[Lb], writes=[pb[si]])
                            if idx > 0:
                                S.op("pe", lambda e: e.matmul(ps[si][:hs, :nq], lhsT=ones[:, :hs], rhs=Lsum[:, :nq], start=False, stop=True),
                                     reads=[BLs], writes=[pb[si]])
                            if idx < nb - 1:
                                if idx == 0:
                                    if hs < 128:
                                        S.op("dve", lambda e: e.memset(Lsum[:, :nq], 0.0), writes=[BLs])
                                    S.op("dve", lambda e: e.tensor_copy(out=Lsum[:hs, :nq], in_=Lp[:hs, :nq]), reads=[Lb], writes=[BLs])
                                else:
                                    S.op("dve", lambda e: e.tensor_tensor(out=Lsum[:hs, :nq], in0=Lsum[:hs, :nq], in1=Lp[:hs, :nq], op=ALU.add),
                                         reads=[Lb, BLs], writes=[BLs])
                            yield
                            ar, arb, _ = Ars.next()
                            S.op("dve", lambda e: e.tensor_tensor(out=ar[:hs, :nq], in0=ps[si][:hs, :nq], in1=zs[:hs, :nq], op=ALU.add),
                                 reads=[pb[si], zsb], writes=[arb])
                            yield
                            A, Ab, _ = As.next()
                            S.op("act", lambda e: e.activation(out=A[:hs, :nq], in_=ar[:hs, :nq], func=AF.Exp, scale=-1.0), reads=[arb], writes=[Ab])
                            if mask is not None:
                                S.op("pool", lambda e: e.tensor_tensor(out=A[:hs, :nq], in0=A[:hs, :nq], in1=mask, op=ALU.mult), reads=[Ab], writes=[Ab])
                            yield
                            S.op("pe", lambda e: e.matmul(ps[Oi][:64, :nq], lhsT=vv, rhs=A[:hs, :nq], start=(idx == 0), stop=(idx == nb - 1)),
                                 reads=[Ab] + list(rb), writes=[pb[Oi]])

                        run_pipelined((unit(i_, b_) for i_, b_ in enumerate(blocks)), 8)
                        m, mb, mi = msb.next()
                        S.op("act", lambda e: e.activation(out=m[:, :nq], in_=ps[Oi][:64, :nq], func=AF.Copy), reads=[pb[Oi]], writes=[mb])
                        banks.unpin(Oi)
                        S.dma("sp", f"ms{mi}", lambda e: e.dma_start(out=mix_dst(64, 64, out_cols, nq), in_=m[:, :nq]), reads=[mb], writes=[Bmsw[mi]])

                    def diff_attend(q1, q2, nq, blocks, out_cols):
                        O1 = banks.get(pin=True)
                        O2 = banks.get(pin=True)
                        nb = len(blocks)

                        def unit(idx, blk):
                            k1T, k2T, vv, hs, mask, rb = blk
                            a1 = banks.get()
                            S.op("pe", lambda e: e.matmul(ps[a1][:hs, :nq], lhsT=k1T, rhs=q1, start=True, stop=True), reads=rb, writes=[pb[a1]])
                            a2 = banks.get()
                            S.op("pe", lambda e: e.matmul(ps[a2][:hs, :nq], lhsT=k2T, rhs=q2, start=True, stop=True), reads=rb, writes=[pb[a2]])
                            yield
                            Pt, P1b, pi_ = Pc.next()
                            P2b = P2bufs[pi_]
                            P1 = Pt[:, 0:512]
                            P2 = Pt[:, 512:1024]
                            S.op("act", lambda e: e.activation(out=P1[:hs, :nq], in_=ps[a1][:hs, :nq], func=AF.Exp, scale=0.125), reads=[pb[a1]], writes=[P1b])
                            S.op("act", lambda e: e.activation(out=P2[:hs, :nq], in_=ps[a2][:hs, :nq], func=AF.Exp, scale=0.125), reads=[pb[a2]], writes=[P2b])
                            if mask is not None:
                                S.op("pool", lambda e: e.tensor_tensor(out=P1[:hs, :nq], in0=P1[:hs, :nq], in1=mask, op=ALU.mult), reads=[P1b], writes=[P1b])
                                S.op("dve", lambda e: e.tensor_tensor(out=P2[:hs, :nq], in0=P2[:hs, :nq], in1=mask, op=ALU.mult), reads=[P2b], writes=[P2b])
                            yield
                            st_, sp_ = (idx == 0), (idx == nb - 1)
                            S.op("pe", lambda e: e.matmul(ps[O1][:, :nq], lhsT=vv, rhs=P1[:hs, :nq], start=st_, stop=sp_), reads=[P1b] + list(rb), writes=[pb[O1]])
                            S.op("pe", lambda e: e.matmul(ps[O2][:, :nq], lhsT=vv, rhs=P2[:hs, :nq], start=st_, stop=sp_), reads=[P2b] + list(rb), writes=[pb[O2]])
                            if nq == 512 and idx == 0:
                                S.op("dve", lambda e: e.tensor_copy(out=pac[:hs, :], in_=Pt[:hs, :]), reads=[P1b, P2b], writes=[Bpa])
                            elif nq == 512:
                                S.op("dve", lambda e: e.tensor_tensor(out=pac[:hs, :], in0=pac[:hs, :], in1=Pt[:hs, :], op=ALU.add),
                                     reads=[P1b, P2b, Bpa], writes=[Bpa])
                            elif idx == 0:
                                S.op("dve", lambda e: e.tensor_copy(out=pa1[:hs, :nq], in_=P1[:hs, :nq]), reads=[P1b], writes=[Bpa])
                                S.op("dve", lambda e: e.tensor_copy(out=pa2[:hs, :nq], in_=P2[:hs, :nq]), reads=[P2b], writes=[Bpa])
                            else:
                                S.op("dve", lambda e: e.tensor_tensor(out=pa1[:hs, :nq], in0=pa1[:hs, :nq], in1=P1[:hs, :nq], op=ALU.add), reads=[P1b, Bpa], writes=[Bpa])
                                S.op("dve", lambda e: e.tensor_tensor(out=pa2[:hs, :nq], in0=pa2[:hs, :nq], in1=P2[:hs, :nq], op=ALU.add), reads=[P2b, Bpa], writes=[Bpa])

                        run_pipelined((unit(i_, b_) for i_, b_ in enumerate(blocks)), 3)
                        D1 = banks.get()
                        S.op("pe", lambda e: e.matmul(ps[D1][:, :nq], lhsT=ones[:, :], rhs=pa1[:, :nq], start=True, stop=True), reads=[Bpa], writes=[pb[D1]])
                        D2 = banks.get()
                        S.op("pe", lambda e: e.matmul(ps[D2][:, :nq], lhsT=ones[:, :], rhs=pa2[:, :nq], start=True, stop=True), reads=[Bpa], writes=[pb[D2]])
                        S.op("dve", lambda e: e.reciprocal(out=f1[:, :nq], in_=ps[D1][:, :nq]), reads=[pb[D1]], writes=[Bf])
                        S.op("dve", lambda e: e.reciprocal(out=f2[:, :nq], in_=ps[D2][:, :nq]), reads=[pb[D2]], writes=[Bf])
                        S.op("dve", lambda e: e.tensor_tensor(out=f3[:, :nq], in0=ps[O1][:, :nq], in1=f1[:, :nq], op=ALU.mult), reads=[pb[O1], Bf], writes=[Bf])
                        S.op("dve", lambda e: e.tensor_tensor(out=f4[:, :nq], in0=ps[O2][:, :nq], in1=f2[:, :nq], op=ALU.mult), reads=[pb[O2], Bf], writes=[Bf])
                        for bnk in (O1, O2):
                            banks.unpin(bnk)
                        S.op("dve", lambda e: e.scalar_tensor_tensor(out=f3[:, :nq], in0=f4[:, :nq], scalar=nlam_t[:, l:l + 1], in1=f3[:, :nq],
                                                                     op0=ALU.mult, op1=ALU.add), reads=[Bf], writes=[Bf])
                        S.op("act", lambda e: e.activation(out=fsq[:, :nq], in_=f3[:, :nq], func=AF.Square), reads=[Bf], writes=[Bf])
                        ssb = banks.get()
                        S.op("pe", lambda e: e.matmul(ps[ssb][:, :nq], lhsT=ones[:, :], rhs=fsq[:, :nq], start=True, stop=True), reads=[Bf], writes=[pb[ssb]])
                        S.op("act", lambda e: e.activation(out=f1[:, :nq], in_=ps[ssb][:, :nq], func=AF.Ln, scale=1.0 / 128.0, bias=eps_t[:, 0:1]),
                             reads=[pb[ssb], Bf], writes=[Bf])
                        S.op("act", lambda e: e.activation(out=f2[:, :nq], in_=f1[:, :nq], func=AF.Exp, scale=-0.5), reads=[Bf], writes=[Bf])
                        m, mb, mi = mdf.next()
                        S.op("dve", lambda e: e.scalar_tensor_tensor(out=m[:, :nq], in0=f3[:, :nq], scalar=gd_t[:, l:l + 1], in1=f2[:, :nq],
                                                                     op0=ALU.mult, op1=ALU.mult), reads=[Bf], writes=[mb])
                        S.dma("sp", f"md{mi}", lambda e: e.dma_start(out=mix_dst(128, 128, out_cols, nq), in_=m[:, :nq]), reads=[mb], writes=[Bmdw[mi]])

                    for qt in range(16):
                        c0 = qt * 512
                        blocks = []
                        for kb in range(4 * qt + 3, -1, -1):
                            o = kb - 4 * qt
                            mask = ms_sb[:, o * 512:(o + 1) * 512] if o >= 0 else None
                            blocks.append((skTn[:, kb * 128:(kb + 1) * 128], sv[:, kb, :], 128, mask, ()))
                        sb_attend(sqT[:, c0:c0 + 512], 512, blocks, c0)
                        if STOP_AFTER == 33:
                            raise _Stop()
                        blocks = []
                        for kb in range(0, 4 * qt + 4):
                            o = kb - 4 * qt
                            mask = md_sb[:, o * 512:(o + 1) * 512] if o >= 0 else None
                            blocks.append((dkT[0:64, kb * 128:(kb + 1) * 128], dkT[64:128, kb * 128:(kb + 1) * 128], dv[:, kb, :], 128, mask, ()))
                        diff_attend(dqT[0:64, c0:c0 + 512], dqT[64:128, c0:c0 + 512], 512, blocks, c0)
                        if qt % 2 == 1:
                            q = qt // 2
                            ag_chunk(f"agm{l}_{q}", mixT_src[l][0][q * 256:(q + 1) * 256, :], mixT_all[l][0][q * 1024:(q + 1) * 1024, :], Bmsw + Bmdw)
                        if STOP_AFTER == 34:
                            raise _Stop()
                    if STOP_AFTER == 35:
                        raise _Stop()
                    for r in range(4):
                        base = (l * 4 + r)
                        S.dma("pool", "ca0", lambda e: e.dma_start(out=skTp[:], in_=c_sbkT[base * 64:(base + 1) * 64, :]), writes=[Bca[0]])
                        S.op("act", lambda e: e.activation(out=skTp[:], in_=skTp[:], func=AF.Identity, scale=-0.125), reads=[Bca[0]], writes=[Bca[0]])
                        S.dma("pool", "ca1", lambda e: e.dma_start(out=svp[:], in_=c_sbv[base * PAST:(base + 1) * PAST, :].rearrange("(k p) d -> p k d", p=128)),
                              writes=[Bca[1]])
                        S.dma("pool", "ca2", lambda e: e.dma_start(out=dkTp[:], in_=c_dkT[base * 128:(base + 1) * 128, :]), writes=[Bca[2]])
                        S.dma("pool", "ca3", lambda e: e.dma_start(out=dvp[:], in_=c_dv[base * PAST:(base + 1) * PAST, :].rearrange("(k p) d -> p k d", p=128)),
                              writes=[Bca[3]])
                        c0 = SEQ + 32 * r
                        blocks = [(skTn[:, c0:c0 + 32], sv[0:32, 64 + r, :], 32, ms_sb[0:32, 0:32], ())]
                        for kb in range(15, -1, -1):
                            blocks.append((skTp[:, kb * 128:(kb + 1) * 128], svp[:, kb, :], 128, None, (Bca[0], Bca[1])))
                        sb_attend(sqT[:, c0:c0 + 32], 32, blocks, c0)
                        blocks = []
                        for kb in range(16):
                            blocks.append((dkTp[0:64, kb * 128:(kb + 1) * 128], dkTp[64:128, kb * 128:(kb + 1) * 128], dvp[:, kb, :], 128, None,
                                           (Bca[2], Bca[3])))
                        blocks.append((dkT[0:64, c0:c0 + 32], dkT[64:128, c0:c0 + 32], dv[0:32, 64 + r, :], 32, None, ()))
                        diff_attend(dqT[0:64, c0:c0 + 32], dqT[64:128, c0:c0 + 32], 32, blocks, c0)
                    ag_chunk(f"agm{l}_8", mixT_src[l][1], mixT_all[l][1], Bmsw + Bmdw)
                    S.barrier(skip_prefix="ag")

        def schedule():
            if STOP_AFTER == 0:
                return
            for l in range(L):
                head_phase(l)
                if STOP_AFTER == 3:
                    return
                if l == 0:
                    token_phase(0, xT_in, xT_scr, True, gpre(1, 0), hT_src[1], mixT_all[0], hT_all[1], 1)
                else:
                    token_phase(1, xT_scr, yT, True, None, None, mixT_all[1])
                if STOP_AFTER == 5:
                    return
        try:
            schedule()
        except _Stop:
            g.pop_all()
            S.finish()
            return nc, S
        S.finish()
    return nc, S


_CACHE = {}


def _consts():
    j = np.arange(128)[:, None]
    s = np.arange(128)[None, :]
    tri = (j >= s).astype(np.float32)
    t = np.arange(512)[None, :]
    ms = np.zeros((128, 2048), np.float32)
    md = np.zeros((128, 2048), np.float32)
    for o in range(4):
        ks = 128 * o + np.arange(128)[:, None]
        ms[:, o * 512:(o + 1) * 512] = (ks < t)
        md[:, o * 512:(o + 1) * 512] = ((ks // 64) <= (t // 64))
    return {"ones_c": np.ones((128, 128), np.float32), "tri_c": tri, "ident_c": np.eye(128, dtype=np.float32),
            "ms_c": ms, "md_c": md}


def kernel(x_prompt, x_sample, cache_sb_k, cache_sb_v, cache_diff_k, cache_diff_v,
           cache_mem_k, cache_mem_v, state_pool, mem_prompt,
           g_pre, g_post, g_mem, w_in, w_out, pool_w, pool_scale,
           lam_q1, lam_k1, lam_q2, lam_k2, diff_g, wq_m, wk_m, wv_m, wo_m,
           w_gate, w_up, w_down):
    f = lambda a: np.ascontiguousarray(np.asarray(a, dtype=np.float32))
    x_prompt, x_sample = f(x_prompt), f(x_sample)
    cache_sb_k, cache_sb_v, cache_diff_k, cache_diff_v = f(cache_sb_k), f(cache_sb_v), f(cache_diff_k), f(cache_diff_v)
    cache_mem_k, cache_mem_v, state_pool, mem_prompt = f(cache_mem_k), f(cache_mem_v), f(state_pool), f(mem_prompt)
    g_pre, g_post, g_mem, w_in, w_out, pool_w, pool_scale = f(g_pre), f(g_post), f(g_mem), f(w_in), f(w_out), f(pool_w), f(pool_scale)
    lam_q1, lam_k1, lam_q2, lam_k2, diff_g = f(lam_q1), f(lam_k1), f(lam_q2), f(lam_k2), f(diff_g)
    wq_m, wk_m, wv_m, wo_m, w_gate, w_up, w_down = f(wq_m), f(wk_m), f(wv_m), f(wo_m), f(w_gate), f(w_up), f(w_down)

    if "nc" not in _CACHE:
        _CACHE["nc"] = build_program()[0]
    nc = _CACHE["nc"]
    consts = _consts()
    gvec = np.zeros((128, NG), np.float32)
    for l in range(L):
        for i in range(3):
            gvec[:, (l * 3 + i) * 8:(l * 3 + i) * 8 + 8] = g_pre[l, i].reshape(8, 128).T
            gvec[:, 48 + (l * 3 + i) * 8:48 + (l * 3 + i) * 8 + 8] = g_post[l, i].reshape(8, 128).T
        gvec[:, 96 + l] = diff_g[l]
    gmem_b = np.ascontiguousarray(np.broadcast_to(g_mem[:, None, :], (L, 128, D)).reshape(L * 128, D))
    lamv = np.zeros((64, 4 * L), np.float32)
    for l in range(L):
        lamv[:, 4 * l + 0] = lam_q1[l]
        lamv[:, 4 * l + 1] = lam_k1[l]
        lamv[:, 4 * l + 2] = lam_q2[l]
        lamv[:, 4 * l + 3] = lam_k2[l]
    perm = []
    for h in range(4):
        perm += list(range(64 * h, 64 * h + 64)) + list(range(256 + 64 * h, 256 + 64 * h + 64)) + list(range(512 + 128 * h, 512 + 128 * h + 128))
    perm = np.array(perm)
    shared = dict(consts)
    shared.update({
        "w_out_p": w_out[:, perm, :].reshape(L * D, D), "wq": wq_m.reshape(L * D, D), "wk": wk_m.reshape(L * D, D),
        "wv": wv_m.reshape(L * D, D), "wo": wo_m.reshape(L * D, D), "w_gate": w_gate.reshape(L * D, DFF),
        "w_up": w_up.reshape(L * D, DFF), "w_down": w_down.reshape(L * DFF, D), "gvec": gvec, "gmem_b": gmem_b, "lamv": lamv,
    })
    shared = {k: np.ascontiguousarray(v, dtype=np.float32) for k, v in shared.items()}
    xfull = [np.ascontiguousarray(np.concatenate([x_prompt[b].T] + [x_sample[4 * b + r].T for r in range(4)], axis=1)) for b in range(2)]
    in_maps = []
    for c in range(8):
        b, h = c // 4, c % 4
        m = dict(shared)
        m["xT"] = np.ascontiguousarray(np.concatenate([x_prompt[b, TOKP * h:TOKP * (h + 1)].T, x_sample[c].T], axis=1))
        m["xT_full"] = xfull[b]
        cols = (list(range(256 + 64 * h, 256 + 64 * h + 64)) + list(range(512 + 64 * h, 512 + 64 * h + 64)) + list(range(64 * h, 64 * h + 64))
                + list(range(1024 + 128 * h, 1024 + 128 * h + 128)) + list(range(1536 + 128 * h, 1536 + 128 * h + 128))
                + list(range(768 + 64 * h, 768 + 64 * h + 64)) + list(range(2048 + 128 * h, 2048 + 128 * h + 128)))
        m["w_in_h"] = np.ascontiguousarray(w_in[:, :, cols].reshape(L * D, 640))
        m["pool_w_h"] = np.ascontiguousarray(pool_w[:, h].reshape(L * 64, 64))
        pv = np.zeros((64, 32), np.float32)
        for l in range(L):
            pv[:, l] = pool_scale[l, 64 * h:64 * h + 64]
        w = WINDOWS[h]
        pv[:, 2 + h] = 1.0 / w
        tt = np.arange(16)
        pv[:, 8:24] = (w / np.minimum(w, tt + 1))[None, :]
        m["pvec"] = pv
        ss = [4 * b + r for r in range(4)]
        m["c_sbkT"] = np.ascontiguousarray(cache_sb_k[:, ss][:, :, :, h, :].transpose(0, 1, 3, 2).reshape(L * 4 * 64, PAST))
        m["c_sbv"] = np.ascontiguousarray(cache_sb_v[:, ss][:, :, :, h, :].reshape(L * 4 * PAST, 64))
        m["c_dkT"] = np.ascontiguousarray(cache_diff_k[:, ss][:, :, :, h, :].transpose(0, 1, 3, 2).reshape(L * 4 * 128, PAST))
        m["c_dv"] = np.ascontiguousarray(cache_diff_v[:, ss][:, :, :, h, :].reshape(L * 4 * PAST, 128))
        sp = np.zeros((L, 4, 64, 16), np.float32)
        sp[:, :, :, 0:15] = state_pool[:, ss, :, 64 * h:64 * h + 64].transpose(0, 1, 3, 2)
        m["spoolT"] = sp.reshape(L * 4 * 64, 16)
        m["c_memk"] = np.ascontiguousarray(cache_mem_k[:, c].reshape(L * 256, D))
        m["c_memv"] = np.ascontiguousarray(cache_mem_v[:, c].reshape(L * 256, D))
        m["memp"] = np.ascontiguousarray(mem_prompt[b])
        in_maps.append(m)

    if STOP_AFTER is not None and (STOP_AFTER >= 30 or STOP_AFTER in (1, 2, 3)):
        for m in in_maps:
            for k in ("w_out_p", "wq", "wk", "wv", "wo", "w_gate", "w_up", "w_down", "c_memk", "c_memv", "memp", "gmem_b"):
                m[k] = np.zeros((1, 1), np.float32)
    res = run_bass_kernel_spmd(nc, in_maps, core_ids=list(range(8)))
    R = res.results
    y_prompt = np.zeros((2, SEQ, D), np.float32)
    y_sample = np.zeros((8, NS, D), np.float32)
    sbk_p = np.zeros((L, 2, SEQ, 4, 64), np.float32)
    sbv_p = np.zeros((L, 2, SEQ, 4, 64), np.float32)
    dk_p = np.zeros((L, 2, SEQ, 4, 128), np.float32)
    dv_p = np.zeros((L, 2, SEQ, 4, 128), np.float32)
    pool_p = np.zeros((L, 2, 15, 256), np.float32)
    mk_p = np.zeros((L, 2, 256, 4, 256), np.float32)
    mv_p = np.zeros((L, 2, 256, 4, 256), np.float32)
    sbk_s = np.zeros((L, 8, NS, 4, 64), np.float32)
    sbv_s = np.zeros((L, 8, NS, 4, 64), np.float32)
    dk_s = np.zeros((L, 8, NS, 4, 128), np.float32)
    dv_s = np.zeros((L, 8, NS, 4, 128), np.float32)
    pool_s = np.zeros((L, 8, 15, 256), np.float32)
    for c in range(8):
        b, h = c // 4, c % 4
        r = R[c]
        yt = np.asarray(r["yT"])
        y_prompt[b, TOKP * h:TOKP * (h + 1)] = yt[:, :TOKP].T
        y_sample[c] = yt[:, TOKP:].T
        skT = np.asarray(r["skT_o"]).reshape(L, 64, HC)
        dkT = np.asarray(r["dkT_o"]).reshape(L, 128, HC)
        svo = np.asarray(r["sv_o"]).reshape(L, HC, 64)
        dvo = np.asarray(r["dv_o"]).reshape(L, HC, 128)
        po = np.asarray(r["pool_o"]).reshape(L, 64, 75)
        sbk_p[:, b, :, h, :] = skT[:, :, :SEQ].transpose(0, 2, 1)
        dk_p[:, b, :, h, :] = dkT[:, :, :SEQ].transpose(0, 2, 1)
        sbv_p[:, b, :, h, :] = svo[:, :SEQ]
        dv_p[:, b, :, h, :] = dvo[:, :SEQ]
        pool_p[:, b, :, 64 * h:64 * h + 64] = po[:, :, 0:15].transpose(0, 2, 1)
        for rr in range(4):
            s = 4 * b + rr
            sl = slice(SEQ + 32 * rr, SEQ + 32 * rr + 32)
            sbk_s[:, s, :, h, :] = skT[:, :, sl].transpose(0, 2, 1)
            dk_s[:, s, :, h, :] = dkT[:, :, sl].transpose(0, 2, 1)
            sbv_s[:, s, :, h, :] = svo[:, sl]
            dv_s[:, s, :, h, :] = dvo[:, sl]
            pool_s[:, s, :, 64 * h:64 * h + 64] = po[:, :, 15 + 15 * rr:30 + 15 * rr].transpose(0, 2, 1)
        if h == 0:
            mk_p[:, b] = np.asarray(r["memk_o"]).reshape(L, 256, 4, 256)
            mv_p[:, b] = np.asarray(r["memv_o"]).reshape(L, 256, 4, 256)
    return (y_prompt, y_sample, sbk_p, sbv_p, dk_p, dv_p, pool_p, mk_p, mv_p, sbk_s, sbv_s, dk_s, dv_s, pool_s)
```

```python
import math
import contextlib
import numpy as np
import concourse.bass as bass
import concourse.mybir as mybir
from concourse.bass_utils import run_bass_kernel_spmd

F32 = mybir.dt.float32
BF16 = mybir.dt.bfloat16
AF = mybir.ActivationFunctionType
ALU = mybir.AluOpType
AX = mybir.AxisListType

L = 2
D = 1024
KC = 8
SEQ = 8192
NS = 32
PAST = 2048
TOKP = 2048
NT = TOKP + NS
HC = SEQ + 4 * NS
DFF = 2816
FC = 22
EPS = 1e-6
WINDOWS = (2, 4, 8, 16)
GROUPS = [[0, 1, 2, 3], [4, 5, 6, 7]]
NG = 98
WSLOT = 4096
STOP_AFTER = None
DBG = ''


class _Stop(Exception):
    pass


class Buf:
    __slots__ = ("last_w", "readers")

    def __init__(self):
        self.last_w = None
        self.readers = {}


class Sched:
    def __init__(self, nc):
        self.nc = nc
        self.E = {"pe": nc.tensor, "act": nc.scalar, "dve": nc.vector, "pool": nc.gpsimd, "sp": nc.sync}
        self.esem = {}
        self.ecnt = {}
        for e in ("pe", "act", "dve", "pool"):
            self.esem[e] = nc.alloc_semaphore("se_" + e)
            self.ecnt[e] = 0
        self.csem = {}
        self.ccnt = {}
        self.known = {e: {} for e in self.E}
        self.pending = {e: None for e in self.E}
        self.nops = 0

    def buf(self):
        return Buf()

    def _waits(self, eng, reads, writes, skip_waw=False):
        need = {}

        def add(tok, raw):
            if tok is None:
                return
            if tok[0] == "e" and tok[1] == eng:
                if eng == "pe":
                    return
            k = (tok[0], tok[1])
            if need.get(k, 0) < tok[2]:
                need[k] = tok[2]

        for b in reads:
            add(b.last_w, True)
        for b in writes:
            if not skip_waw:
                add(b.last_w, False)
            for k, v in b.readers.items():
                add((k[0], k[1], v), False)
        if self.pending[eng] is not None:
            for tok in self.pending[eng]:
                if tok[0] == "e" and tok[1] == eng:
                    continue
                k = (tok[0], tok[1])
                if need.get(k, 0) < tok[2]:
                    need[k] = tok[2]
            self.pending[eng] = None
        kn = self.known[eng]
        for k, v in need.items():
            if kn.get(k, 0) >= v:
                continue
            kn[k] = v
            sem = self.esem[k[1]] if k[0] == "e" else self.csem[k[1]]
            self.E[eng].wait_ge(sem, v)

    def _record(self, tok, reads, writes):
        k = (tok[0], tok[1])
        for b in reads:
            if b.readers.get(k, 0) < tok[2]:
                b.readers[k] = tok[2]
        for b in writes:
            b.last_w = tok
            b.readers = {}

    def op(self, eng, fn, reads=(), writes=()):
        self._waits(eng, reads, writes)
        ins = fn(self.E[eng])
        self.ecnt[eng] += 1
        ins.then_inc(self.esem[eng], 1)
        self._record(("e", eng, self.ecnt[eng]), reads, writes)
        self.nops += 1

    def dma(self, q, chan, fn, reads=(), writes=(), skip_waw=False, inc=16):
        if chan not in self.csem:
            self.csem[chan] = self.nc.alloc_semaphore("sc_" + chan)
            self.ccnt[chan] = 0
        self._waits(q, reads, writes, skip_waw)
        ins = fn(self.E[q])
        self.ccnt[chan] += inc
        ins.then_inc(self.csem[chan], inc)
        self._record(("c", chan, self.ccnt[chan]), reads, writes)
        self.nops += 1

    def barrier(self, skip_prefix=None):
        toks = [("e", e, n) for e, n in self.ecnt.items() if n > 0]
        toks += [("c", c, n) for c, n in self.ccnt.items() if n > 0 and not (skip_prefix and c.startswith(skip_prefix))]
        for e in self.E:
            self.pending[e] = list(toks)

    def finish(self):
        self.barrier()
        self._waits("sp", (), ())


_SLOT_UID = [0]


class Slots:
    def __init__(self, nc, S, stack, name, n, shape, dt):
        _SLOT_UID[0] += 1
        self.t = [stack.enter_context(nc.sbuf_tensor(f"sl_{name}_{_SLOT_UID[0]}_{i}", list(shape), dt)) for i in range(n)]
        self.b = [S.buf() for _ in range(n)]
        self.i = 0
        self.n = n

    def next(self):
        i = self.i
        self.i = (i + 1) % self.n
        return self.t[i], self.b[i], i


class Banks:
    def __init__(self, nc, S, stack):
        self.t = [stack.enter_context(nc.psum_tensor(f"ps{i}", [128, 512], F32)) for i in range(8)]
        self.b = [S.buf() for _ in range(8)]
        self.pinned = set()
        self.rr = 0

    def get(self, pin=False):
        for _ in range(16):
            i = self.rr
            self.rr = (self.rr + 1) % 8
            if i not in self.pinned:
                if pin:
                    self.pinned.add(i)
                return i
        raise RuntimeError("no psum bank")

    def unpin(self, i):
        self.pinned.discard(i)


def build_program():
    nc = bass.Bass("TRN2", target_bir_lowering=False)
    S = Sched(nc)

    small = STOP_AFTER is not None and (STOP_AFTER >= 30 or STOP_AFTER in (1, 2, 3))
    BIG = ("w_out_p", "wq", "wk", "wv", "wo", "w_gate", "w_up", "w_down", "c_memk", "c_memv", "memp", "gmem_b")

    def din(name, shape):
        if small and name in BIG:
            shape = [1, 1]
        return nc.dram_tensor(name, list(shape), F32, kind="ExternalInput").ap()

    def dout(name, shape):
        return nc.dram_tensor(name, list(shape), F32, kind="ExternalOutput").ap()

    def dint(name, shape):
        return nc.dram_tensor(name, list(shape), F32).ap()

    xT_in = din("xT", [D, NT])
    xT_full = din("xT_full", [D, HC])
    w_in_h = din("w_in_h", [L * D, 640])
    pool_w_h = din("pool_w_h", [L * 64, 64])
    pvec_d = din("pvec", [64, 32])
    w_out_p = din("w_out_p", [L * D, D])
    wq_d = din("wq", [L * D, D])
    wk_d = din("wk", [L * D, D])
    wv_d = din("wv", [L * D, D])
    wo_d = din("wo", [L * D, D])
    wg_d = din("w_gate", [L * D, DFF])
    wu_d = din("w_up", [L * D, DFF])
    wd_d = din("w_down", [L * DFF, D])
    gvec_d = din("gvec", [128, NG])
    gmem_d = din("gmem_b", [L * 128, D])
    lamv_d = din("lamv", [64, 4 * L])
    c_sbkT = din("c_sbkT", [L * 4 * 64, PAST])
    c_sbv = din("c_sbv", [L * 4 * PAST, 64])
    c_dkT = din("c_dkT", [L * 4 * 128, PAST])
    c_dv = din("c_dv", [L * 4 * PAST, 128])
    spoolT = din("spoolT", [L * 4 * 64, 16])
    c_memk = din("c_memk", [L * 256, D])
    c_memv = din("c_memv", [L * 256, D])
    memp_d = din("memp", [256, D])
    ones_d = din("ones_c", [128, 128])
    tri_d = din("tri_c", [128, 128])
    ident_d = din("ident_c", [128, 128])
    ms_d = din("ms_c", [128, 2048])
    md_d = din("md_c", [128, 2048])
    yT = dout("yT", [D, NT])
    skT_o = dout("skT_o", [L * 64, HC])
    dkT_o = dout("dkT_o", [L * 128, HC])
    sv_o = dout("sv_o", [L * HC, 64])
    dv_o = dout("dv_o", [L * HC, 128])
    pool_o = dout("pool_o", [L * 64, 75])
    memk_o = dout("memk_o", [L * 256, D])
    memv_o = dout("memv_o", [L * 256, D])
    hT_src = [(dint(f"hT_src{l}", [8 * D, 256]), dint(f"hT_srcs{l}", [D, NS])) for l in range(L)]
    hT_all = [(dint(f"hT_all{l}", [8 * 4 * D, 256]), dint(f"hT_alls{l}", [4 * D, NS])) for l in range(L)]
    mixT_src = [(dint(f"mixT_src{l}", [8 * 256, 1024]), dint(f"mixT_srcs{l}", [256, 4 * NS])) for l in range(L)]
    mixT_all = [(dint(f"mixT_all{l}", [8 * 1024, 1024]), dint(f"mixT_alls{l}", [1024, 4 * NS])) for l in range(L)]
    xT_scr = dint("xT_scr", [D, NT])

    TILES = [(0, 512, "p"), (512, 512, "p"), (1024, 512, "p"), (1536, 512, "p"), (2048, NS, "s")]

    with contextlib.ExitStack() as g:
        uid = [0]

        def sbt(stack, name, shape, dt=F32):
            uid[0] += 1
            return stack.enter_context(nc.sbuf_tensor(f"sb_{name}_{uid[0]}", list(shape), dt))

        banks = Banks(nc, S, g)
        ps = banks.t
        pb = banks.b

        ones = sbt(g, "ones", [128, 128], BF16)
        ones_f = sbt(g, "ones_f", [128, 128], F32)
        triT = sbt(g, "triT", [128, 128], BF16)
        ident_f = sbt(g, "ident_f", [128, 128], F32)
        ms_sb = sbt(g, "ms_sb", [128, 2048], BF16)
        md_sb = sbt(g, "md_sb", [128, 2048], BF16)
        gvec = sbt(g, "gvec", [128, NG], F32)
        pvec_sb = sbt(g, "pvec_sb", [64, 32], F32)
        lamv = sbt(g, "lamv", [64, 4 * L], F32)
        eps_t = sbt(g, "eps_t", [128, 1], F32)
        one_t = sbt(g, "one_t", [128, 1], F32)
        gd_t = sbt(g, "gd_t", [128, L], F32)
        nlam_t = sbt(g, "nlam_t", [128, L], F32)
        Bc = S.buf()
        S.dma("pool", "const", lambda e: e.dma_start(out=ones[:], in_=ones_d), writes=[Bc], skip_waw=True)
        S.dma("pool", "const", lambda e: e.dma_start(out=triT[:], in_=tri_d), writes=[Bc], skip_waw=True)
        S.dma("pool", "const", lambda e: e.dma_start(out=ms_sb[:], in_=ms_d), writes=[Bc], skip_waw=True)
        S.dma("pool", "const", lambda e: e.dma_start(out=md_sb[:], in_=md_d), writes=[Bc], skip_waw=True)
        S.dma("sp", "const2", lambda e: e.dma_start(out=ones_f[:], in_=ones_d), writes=[Bc], skip_waw=True)
        S.dma("sp", "const2", lambda e: e.dma_start(out=ident_f[:], in_=ident_d), writes=[Bc], skip_waw=True)
        S.dma("sp", "const2", lambda e: e.dma_start(out=gvec[:], in_=gvec_d), writes=[Bc], skip_waw=True)
        S.dma("sp", "const2", lambda e: e.dma_start(out=pvec_sb[:], in_=pvec_d), writes=[Bc], skip_waw=True)
        S.dma("sp", "const2", lambda e: e.dma_start(out=lamv[:], in_=lamv_d), writes=[Bc], skip_waw=True)
        S.op("dve", lambda e: e.memset(eps_t[:], EPS), writes=[Bc])
        S.op("dve", lambda e: e.memset(one_t[:], 1.0), writes=[Bc])
        S.barrier()
        with contextlib.ExitStack() as st0:
            prods = sbt(st0, "prods", [64, 2 * L], F32)
            ev = sbt(st0, "ev", [128, 2 * L], F32)
            Bp = S.buf()
            for l in range(L):
                S.op("dve", lambda e: e.tensor_tensor(out=prods[:, 2 * l:2 * l + 1], in0=lamv[:, 4 * l:4 * l + 1],
                                                      in1=lamv[:, 4 * l + 1:4 * l + 2], op=ALU.mult), writes=[Bp])
                S.op("dve", lambda e: e.tensor_tensor(out=prods[:, 2 * l + 1:2 * l + 2], in0=lamv[:, 4 * l + 2:4 * l + 3],
                                                      in1=lamv[:, 4 * l + 3:4 * l + 4], op=ALU.mult), writes=[Bp])
            bi = banks.get()
            S.op("pe", lambda e: e.matmul(ps[bi][:, 0:2 * L], lhsT=ones_f[0:64, :], rhs=prods[:, :], start=True, stop=True),
                 reads=[Bp], writes=[pb[bi]])
            Be = S.buf()
            S.op("act", lambda e: e.activation(out=ev[:], in_=ps[bi][:, 0:2 * L], func=AF.Exp), reads=[pb[bi]], writes=[Be])
            for l in range(L):
                lam_init = 0.8 - 0.6 * math.exp(-0.3 * l)
                S.op("dve", lambda e: e.tensor_tensor(out=nlam_t[:, l:l + 1], in0=ev[:, 2 * l + 1:2 * l + 2],
                                                      in1=ev[:, 2 * l:2 * l + 1], op=ALU.subtract), reads=[Be], writes=[Bc])
                S.op("dve", lambda e: e.tensor_scalar(out=nlam_t[:, l:l + 1], in0=nlam_t[:, l:l + 1], scalar1=-lam_init,
                                                      scalar2=None, op0=ALU.add), reads=[Bc], writes=[Bc])
                S.op("dve", lambda e: e.tensor_scalar(out=gd_t[:, l:l + 1], in0=gvec[:, 96 + l:97 + l], scalar1=1.0 - lam_init,
                                                      scalar2=None, op0=ALU.mult), writes=[Bc])
            S.barrier()

        pid4 = nc.gpsimd.partition_id() % 4
        dyn_off = {}
        for c0_ in (0, 1024):
            dyn_off[c0_] = nc.gpsimd.snap(pid4 * 2048 + c0_, min_val=c0_, max_val=3 * 2048 + c0_)
        dyn_off[2048] = nc.gpsimd.snap(pid4 * NS, min_val=0, max_val=3 * NS)

        def gpre(l, i):
            return (l * 3 + i) * 8

        def gpost(l, i):
            return 48 + (l * 3 + i) * 8

        ag_bufs = {}
        ag_byq = {}

        def ag_chunk(chan, src_ap, dst_ap, reads):
            b_ = S.buf()
            ag_bufs.setdefault(chan.split("_")[0], []).append(b_)
            ag_byq[chan] = b_
            S.dma("pool", chan, lambda e: e.collective_compute("AllGather", ALU.bypass, replica_groups=GROUPS, ins=[src_ap], outs=[dst_ap]),
                  reads=reads, writes=[b_], inc=1)

        def token_phase(l, x_src, x_dst, do_sub, next_g, hT_dst, mix_all, hT_gat=None, ag_l=0):
            Bhs = S.buf()
            with contextlib.ExitStack() as sc:
                wsl = Slots(nc, S, sc, "wsl", 4, [128, WSLOT], BF16)
                stat = {"bank": None}
                if do_sub:
                    mkT_p = sbt(sc, "mkT_p", [128, 8, 256], BF16)
                    mv_p = sbt(sc, "mv_p", [128, 2, D], BF16)
                    mkT_s = sbt(sc, "mkT_s", [128, 8, 256], BF16)
                    mv_s = sbt(sc, "mv_s", [128, 2, D], BF16)
                    Bmk = S.buf()

                def wload(W, r0, kcn, c0, ncols):
                    t, b, i = wsl.next()
                    view = t[:, 0:kcn * ncols].rearrange("p (k n) -> p k n", k=kcn)
                    src = W[r0:r0 + kcn * 128, c0:c0 + ncols].rearrange("(k p) n -> p k n", p=128)
                    S.dma("pool", f"w{i}", lambda e: e.dma_start(out=view, in_=src), writes=[b])
                    return view, b

                deferred = []

                def run_blocks(blocks, prefetch=3):
                    loaded = []
                    nxt = 0
                    for i, blk in enumerate(blocks):
                        while nxt < len(blocks) and nxt < i + prefetch:
                            W, r0, kcn, c0, ncols, _ = blocks[nxt]
                            loaded.append(wload(W, r0, kcn, c0, ncols))
                            nxt += 1
                        view, b = loaded[i]
                        blk[5](view, b)
                        for d_ in list(deferred):
                            d_[0] -= 1
                            if d_[0] <= 0:
                                deferred.remove(d_)
                                d_[1]()
                    for d_ in list(deferred):
                        deferred.remove(d_)
                        d_[1]()

                if do_sub:
                    with contextlib.ExitStack() as sm:
                        mp = sbt(sm, "mp", [128, 2, D], F32)
                        Bmp = S.buf()
                        sqt = sbt(sm, "sqt", [128, D], F32)
                        ss = sbt(sm, "ss", [128, 2], F32)
                        rm = sbt(sm, "rm", [128, 2], F32)
                        gm = sbt(sm, "gm", [128, D], F32)
                        hm = sbt(sm, "hm", [128, 2, D], F32)
                        Bhm = S.buf()
                        hmT = sbt(sm, "hmT", [128, 8, 256], BF16)
                        BhmT = S.buf()
                        mkn = sbt(sm, "mkn", [128, 2, D], F32)
                        Bmkn = S.buf()
                        mvn = sbt(sm, "mvn", [128, 2, D], F32)
                        Bmvn = S.buf()
                        ck = sbt(sm, "ck", [128, 2, D], F32)
                        Bck = S.buf()
                        S.dma("sp", "me", lambda e: e.dma_start(out=mp[:], in_=memp_d.rearrange("(i p) d -> p i d", p=128)), writes=[Bmp])
                        Bgm = S.buf()
                        S.dma("sp", "me2", lambda e: e.dma_start(out=gm[:], in_=gmem_d[l * 128:(l + 1) * 128, :]), writes=[Bgm])
                        S.dma("sp", "ck", lambda e: e.dma_start(out=ck[:], in_=c_memk[l * 256:(l + 1) * 256, :].rearrange("(i p) d -> p i d", p=128)),
                              writes=[Bck])
                        S.dma("pool", "cv", lambda e: e.dma_start(out=mv_s[:], in_=c_memv[l * 256:(l + 1) * 256, :].rearrange("(i p) d -> p i d", p=128)),
                              writes=[S.buf()])
                        Bss = S.buf()
                        for i in range(2):
                            S.op("dve", lambda e: e.tensor_tensor(out=sqt[:], in0=mp[:, i, :], in1=mp[:, i, :], op=ALU.mult), reads=[Bmp], writes=[Bss])
                            S.op("dve", lambda e: e.reduce_sum(out=ss[:, i:i + 1], in_=sqt[:], axis=AX.X), reads=[Bss], writes=[Bss])
                        S.op("act", lambda e: e.activation(out=rm[:], in_=ss[:], func=AF.Ln, scale=1.0 / D, bias=eps_t[:, 0:1]), reads=[Bss], writes=[Bss])
                        S.op("act", lambda e: e.activation(out=rm[:], in_=rm[:], func=AF.Exp, scale=-0.5), reads=[Bss], writes=[Bss])
                        for i in range(2):
                            S.op("dve", lambda e: e.scalar_tensor_tensor(out=hm[:, i, :], in0=mp[:, i, :], scalar=rm[:, i:i + 1], in1=gm[:],
                                                                         op0=ALU.mult, op1=ALU.mult), reads=[Bmp, Bss, Bgm], writes=[Bhm])

                        def transpose_to(src, srcb, dst, dstb):
                            for c8 in range(8):
                                bi = banks.get()
                                for i in range(2):
                                    S.op("pe", lambda e: e.transpose(ps[bi][:, i * 128:(i + 1) * 128], src[:, i, c8 * 128:(c8 + 1) * 128], ident_f[:]),
                                         reads=[srcb], writes=[pb[bi]])
                                S.op("act", lambda e: e.activation(out=dst[:, c8, :], in_=ps[bi][:, 0:256], func=AF.Copy), reads=[pb[bi]], writes=[dstb])

                        transpose_to(hm, Bhm, hmT, BhmT)
                        transpose_to(ck, Bck, mkT_s, Bmk)
                        blocks = []

                        def mk_block(dstn, dstb, cb):
                            def f(view, wb):
                                for i in range(2):
                                    bi = banks.get()
                                    for kc in range(8):
                                        S.op("pe", lambda e: e.matmul(ps[bi][:, :], lhsT=hmT[:, kc, i * 128:(i + 1) * 128], rhs=view[:, kc, :],
                                                                      start=(kc == 0), stop=(kc == 7)), reads=[BhmT, wb], writes=[pb[bi]])
                                    S.op("act", lambda e: e.activation(out=dstn[:, i, cb * 512:(cb + 1) * 512], in_=ps[bi][:, :], func=AF.Copy),
                                         reads=[pb[bi]], writes=[dstb])
                            return f
                        for cb in range(2):
                            blocks.append((wk_d, l * D, 8, cb * 512, 512, mk_block(mkn, Bmkn, cb)))
                        for cb in range(2):
                            blocks.append((wv_d, l * D, 8, cb * 512, 512, mk_block(mvn, Bmvn, cb)))
                        run_blocks(blocks)
                        S.dma("sp", "mo0", lambda e: e.dma_start(out=memk_o[l * 256:(l + 1) * 256, :].rearrange("(i p) d -> p i d", p=128), in_=mkn[:]),
                              reads=[Bmkn])
                        S.dma("sp", "mo1", lambda e: e.dma_start(out=memv_o[l * 256:(l + 1) * 256, :].rearrange("(i p) d -> p i d", p=128), in_=mvn[:]),
                              reads=[Bmvn])
                        transpose_to(mkn, Bmkn, mkT_p, Bmk)
                        for i in range(2):
                            S.op("dve", lambda e: e.tensor_copy(out=mv_p[:, i, :], in_=mvn[:, i, :]), reads=[Bmvn], writes=[Bmk])
                        S.barrier(skip_prefix="ag")

                def make_ctx(W, tiles_, tag):
                    stat = {"bank": None}
                    sg_bufs = {}
                    Bhs = S.buf()
                    xT = sbt(sc, "xTt", [128, 8, W], F32)
                    Bx = [S.buf() for _ in range(8)]
                    osb = sbt(sc, "osb", [128, 8, W], F32)
                    Bo = [S.buf() for _ in range(8)]
                    hT = sbt(sc, "hTt", [128, 8, W], BF16)
                    Bh = [S.buf() for _ in range(8)]
                    sqs = Slots(nc, S, sc, "sqs", 2, [128, W], BF16)
                    lnt = sbt(sc, "lnt", [128, W], F32)
                    Bln = S.buf()
                    rstd = sbt(sc, "rstd", [128, W], F32)
                    Brs = S.buf()
                    if do_sub:
                        mixT = sbt(sc, "mixTt", [128, 8, W], BF16)
                        Bm = S.buf()
                        qT = sbt(sc, "qTt", [128, 8, W], BF16)
                        Bq = [S.buf() for _ in range(8)]
                        oT = sbt(sc, "oTt", [128, 8, W], BF16)
                        Boo = [S.buf() for _ in range(8)]
                        actT = sbt(sc, "actT", [128, FC, W], BF16)
                        Ba = [S.buf() for _ in range(FC)]
                        Psl = Slots(nc, S, sc, "Psl", 2, [128, 2, W], BF16)
                        rdn = Slots(nc, S, sc, "rdn", 2, [128, W], F32)
                        sgs = Slots(nc, S, sc, "sgs", 2, [128, 4, W], F32)
                    def stats_add(src_ap, tt, n, src_bufs):
                        sq, sqb, _ = sqs.next()
                        S.op("act", lambda e: e.activation(out=sq[:, :tt], in_=src_ap, func=AF.Square), reads=src_bufs, writes=[sqb])
                        if n == 0:
                            stat["bank"] = banks.get(pin=True)
                        bi = stat["bank"]
                        S.op("pe", lambda e: e.matmul(ps[bi][:, :tt], lhsT=ones[:, :], rhs=sq[:, :tt], start=(n == 0), stop=(n == 7)),
                             reads=[sqb], writes=[pb[bi]])

                    def rstd_compute(tt):
                        bi = stat["bank"]
                        S.op("act", lambda e: e.activation(out=lnt[:, :tt], in_=ps[bi][:, :tt], func=AF.Ln, scale=1.0 / D, bias=eps_t[:, 0:1]),
                             reads=[pb[bi]], writes=[Bln])
                        S.op("act", lambda e: e.activation(out=rstd[:, :tt], in_=lnt[:, :tt], func=AF.Exp, scale=-0.5),
                             reads=[Bln], writes=[Brs])
                        banks.unpin(bi)
                        stat["bank"] = None

                    def sub_out(n, bi, tt):
                        S.op("act", lambda e: e.activation(out=osb[:, n, :tt], in_=ps[bi][:, :tt], func=AF.Copy), reads=[pb[bi]], writes=[Bo[n]])
                        stats_add(ps[bi][:, :tt], tt, n, [pb[bi]])

                    def post_norm(gb, tt):
                        rstd_compute(tt)
                        for n in range(8):
                            S.op("dve", lambda e: e.tensor_tensor(out=osb[:, n, :tt], in0=osb[:, n, :tt], in1=rstd[:, :tt], op=ALU.mult),
                                 reads=[Bo[n], Brs], writes=[Bo[n]])
                        for n in range(8):
                            S.op("dve", lambda e: e.scalar_tensor_tensor(out=xT[:, n, :tt], in0=osb[:, n, :tt], scalar=gvec[:, gb + n:gb + n + 1],
                                                                         in1=xT[:, n, :tt], op0=ALU.mult, op1=ALU.add),
                                 reads=[Bo[n], Bx[n]], writes=[Bx[n]])

                    def pre_norm(gb, tt, dst, dstb):
                        for n in range(8):
                            stats_add(xT[:, n, :tt], tt, n, [Bx[n]])
                        rstd_compute(tt)
                        for n in range(8):
                            S.op("dve", lambda e: e.scalar_tensor_tensor(out=dst[:, n, :tt], in0=xT[:, n, :tt], scalar=gvec[:, gb + n:gb + n + 1],
                                                                         in1=rstd[:, :tt], op0=ALU.mult, op1=ALU.mult),
                                 reads=[Bx[n], Brs], writes=[dstb[n]])

                    blocks = []
                    for (c0, tt, kind) in tiles_:
                        def load_tile(c0=c0, tt=tt, kind=kind):
                            S.dma("sp", "xT" + tag, lambda e: e.dma_start(out=xT[:, :, :tt], in_=x_src[:, c0:c0 + tt].rearrange("(k p) n -> p k n", p=128)),
                                  writes=Bx)
                            if do_sub:
                                if kind == "p":
                                    off = dyn_off[(c0 // 1024) * 1024]
                                    cc = c0 % 1024
                                    src = mix_all[0][bass.ds(off, 1024), cc:cc + tt].rearrange("(c p) n -> p c n", p=128)
                                else:
                                    src = mix_all[1].rearrange("(c p) n -> p c n", p=128)[:, :, bass.ds(dyn_off[2048], tt)]
                                S.dma("pool", "mx" + tag, lambda e: e.dma_start(out=mixT[:, :, :tt], in_=src), reads=ag_bufs.get(f"agm{l}", []), writes=[Bm])

                        def end_tile(c0=c0, tt=tt):
                            if next_g is not None:
                                pre_norm(next_g, tt, osb, Bo)
                                if tt == 512:
                                    for hf in range(2):
                                        q = c0 // 256 + hf
                                        S.dma("sp", "hs" + tag, lambda e: e.dma_start(out=hT_dst[0][q * D:(q + 1) * D, :].rearrange("(k p) n -> p k n", p=128),
                                                                                in_=osb[:, :, 256 * hf:256 * (hf + 1)]), reads=Bo, writes=[Bhs], skip_waw=True)
                                    for hf in range(2):
                                        q = c0 // 256 + hf
                                        deferred.append([3, lambda q=q: ag_chunk(f"agh{ag_l}_{q}", hT_dst[0][q * D:(q + 1) * D, :], hT_gat[0][q * 4 * D:(q + 1) * 4 * D, :], [Bhs])])
                                else:
                                    S.dma("sp", "hs" + tag, lambda e: e.dma_start(out=hT_dst[1].rearrange("(k p) n -> p k n", p=128), in_=osb[:, :, :tt]),
                                          reads=Bo, writes=[Bhs], skip_waw=True)
                                    deferred.append([3, lambda: ag_chunk(f"agh{ag_l}_8", hT_dst[1], hT_gat[1], [Bhs])])
                            if x_dst is not None:
                                S.dma("sp", "xs" + tag, lambda e: e.dma_start(out=x_dst[:, c0:c0 + tt].rearrange("(k p) n -> p k n", p=128), in_=xT[:, :, :tt]),
                                      reads=Bx)

                        if not do_sub:
                            load_tile()
                            end_tile()
                            continue

                        first = [True]

                        def proj_block(rhsT, rhsb, cb, tt, consume, after=None, pre=None):
                            def f(view, wb):
                                if pre is not None:
                                    pre()
                                for nl in range(4):
                                    n = cb * 4 + nl
                                    bi = banks.get()
                                    for kc in range(8):
                                        S.op("pe", lambda e: e.matmul(ps[bi][:, :tt], lhsT=view[:, kc, nl * 128:(nl + 1) * 128], rhs=rhsT[:, kc, :tt],
                                                                      start=(kc == 0), stop=(kc == 7)), reads=[rhsb[kc], wb], writes=[pb[bi]])
                                    consume(n, bi)
                                if after is not None:
                                    after()
                            return f

                        def attn_core(tt=tt, kind=kind):
                            mkT = mkT_p if kind == "p" else mkT_s
                            mvv = mv_p if kind == "p" else mv_s
                            for hm_ in range(4):
                                sc_ = [banks.get(), banks.get()]
                                for mc in range(2):
                                    for dc in range(2):
                                        S.op("pe", lambda e: e.matmul(ps[sc_[mc]][:, :tt], lhsT=mkT[:, 2 * hm_ + dc, mc * 128:(mc + 1) * 128],
                                                                      rhs=qT[:, 2 * hm_ + dc, :tt], start=(dc == 0), stop=(dc == 1)),
                                             reads=[Bmk, Bq[2 * hm_ + dc]], writes=[pb[sc_[mc]]])
                                P, Pb, _ = Psl.next()
                                for mc in range(2):
                                    S.op("act", lambda e: e.activation(out=P[:, mc, :tt], in_=ps[sc_[mc]][:, :tt], func=AF.Exp, scale=1.0 / 16.0),
                                         reads=[pb[sc_[mc]]], writes=[Pb])
                                dn = banks.get()
                                for mc in range(2):
                                    S.op("pe", lambda e: e.matmul(ps[dn][:, :tt], lhsT=ones[:, :], rhs=P[:, mc, :tt], start=(mc == 0), stop=(mc == 1)),
                                         reads=[Pb], writes=[pb[dn]])
                                ob = [banks.get(), banks.get()]
                                for dc in range(2):
                                    for mc in range(2):
                                        S.op("pe", lambda e: e.matmul(ps[ob[dc]][:, :tt], lhsT=mvv[:, mc, hm_ * 256 + dc * 128:hm_ * 256 + (dc + 1) * 128],
                                                                      rhs=P[:, mc, :tt], start=(mc == 0), stop=(mc == 1)),
                                             reads=[Bmk, Pb], writes=[pb[ob[dc]]])
                                rd, rdb, _ = rdn.next()
                                S.op("dve", lambda e: e.reciprocal(out=rd[:, :tt], in_=ps[dn][:, :tt]), reads=[pb[dn]], writes=[rdb])
                                for dc in range(2):
                                    S.op("dve", lambda e: e.tensor_tensor(out=oT[:, 2 * hm_ + dc, :tt], in0=ps[ob[dc]][:, :tt], in1=rd[:, :tt], op=ALU.mult),
                                         reads=[pb[ob[dc]], rdb], writes=[Boo[2 * hm_ + dc]])

                        for cb in range(2):
                            blocks.append((w_out_p, l * D, 8, cb * 512, 512,
                                           proj_block(mixT, [Bm] * 8, cb, tt, lambda n, bi, tt=tt: sub_out(n, bi, tt),
                                                      after=(lambda tt=tt: (post_norm(gpost(l, 0), tt), pre_norm(gpre(l, 1), tt, hT, Bh))) if cb == 1 else None,
                                                      pre=load_tile if cb == 0 else None)))
                        def q_consume(n, bi, tt=tt):
                            S.op("act", lambda e: e.activation(out=qT[:, n, :tt], in_=ps[bi][:, :tt], func=AF.Copy), reads=[pb[bi]], writes=[Bq[n]])
                        for cb in range(2):
                            blocks.append((wq_d, l * D, 8, cb * 512, 512,
                                           proj_block(hT, Bh, cb, tt, q_consume, after=attn_core if cb == 1 else None)))
                        for cb in range(2):
                            blocks.append((wo_d, l * D, 8, cb * 512, 512,
                                           proj_block(oT, Boo, cb, tt, lambda n, bi, tt=tt: sub_out(n, bi, tt),
                                                      after=(lambda tt=tt: (post_norm(gpost(l, 1), tt), pre_norm(gpre(l, 2), tt, hT, Bh))) if cb == 1 else None)))
                        gslot = {}
                        for cb in range(6):
                            ncols = 512 if cb < 5 else 256

                            def gate_f(view, wb, cb=cb, ncols=ncols, tt=tt):
                                sg, _sgb0, si_ = sgs.next()
                                sgb = sg_bufs.setdefault(si_, [S.buf() for _ in range(4)])
                                gslot[cb] = (sg, sgb)
                                for jl in range(ncols // 128):
                                    bi = banks.get()
                                    for kc in range(8):
                                        S.op("pe", lambda e: e.matmul(ps[bi][:, :tt], lhsT=view[:, kc, jl * 128:(jl + 1) * 128], rhs=hT[:, kc, :tt],
                                                                      start=(kc == 0), stop=(kc == 7)), reads=[Bh[kc], wb], writes=[pb[bi]])
                                    S.op("act", lambda e: e.activation(out=sg[:, jl, :tt], in_=ps[bi][:, :tt], func=AF.Silu), reads=[pb[bi]], writes=[sgb[jl]])

                            def up_f(view, wb, cb=cb, ncols=ncols, tt=tt):
                                sg, sgb = gslot[cb]
                                for jl in range(ncols // 128):
                                    j = cb * 4 + jl
                                    bi = banks.get()
                                    for kc in range(8):
                                        S.op("pe", lambda e: e.matmul(ps[bi][:, :tt], lhsT=view[:, kc, jl * 128:(jl + 1) * 128], rhs=hT[:, kc, :tt],
                                                                      start=(kc == 0), stop=(kc == 7)), reads=[Bh[kc], wb], writes=[pb[bi]])
                                    S.op("dve", lambda e: e.tensor_tensor(out=actT[:, j, :tt], in0=ps[bi][:, :tt], in1=sg[:, jl, :tt], op=ALU.mult),
                                         reads=[pb[bi], sgb[jl]], writes=[Ba[j]])
                            blocks.append((wg_d, l * D, 8, cb * 512, ncols, gate_f))
                            blocks.append((wu_d, l * D, 8, cb * 512, ncols, up_f))
                        for n in range(8):
                            def down_f(view, wb, n=n, tt=tt, end_tile=end_tile):
                                bi = banks.get()
                                for j in range(FC):
                                    S.op("pe", lambda e: e.matmul(ps[bi][:, :tt], lhsT=view[:, j, :], rhs=actT[:, j, :tt],
                                                                  start=(j == 0), stop=(j == FC - 1)), reads=[Ba[j], wb], writes=[pb[bi]])
                                sub_out(n, bi, tt)
                                if n == 7:
                                    post_norm(gpost(l, 2), tt)
                                    end_tile()
                            blocks.append((wd_d, l * DFF, FC, n * 128, 128, down_f))
                    return blocks

                if do_sub:
                    bm = make_ctx(512, TILES[:4], "m")
                    bs = make_ctx(NS, TILES[4:], "s")
                    nz = len(bs)
                    merged = bm[:len(bm) - nz]
                    for k_ in range(nz):
                        a_, b_ = bm[len(bm) - nz + k_], bs[k_]
                        assert a_[:5] == b_[:5]
                        merged.append(a_[:5] + ((lambda view, wb, fa=a_[5], fb=b_[5]: (fa(view, wb), fb(view, wb))),))
                    run_blocks(merged)
                else:
                    make_ctx(512, TILES, "m")
                S.barrier(skip_prefix="ag")

        def head_phase(l):
            hall = hT_all[l]

            def mix_dst(row0, nrows, col, n):
                if col < SEQ:
                    q, cc = col // 1024, col % 1024
                    return mixT_src[l][0][q * 256 + row0:q * 256 + row0 + nrows, cc:cc + n]
                return mixT_src[l][1][row0:row0 + nrows, col - SEQ:col - SEQ + n]
            with contextlib.ExitStack() as sc:
                sqT = sbt(sc, "sqT", [64, HC], BF16)
                skTn = sbt(sc, "skTn", [64, HC], BF16)
                dqT = sbt(sc, "dqT", [128, HC], BF16)
                dkT = sbt(sc, "dkT", [128, HC], BF16)
                sv = sbt(sc, "svr", [128, 68, 64], BF16)
                dv = sbt(sc, "dvr", [128, 68, 128], BF16)
                with contextlib.ExitStack() as sp:
                    w_sb = sbt(sp, "w_sb", [128, 8, 640], BF16)
                    Bw = S.buf()
                    pw_sb = sbt(sp, "pw_sb", [64, 64], BF16)
                    hTs = Slots(nc, S, sp, "hTs", 2, [128, 8, 512], BF16)
                    kst = Slots(nc, S, sp, "kst", 2, [128, 512], F32)
                    vst = Slots(nc, S, sp, "vst", 2, [128, 4, 192], F32)
                    uts = Slots(nc, S, sp, "uts", 2, [64, 528], F32)
                    us = sbt(sp, "us", [64, 4, 48], F32)
                    Bus = S.buf()
                    s1 = sbt(sp, "s1", [64, 528], F32)
                    s2 = sbt(sp, "s2", [64, 528], F32)
                    s3 = sbt(sp, "s3", [64, 528], F32)
                    s4 = sbt(sp, "s4", [64, 528], F32)
                    acc = sbt(sp, "acc", [64, 512], F32)
                    dT = sbt(sp, "dT", [64, 512], BF16)
                    Bsc = S.buf()
                    mps = Slots(nc, S, sp, "mps", 2, [64, 512], F32)
                    S.dma("pool", "wi", lambda e: e.dma_start(out=w_sb[:], in_=w_in_h[l * D:(l + 1) * D, :].rearrange("(k p) n -> p k n", p=128)),
                          writes=[Bw])
                    Bpw = S.buf()
                    S.dma("pool", "wi2", lambda e: e.dma_start(out=pw_sb[:], in_=pool_w_h[l * 64:(l + 1) * 64, :]), writes=[Bpw])
                    S.op("dve", lambda e: e.memset(us[:], 0.0), writes=[Bus])
                    if 'c' not in DBG:
                      S.dma("sp", "spl", lambda e: e.dma_start(out=us[:, :, 1:16],
                                                             in_=spoolT[l * 256:(l + 1) * 256, 0:15].rearrange("(r p) n -> p r n", p=64)),
                            writes=[Bus])

                    if l == 0:
                        xfs = Slots(nc, S, sp, "xfs", 2, [128, 8, 512], F32)
                        nsq = Slots(nc, S, sp, "nsq", 2, [128, 512], BF16)
                        nln = sbt(sp, "nln", [128, 512], F32)
                        nrs = sbt(sp, "nrs", [128, 512], F32)
                        Bnr = S.buf()

                    def load_x(i):
                        xf, xb_, si = xfs.next()
                        nt_ = 512 if i < 16 else 128
                        S.dma("sp", f"xf{si}", lambda e: e.dma_start(out=xf[:, :, :nt_], in_=xT_full[:, 512 * i:512 * i + nt_].rearrange("(k p) n -> p k n", p=128)),
                              writes=[xb_])
                        return xf, xb_, nt_

                    def norm_x(xl):
                        xf, xb_, nt_ = xl
                        t, b, si = hTs.next()
                        bi = banks.get(pin=True)
                        for n in range(8):
                            sq, sqb, _ = nsq.next()
                            S.op("act", lambda e: e.activation(out=sq[:, :nt_], in_=xf[:, n, :nt_], func=AF.Square), reads=[xb_], writes=[sqb])
                            S.op("pe", lambda e: e.matmul(ps[bi][:, :nt_], lhsT=ones[:, :], rhs=sq[:, :nt_], start=(n == 0), stop=(n == 7)),
                                 reads=[sqb], writes=[pb[bi]])
                        S.op("act", lambda e: e.activation(out=nln[:, :nt_], in_=ps[bi][:, :nt_], func=AF.Ln, scale=1.0 / D, bias=eps_t[:, 0:1]),
                             reads=[pb[bi]], writes=[Bnr])
                        banks.unpin(bi)
                        S.op("act", lambda e: e.activation(out=nrs[:, :nt_], in_=nln[:, :nt_], func=AF.Exp, scale=-0.5), reads=[Bnr], writes=[Bnr])
                        gb = gpre(0, 0)
                        for n in range(8):
                            S.op("dve", lambda e: e.scalar_tensor_tensor(out=t[:, n, :nt_], in0=xf[:, n, :nt_], scalar=gvec[:, gb + n:gb + n + 1],
                                                                         in1=nrs[:, :nt_], op0=ALU.mult, op1=ALU.mult),
                                 reads=[xb_, Bnr], writes=[b])
                        return t, b

                    def load_h(i):
                        t, b, si = hTs.next()
                        if i < 16:
                            r, c0 = i // 4, (i % 4) * 512
                            for hf in range(2):
                                q = c0 // 256 + hf
                                row = q * 4 * D + r * D
                                S.dma("pool", f"hT{si}", lambda e: e.dma_start(
                                    out=t[:, :, 256 * hf:256 * (hf + 1)], in_=hall[0][row:row + D, :].rearrange("(k p) n -> p k n", p=128)),
                                    reads=[ag_byq[f"agh{l}_{q}"]], writes=[b], skip_waw=True)
                        else:
                            for r in range(4):
                                S.dma("pool", f"hT{si}", lambda e: e.dma_start(
                                    out=t[:, :, 32 * r:32 * r + 32], in_=hall[1][r * D:(r + 1) * D, :].rearrange("(k p) n -> p k n", p=128)),
                                    reads=[ag_byq[f"agh{l}_8"]], writes=[b], skip_waw=True)
                        return t, b

                    def pool_scan(u_ap, nt, first, out_cols, ub):
                        n = 16 + nt
                        S.op("dve", lambda e: e.tensor_tensor(out=s1[:, 1:n], in0=u_ap[:, 1:n], in1=u_ap[:, 0:n - 1], op=ALU.add), reads=[ub], writes=[Bsc])
                        S.op("dve", lambda e: e.tensor_tensor(out=s2[:, 3:n], in0=s1[:, 3:n], in1=s1[:, 1:n - 2], op=ALU.add), reads=[Bsc], writes=[Bsc])
                        S.op("dve", lambda e: e.tensor_tensor(out=s3[:, 7:n], in0=s2[:, 7:n], in1=s2[:, 3:n - 4], op=ALU.add), reads=[Bsc], writes=[Bsc])
                        S.op("dve", lambda e: e.tensor_tensor(out=s4[:, 15:n], in0=s3[:, 15:n], in1=s3[:, 7:n - 8], op=ALU.add), reads=[Bsc], writes=[Bsc])
                        S.op("dve", lambda e: e.tensor_scalar(out=acc[:, :nt], in0=s1[:, 16:n], scalar1=pvec_sb[:, 2:3], scalar2=None, op0=ALU.mult),
                             reads=[Bsc], writes=[Bsc])
                        for k, sk_ in enumerate((s2, s3, s4)):
                            S.op("dve", lambda e: e.scalar_tensor_tensor(out=acc[:, :nt], in0=sk_[:, 16:n], scalar=pvec_sb[:, 3 + k:4 + k], in1=acc[:, :nt],
                                                                         op0=ALU.mult, op1=ALU.add), reads=[Bsc], writes=[Bsc])
                        if first:
                            S.op("dve", lambda e: e.tensor_tensor(out=acc[:, 0:16], in0=acc[:, 0:16], in1=pvec_sb[:, 8:24], op=ALU.mult), reads=[Bsc], writes=[Bsc])
                        S.op("dve", lambda e: e.tensor_tensor(out=dT[:, :nt], in0=acc[:, :nt], in1=u_ap[:, 16:n], op=ALU.subtract), reads=[Bsc, ub], writes=[Bsc])
                        bi = banks.get()
                        S.op("pe", lambda e: e.matmul(ps[bi][0:64, :nt], lhsT=pw_sb[:, :], rhs=dT[:, :nt], start=True, stop=True),
                             reads=[Bsc, Bpw], writes=[pb[bi]])
                        m, mb, mi = mps.next()
                        S.op("dve", lambda e: e.tensor_scalar(out=m[:, :nt], in0=ps[bi][0:64, :nt], scalar1=pvec_sb[:, l:l + 1], scalar2=None, op0=ALU.mult),
                             reads=[pb[bi]], writes=[mb])
                        S.dma("sp", f"mp{mi}", lambda e: e.dma_start(out=mix_dst(0, 64, out_cols, nt), in_=m[:, :nt]), reads=[mb])

                    prev_u = None
                    if l == 0:
                        xq = [load_x(0), load_x(1)]
                        hcur = norm_x(xq[0])
                    else:
                        nxt = load_h(0)
                    for i in range(17):
                        if l == 0:
                            hT, hb = hcur
                            if i < 16:
                                hcur = norm_x(xq[(i + 1) % 2])
                                if i + 2 <= 16:
                                    xq[i % 2] = load_x(i + 2)
                        else:
                            hT, hb = nxt
                            if i < 16:
                                nxt = load_h(i + 1)
                        nt = 512 if i < 16 else 128
                        t0 = 512 * i
                        u_t, u_b, _ = uts.next()
                        for (nm, wc0, M) in (("sq", 0, 64), ("sk", 64, 64), ("u", 128, 64), ("dq", 192, 128), ("dk", 320, 128)):
                            if 'e' in DBG:
                                continue
                            if 'd' in DBG and nm in ("sk", "dk"):
                                continue
                            bi = banks.get()
                            for kc in range(8):
                                S.op("pe", lambda e: e.matmul(ps[bi][:M, :nt], lhsT=w_sb[:, kc, wc0:wc0 + M], rhs=hT[:, kc, :nt],
                                                              start=(kc == 0), stop=(kc == 7)), reads=[Bw, hb], writes=[pb[bi]])
                            if nm == "sq":
                                S.op("act", lambda e: e.activation(out=sqT[:, t0:t0 + nt], in_=ps[bi][:64, :nt], func=AF.Copy), reads=[pb[bi]])
                            elif nm == "dq":
                                S.op("act", lambda e: e.activation(out=dqT[:, t0:t0 + nt], in_=ps[bi][:, :nt], func=AF.Copy), reads=[pb[bi]])
                            elif nm == "sk":
                                k_, kb_, ki = kst.next()
                                S.op("act", lambda e: e.activation(out=k_[:64, :nt], in_=ps[bi][:64, :nt], func=AF.Copy), reads=[pb[bi]], writes=[kb_])
                                S.op("dve", lambda e: e.tensor_scalar(out=skTn[:, t0:t0 + nt], in0=k_[:64, :nt], scalar1=-0.125, scalar2=None, op0=ALU.mult),
                                     reads=[kb_])
                                if 'h' not in DBG:
                                    S.dma("sp", f"ks{ki}", lambda e: e.dma_start(out=skT_o[l * 64:(l + 1) * 64, t0:t0 + nt], in_=k_[:64, :nt]), reads=[kb_])
                            elif nm == "dk":
                                k_, kb_, ki = kst.next()
                                S.op("act", lambda e: e.activation(out=k_[:, :nt], in_=ps[bi][:, :nt], func=AF.Copy), reads=[pb[bi]], writes=[kb_])
                                S.op("dve", lambda e: e.tensor_copy(out=dkT[:, t0:t0 + nt], in_=k_[:, :nt]), reads=[kb_])
                                if 'h' not in DBG:
                                    S.dma("sp", f"ks{ki}", lambda e: e.dma_start(out=dkT_o[l * 128:(l + 1) * 128, t0:t0 + nt], in_=k_[:, :nt]), reads=[kb_])
                            else:
                                if i < 16:
                                    S.op("act", lambda e: e.activation(out=u_t[:, 16:16 + nt], in_=ps[bi][:64, :nt], func=AF.Copy), reads=[pb[bi]], writes=[u_b])
                                else:
                                    for r in range(4):
                                        S.op("act", lambda e: e.activation(out=us[:, r, 16:48], in_=ps[bi][:64, 32 * r:32 * r + 32], func=AF.Copy),
                                             reads=[pb[bi]], writes=[Bus])
                        v_, vb_, vi = vst.next()
                        if 'b' in DBG:
                            pass
                        elif i < 16:
                            for half in range(2):
                                bi = banks.get()
                                for s_ in range(2):
                                    sub = half * 2 + s_
                                    for kc in range(8):
                                        S.op("pe", lambda e: e.matmul(ps[bi][:, s_ * 192:(s_ + 1) * 192], lhsT=hT[:, kc, sub * 128:(sub + 1) * 128],
                                                                      rhs=w_sb[:, kc, 448:640], start=(kc == 0), stop=(kc == 7)), reads=[Bw, hb], writes=[pb[bi]])
                                for s_ in range(2):
                                    sub = half * 2 + s_
                                    blk = 4 * i + sub
                                    S.op("dve", lambda e: e.tensor_copy(out=v_[:, sub, :], in_=ps[bi][:, s_ * 192:(s_ + 1) * 192]), reads=[pb[bi]], writes=[vb_])
                                    S.op("pool", lambda e: e.tensor_copy(out=sv[:, blk, :], in_=v_[:, sub, 0:64]), reads=[vb_])
                                    S.op("pool", lambda e: e.tensor_copy(out=dv[:, blk, :], in_=v_[:, sub, 64:192]), reads=[vb_])
                            S.dma("sp", f"vs{vi}", lambda e: e.dma_start(
                                out=sv_o[l * HC + t0:l * HC + t0 + 512, :].rearrange("(s p) d -> p s d", p=128), in_=v_[:, :, 0:64]), reads=[vb_])
                            S.dma("sp", f"vs{vi}", lambda e: e.dma_start(
                                out=dv_o[l * HC + t0:l * HC + t0 + 512, :].rearrange("(s p) d -> p s d", p=128), in_=v_[:, :, 64:192]), reads=[vb_])
                        else:
                            for r in range(4):
                                bi = banks.get()
                                for kc in range(8):
                                    S.op("pe", lambda e: e.matmul(ps[bi][:32, 0:192], lhsT=hT[:, kc, 32 * r:32 * r + 32], rhs=w_sb[:, kc, 448:640],
                                                                  start=(kc == 0), stop=(kc == 7)), reads=[Bw, hb], writes=[pb[bi]])
                                S.op("dve", lambda e: e.tensor_copy(out=v_[:32, r, :], in_=ps[bi][:32, 0:192]), reads=[pb[bi]], writes=[vb_])
                                S.op("pool", lambda e: e.tensor_copy(out=sv[:32, 64 + r, :], in_=v_[:32, r, 0:64]), reads=[vb_])
                                S.op("pool", lambda e: e.tensor_copy(out=dv[:32, 64 + r, :], in_=v_[:32, r, 64:192]), reads=[vb_])
                            S.dma("sp", f"vs{vi}", lambda e: e.dma_start(
                                out=sv_o[l * HC + SEQ:l * HC + SEQ + 128, :].rearrange("(r p) d -> p r d", p=32), in_=v_[:32, :, 0:64]), reads=[vb_])
                            S.dma("sp", f"vs{vi}", lambda e: e.dma_start(
                                out=dv_o[l * HC + SEQ:l * HC + SEQ + 128, :].rearrange("(r p) d -> p r d", p=32), in_=v_[:32, :, 64:192]), reads=[vb_])
                        if 'a' in DBG:
                            pass
                        elif i < 16:
                            if i == 0:
                                S.op("dve", lambda e: e.memset(u_t[:, 0:16], 0.0), writes=[u_b])
                            else:
                                pu, pub = prev_u
                                S.op("dve", lambda e: e.tensor_copy(out=u_t[:, 0:16], in_=pu[:, 512:528]), reads=[pub], writes=[u_b])
                            pool_scan(u_t, 512, i == 0, t0, u_b)
                            if i == 15:
                                S.dma("sp", "po", lambda e: e.dma_start(out=pool_o[l * 64:(l + 1) * 64, 0:15], in_=u_t[:, 16 + 497:16 + 512]), reads=[u_b])
                            prev_u = (u_t, u_b)
                        else:
                            for r in range(4):
                                pool_scan(us[:, r, :], 32, False, SEQ + 32 * r, Bus)
                                S.dma("sp", "po", lambda e: e.dma_start(out=pool_o[l * 64:(l + 1) * 64, 15 + 15 * r:30 + 15 * r], in_=us[:, r, 33:48]), reads=[Bus])
                        if STOP_AFTER == 31 and i == 0:
                            raise _Stop()
                    S.barrier(skip_prefix="ag")
                    if STOP_AFTER == 32:
                        raise _Stop()
                with contextlib.ExitStack() as sa:
                    Es = Slots(nc, S, sa, "Es", 3, [128, 512], F32)
                    Zs = Slots(nc, S, sa, "Zs", 6, [128, 512], F32)
                    Ars = Slots(nc, S, sa, "Ars", 3, [128, 512], F32)
                    Ls = Slots(nc, S, sa, "Ls", 4, [128, 512], BF16)
                    As = Slots(nc, S, sa, "As", 3, [128, 512], BF16)
                    P1s = Slots(nc, S, sa, "P1s", 3, [128, 512], BF16)
                    P2s = Slots(nc, S, sa, "P2s", 3, [128, 512], BF16)
                    Lsum = sbt(sa, "Lsum", [128, 512], BF16)
                    BLs = S.buf()
                    pac = sbt(sa, "pac", [128, 1024], BF16)
                    pa1 = pac[:, 0:512]
                    pa2 = pac[:, 512:1024]
                    Pc = Slots(nc, S, sa, "Pc", 3, [128, 1024], BF16)
                    P2bufs = [S.buf() for _ in range(3)]
                    Bpa = S.buf()
                    msb = Slots(nc, S, sa, "msb", 2, [64, 512], F32)
                    mdf = Slots(nc, S, sa, "mdf", 2, [128, 512], F32)
                    f1 = sbt(sa, "f1", [128, 512], F32)
                    f2 = sbt(sa, "f2", [128, 512], F32)
                    f3 = sbt(sa, "f3", [128, 512], F32)
                    f4 = sbt(sa, "f4", [128, 512], F32)
                    fsq = sbt(sa, "fsq", [128, 512], BF16)
                    Bf = S.buf()
                    skTp = sbt(sa, "skTp", [64, PAST], BF16)
                    svp = sbt(sa, "svp", [128, 16, 64], BF16)
                    dkTp = sbt(sa, "dkTp", [128, PAST], BF16)
                    dvp = sbt(sa, "dvp", [128, 16, 128], BF16)
                    Bca = [S.buf() for _ in range(4)]
                    Bmsw = [S.buf(), S.buf()]
                    Bmdw = [S.buf(), S.buf()]

                    def run_pipelined(gens, width):
                        active = []
                        it = iter(gens)
                        more = True
                        while True:
                            if more and len(active) < width:
                                try:
                                    active.append(next(it))
                                except StopIteration:
                                    more = False
                            if not active:
                                break
                            for g_ in list(active):
                                try:
                                    next(g_)
                                except StopIteration:
                                    active.remove(g_)

                    def sb_attend(q_ap, nq, blocks, out_cols):
                        Oi = banks.get(pin=True)
                        nb = len(blocks)

                        def unit(idx, blk):
                            kT, vv, hs, mask, rb = blk
                            zi = banks.get()
                            S.op("pe", lambda e: e.matmul(ps[zi][:hs, :nq], lhsT=kT, rhs=q_ap, start=True, stop=True), reads=rb, writes=[pb[zi]])
                            yield
                            zs, zsb, _ = Zs.next()
                            S.op("dve", lambda e: e.tensor_copy(out=zs[:hs, :nq], in_=ps[zi][:hs, :nq]), reads=[pb[zi]], writes=[zsb])
                            yield
                            E, Eb, _ = Es.next()
                            S.op("act", lambda e: e.activation(out=E[:hs, :nq], in_=zs[:hs, :nq], func=AF.Exp, scale=-1.0), reads=[zsb], writes=[Eb])
                            yield
                            Lp, Lb, _ = Ls.next()
                            S.op("act", lambda e: e.activation(out=Lp[:hs, :nq], in_=E[:hs, :nq], func=AF.Ln, bias=one_t[:hs, 0:1]), reads=[Eb], writes=[Lb])
                            if mask is not None:
                                S.op("pool", lambda e: e.tensor_tensor(out=Lp[:hs, :nq], in0=Lp[:hs, :nq], in1=mask, op=ALU.mult), reads=[Lb], writes=[Lb])
                            yield
                            si = banks.get()
                            S.op("pe", lambda e: e.matmul(ps[si][:hs, :nq], lhsT=triT[:hs, :hs], rhs=Lp[:hs, :nq], start=True, stop=(idx == 0)),
                                 reads=[Lb], writes=[pb[si]])
                            if idx > 0:
                                S.op("pe", lambda e: e.matmul(ps[si][:hs, :nq], lhsT=ones[:, :hs], rhs=Lsum[:, :nq], start=False, stop=True),
                                     reads=[BLs], writes=[pb[si]])
                            if idx < nb - 1:
                                if idx == 0:
                                    if hs < 128:
                                        S.op("dve", lambda e: e.memset(Lsum[:, :nq], 0.0), writes=[BLs])
                                    S.op("dve", lambda e: e.tensor_copy(out=Lsum[:hs, :nq], in_=Lp[:hs, :nq]), reads=[Lb], writes=[BLs])
                                else:
                                    S.op("dve", lambda e: e.tensor_tensor(out=Lsum[:hs, :nq], in0=Lsum[:hs, :nq], in1=Lp[:hs, :nq], op=ALU.add),
                                         reads=[Lb, BLs], writes=[BLs])
                            yield
                            ar, arb, _ = Ars.next()
                            S.op("dve", lambda e: e.tensor_tensor(out=ar[:hs, :nq], in0=ps[si][:hs, :nq], in1=zs[:hs, :nq], op=ALU.add),
                                 reads=[pb[si], zsb], writes=[arb])
                            yield
                            A, Ab, _ = As.next()
                            S.op("act", lambda e: e.activation(out=A[:hs, :nq], in_=ar[:hs, :nq], func=AF.Exp, scale=-1.0), reads=[arb], writes=[Ab])
                            if mask is not None:
                                S.op("dve", lambda e: e.tensor_tensor(out=A[:hs, :nq], in0=A[:hs, :nq], in1=mask, op=ALU.mult), reads=[Ab], writes=[Ab])
                            yield
                            S.op("pe", lambda e: e.matmul(ps[Oi][:64, :nq], lhsT=vv, rhs=A[:hs, :nq], start=(idx == 0), stop=(idx == nb - 1)),
                                 reads=[Ab] + list(rb), writes=[pb[Oi]])

                        run_pipelined((unit(i_, b_) for i_, b_ in enumerate(blocks)), 8)
                        m, mb, mi = msb.next()
                        S.op("act", lambda e: e.activation(out=m[:, :nq], in_=ps[Oi][:64, :nq], func=AF.Copy), reads=[pb[Oi]], writes=[mb])
                        banks.unpin(Oi)
                        S.dma("sp", f"ms{mi}", lambda e: e.dma_start(out=mix_dst(64, 64, out_cols, nq), in_=m[:, :nq]), reads=[mb], writes=[Bmsw[mi]])

                    def diff_attend(q1, q2, nq, blocks, out_cols):
                        O1 = banks.get(pin=True)
                        O2 = banks.get(pin=True)
                        nb = len(blocks)

                        def unit(idx, blk):
                            k1T, k2T, vv, hs, mask, rb = blk
                            a1 = banks.get()
                            S.op("pe", lambda e: e.matmul(ps[a1][:hs, :nq], lhsT=k1T, rhs=q1, start=True, stop=True), reads=rb, writes=[pb[a1]])
                            a2 = banks.get()
                            S.op("pe", lambda e: e.matmul(ps[a2][:hs, :nq], lhsT=k2T, rhs=q2, start=True, stop=True), reads=rb, writes=[pb[a2]])
                            yield
                            Pt, P1b, pi_ = Pc.next()
                            P2b = P2bufs[pi_]
                            P1 = Pt[:, 0:512]
                            P2 = Pt[:, 512:1024]
                            S.op("act", lambda e: e.activation(out=P1[:hs, :nq], in_=ps[a1][:hs, :nq], func=AF.Exp, scale=0.125), reads=[pb[a1]], writes=[P1b])
                            S.op("act", lambda e: e.activation(out=P2[:hs, :nq], in_=ps[a2][:hs, :nq], func=AF.Exp, scale=0.125), reads=[pb[a2]], writes=[P2b])
                            if mask is not None:
                                S.op("pool", lambda e: e.tensor_tensor(out=P1[:hs, :nq], in0=P1[:hs, :nq], in1=mask, op=ALU.mult), reads=[P1b], writes=[P1b])
                                S.op("dve", lambda e: e.tensor_tensor(out=P2[:hs, :nq], in0=P2[:hs, :nq], in1=mask, op=ALU.mult), reads=[P2b], writes=[P2b])
                            yield
                            st_, sp_ = (idx == 0), (idx == nb - 1)
                            S.op("pe", lambda e: e.matmul(ps[O1][:, :nq], lhsT=vv, rhs=P1[:hs, :nq], start=st_, stop=sp_), reads=[P1b] + list(rb), writes=[pb[O1]])
                            S.op("pe", lambda e: e.matmul(ps[O2][:, :nq], lhsT=vv, rhs=P2[:hs, :nq], start=st_, stop=sp_), reads=[P2b] + list(rb), writes=[pb[O2]])
                            if nq == 512 and idx == 0:
                                S.op("dve", lambda e: e.tensor_copy(out=pac[:hs, :], in_=Pt[:hs, :]), reads=[P1b, P2b], writes=[Bpa])
                            elif nq == 512:
                                S.op("dve", lambda e: e.tensor_tensor(out=pac[:hs, :], in0=pac[:hs, :], in1=Pt[:hs, :], op=ALU.add),
                                     reads=[P1b, P2b, Bpa], writes=[Bpa])
                            elif idx == 0:
                                S.op("dve", lambda e: e.tensor_copy(out=pa1[:hs, :nq], in_=P1[:hs, :nq]), reads=[P1b], writes=[Bpa])
                                S.op("dve", lambda e: e.tensor_copy(out=pa2[:hs, :nq], in_=P2[:hs, :nq]), reads=[P2b], writes=[Bpa])
                            else:
                                S.op("dve", lambda e: e.tensor_tensor(out=pa1[:hs, :nq], in0=pa1[:hs, :nq], in1=P1[:hs, :nq], op=ALU.add), reads=[P1b, Bpa], writes=[Bpa])
                                S.op("dve", lambda e: e.tensor_tensor(out=pa2[:hs, :nq], in0=pa2[:hs, :nq], in1=P2[:hs, :nq], op=ALU.add), reads=[P2b, Bpa], writes=[Bpa])

                        run_pipelined((unit(i_, b_) for i_, b_ in enumerate(blocks)), 3)
                        D1 = banks.get()
                        S.op("pe", lambda e: e.matmul(ps[D1][:, :nq], lhsT=ones[:, :], rhs=pa1[:, :nq], start=True, stop=True), reads=[Bpa], writes=[pb[D1]])
                        D2 = banks.get()
                        S.op("pe", lambda e: e.matmul(ps[D2][:, :nq], lhsT=ones[:, :], rhs=pa2[:, :nq], start=True, stop=True), reads=[Bpa], writes=[pb[D2]])
                        S.op("dve", lambda e: e.reciprocal(out=f1[:, :nq], in_=ps[D1][:, :nq]), reads=[pb[D1]], writes=[Bf])
                        S.op("dve", lambda e: e.reciprocal(out=f2[:, :nq], in_=ps[D2][:, :nq]), reads=[pb[D2]], writes=[Bf])
                        S.op("dve", lambda e: e.tensor_tensor(out=f3[:, :nq], in0=ps[O1][:, :nq], in1=f1[:, :nq], op=ALU.mult), reads=[pb[O1], Bf], writes=[Bf])
                        S.op("dve", lambda e: e.tensor_tensor(out=f4[:, :nq], in0=ps[O2][:, :nq], in1=f2[:, :nq], op=ALU.mult), reads=[pb[O2], Bf], writes=[Bf])
                        for bnk in (O1, O2):
                            banks.unpin(bnk)
                        S.op("dve", lambda e: e.scalar_tensor_tensor(out=f3[:, :nq], in0=f4[:, :nq], scalar=nlam_t[:, l:l + 1], in1=f3[:, :nq],
                                                                     op0=ALU.mult, op1=ALU.add), reads=[Bf], writes=[Bf])
                        S.op("act", lambda e: e.activation(out=fsq[:, :nq], in_=f3[:, :nq], func=AF.Square), reads=[Bf], writes=[Bf])
                        ssb = banks.get()
                        S.op("pe", lambda e: e.matmul(ps[ssb][:, :nq], lhsT=ones[:, :], rhs=fsq[:, :nq], start=True, stop=True), reads=[Bf], writes=[pb[ssb]])
                        S.op("act", lambda e: e.activation(out=f1[:, :nq], in_=ps[ssb][:, :nq], func=AF.Ln, scale=1.0 / 128.0, bias=eps_t[:, 0:1]),
                             reads=[pb[ssb], Bf], writes=[Bf])
                        S.op("act", lambda e: e.activation(out=f2[:, :nq], in_=f1[:, :nq], func=AF.Exp, scale=-0.5), reads=[Bf], writes=[Bf])
                        m, mb, mi = mdf.next()
                        S.op("dve", lambda e: e.scalar_tensor_tensor(out=m[:, :nq], in0=f3[:, :nq], scalar=gd_t[:, l:l + 1], in1=f2[:, :nq],
                                                                     op0=ALU.mult, op1=ALU.mult), reads=[Bf], writes=[mb])
                        S.dma("sp", f"md{mi}", lambda e: e.dma_start(out=mix_dst(128, 128, out_cols, nq), in_=m[:, :nq]), reads=[mb], writes=[Bmdw[mi]])

                    for qt in range(16):
                        c0 = qt * 512
                        blocks = []
                        for kb in range(4 * qt + 3, -1, -1):
                            o = kb - 4 * qt
                            mask = ms_sb[:, o * 512:(o + 1) * 512] if o >= 0 else None
                            blocks.append((skTn[:, kb * 128:(kb + 1) * 128], sv[:, kb, :], 128, mask, ()))
                        sb_attend(sqT[:, c0:c0 + 512], 512, blocks, c0)
                        if STOP_AFTER == 33:
                            raise _Stop()
                        blocks = []
                        for kb in range(0, 4 * qt + 4):
                            o = kb - 4 * qt
                            mask = md_sb[:, o * 512:(o + 1) * 512] if o >= 0 else None
                            blocks.append((dkT[0:64, kb * 128:(kb + 1) * 128], dkT[64:128, kb * 128:(kb + 1) * 128], dv[:, kb, :], 128, mask, ()))
                        diff_attend(dqT[0:64, c0:c0 + 512], dqT[64:128, c0:c0 + 512], 512, blocks, c0)
                        if qt % 2 == 1:
                            q = qt // 2
                            ag_chunk(f"agm{l}_{q}", mixT_src[l][0][q * 256:(q + 1) * 256, :], mixT_all[l][0][q * 1024:(q + 1) * 1024, :], Bmsw + Bmdw)
                        if STOP_AFTER == 34:
                            raise _Stop()
                    if STOP_AFTER == 35:
                        raise _Stop()
                    for r in range(4):
                        base = (l * 4 + r)
                        S.dma("pool", "ca0", lambda e: e.dma_start(out=skTp[:], in_=c_sbkT[base * 64:(base + 1) * 64, :]), writes=[Bca[0]])
                        S.op("act", lambda e: e.activation(out=skTp[:], in_=skTp[:], func=AF.Identity, scale=-0.125), reads=[Bca[0]], writes=[Bca[0]])
                        S.dma("pool", "ca1", lambda e: e.dma_start(out=svp[:], in_=c_sbv[base * PAST:(base + 1) * PAST, :].rearrange("(k p) d -> p k d", p=128)),
                              writes=[Bca[1]])
                        S.dma("pool", "ca2", lambda e: e.dma_start(out=dkTp[:], in_=c_dkT[base * 128:(base + 1) * 128, :]), writes=[Bca[2]])
                        S.dma("pool", "ca3", lambda e: e.dma_start(out=dvp[:], in_=c_dv[base * PAST:(base + 1) * PAST, :].rearrange("(k p) d -> p k d", p=128)),
                              writes=[Bca[3]])
                        c0 = SEQ + 32 * r
                        blocks = [(skTn[:, c0:c0 + 32], sv[0:32, 64 + r, :], 32, ms_sb[0:32, 0:32], ())]
                        for kb in range(15, -1, -1):
                            blocks.append((skTp[:, kb * 128:(kb + 1) * 128], svp[:, kb, :], 128, None, (Bca[0], Bca[1])))
                        sb_attend(sqT[:, c0:c0 + 32], 32, blocks, c0)
                        blocks = []
                        for kb in range(16):
                            blocks.append((dkTp[0:64, kb * 128:(kb + 1) * 128], dkTp[64:128, kb * 128:(kb + 1) * 128], dvp[:, kb, :], 128, None,
                                           (Bca[2], Bca[3])))
                        blocks.append((dkT[0:64, c0:c0 + 32], dkT[64:128, c0:c0 + 32], dv[0:32, 64 + r, :], 32, None, ()))
                        diff_attend(dqT[0:64, c0:c0 + 32], dqT[64:128, c0:c0 + 32], 32, blocks, c0)
                    ag_chunk(f"agm{l}_8", mixT_src[l][1], mixT_all[l][1], Bmsw + Bmdw)
                    S.barrier(skip_prefix="ag")

        def schedule():
            if STOP_AFTER == 0:
                return
            for l in range(L):
                head_phase(l)
                if STOP_AFTER == 3:
                    return
                if l == 0:
                    token_phase(0, xT_in, xT_scr, True, gpre(1, 0), hT_src[1], mixT_all[0], hT_all[1], 1)
                else:
                    token_phase(1, xT_scr, yT, True, None, None, mixT_all[1])
                if STOP_AFTER == 5:
                    return
        try:
            schedule()
        except _Stop:
            g.pop_all()
            S.finish()
            return nc, S
        S.finish()
    return nc, S


_CACHE = {}


def _consts():
    j = np.arange(128)[:, None]
    s = np.arange(128)[None, :]
    tri = (j >= s).astype(np.float32)
    t = np.arange(512)[None, :]
    ms = np.zeros((128, 2048), np.float32)
    md = np.zeros((128, 2048), np.float32)
    for o in range(4):
        ks = 128 * o + np.arange(128)[:, None]
        ms[:, o * 512:(o + 1) * 512] = (ks < t)
        md[:, o * 512:(o + 1) * 512] = ((ks // 64) <= (t // 64))
    return {"ones_c": np.ones((128, 128), np.float32), "tri_c": tri, "ident_c": np.eye(128, dtype=np.float32),
            "ms_c": ms, "md_c": md}


def kernel(x_prompt, x_sample, cache_sb_k, cache_sb_v, cache_diff_k, cache_diff_v,
           cache_mem_k, cache_mem_v, state_pool, mem_prompt,
           g_pre, g_post, g_mem, w_in, w_out, pool_w, pool_scale,
           lam_q1, lam_k1, lam_q2, lam_k2, diff_g, wq_m, wk_m, wv_m, wo_m,
           w_gate, w_up, w_down):
    f = lambda a: np.ascontiguousarray(np.asarray(a, dtype=np.float32))
    x_prompt, x_sample = f(x_prompt), f(x_sample)
    cache_sb_k, cache_sb_v, cache_diff_k, cache_diff_v = f(cache_sb_k), f(cache_sb_v), f(cache_diff_k), f(cache_diff_v)
    cache_mem_k, cache_mem_v, state_pool, mem_prompt = f(cache_mem_k), f(cache_mem_v), f(state_pool), f(mem_prompt)
    g_pre, g_post, g_mem, w_in, w_out, pool_w, pool_scale = f(g_pre), f(g_post), f(g_mem), f(w_in), f(w_out), f(pool_w), f(pool_scale)
    lam_q1, lam_k1, lam_q2, lam_k2, diff_g = f(lam_q1), f(lam_k1), f(lam_q2), f(lam_k2), f(diff_g)
    wq_m, wk_m, wv_m, wo_m, w_gate, w_up, w_down = f(wq_m), f(wk_m), f(wv_m), f(wo_m), f(w_gate), f(w_up), f(w_down)

    if "nc" not in _CACHE:
        _CACHE["nc"] = build_program()[0]
    nc = _CACHE["nc"]
    consts = _consts()
    gvec = np.zeros((128, NG), np.float32)
    for l in range(L):
        for i in range(3):
            gvec[:, (l * 3 + i) * 8:(l * 3 + i) * 8 + 8] = g_pre[l, i].reshape(8, 128).T
            gvec[:, 48 + (l * 3 + i) * 8:48 + (l * 3 + i) * 8 + 8] = g_post[l, i].reshape(8, 128).T
        gvec[:, 96 + l] = diff_g[l]
    gmem_b = np.ascontiguousarray(np.broadcast_to(g_mem[:, None, :], (L, 128, D)).reshape(L * 128, D))
    lamv = np.zeros((64, 4 * L), np.float32)
    for l in range(L):
        lamv[:, 4 * l + 0] = lam_q1[l]
        lamv[:, 4 * l + 1] = lam_k1[l]
        lamv[:, 4 * l + 2] = lam_q2[l]
        lamv[:, 4 * l + 3] = lam_k2[l]
    perm = []
    for h in range(4):
        perm += list(range(64 * h, 64 * h + 64)) + list(range(256 + 64 * h, 256 + 64 * h + 64)) + list(range(512 + 128 * h, 512 + 128 * h + 128))
    perm = np.array(perm)
    shared = dict(consts)
    shared.update({
        "w_out_p": w_out[:, perm, :].reshape(L * D, D), "wq": wq_m.reshape(L * D, D), "wk": wk_m.reshape(L * D, D),
        "wv": wv_m.reshape(L * D, D), "wo": wo_m.reshape(L * D, D), "w_gate": w_gate.reshape(L * D, DFF),
        "w_up": w_up.reshape(L * D, DFF), "w_down": w_down.reshape(L * DFF, D), "gvec": gvec, "gmem_b": gmem_b, "lamv": lamv,
    })
    shared = {k: np.ascontiguousarray(v, dtype=np.float32) for k, v in shared.items()}
    xfull = [np.ascontiguousarray(np.concatenate([x_prompt[b].T] + [x_sample[4 * b + r].T for r in range(4)], axis=1)) for b in range(2)]
    in_maps = []
    for c in range(8):
        b, h = c // 4, c % 4
        m = dict(shared)
        m["xT"] = np.ascontiguousarray(np.concatenate([x_prompt[b, TOKP * h:TOKP * (h + 1)].T, x_sample[c].T], axis=1))
        m["xT_full"] = xfull[b]
        cols = (list(range(256 + 64 * h, 256 + 64 * h + 64)) + list(range(512 + 64 * h, 512 + 64 * h + 64)) + list(range(64 * h, 64 * h + 64))
                + list(range(1024 + 128 * h, 1024 + 128 * h + 128)) + list(range(1536 + 128 * h, 1536 + 128 * h + 128))
                + list(range(768 + 64 * h, 768 + 64 * h + 64)) + list(range(2048 + 128 * h, 2048 + 128 * h + 128)))
        m["w_in_h"] = np.ascontiguousarray(w_in[:, :, cols].reshape(L * D, 640))
        m["pool_w_h"] = np.ascontiguousarray(pool_w[:, h].reshape(L * 64, 64))
        pv = np.zeros((64, 32), np.float32)
        for l in range(L):
            pv[:, l] = pool_scale[l, 64 * h:64 * h + 64]
        w = WINDOWS[h]
        pv[:, 2 + h] = 1.0 / w
        tt = np.arange(16)
        pv[:, 8:24] = (w / np.minimum(w, tt + 1))[None, :]
        m["pvec"] = pv
        ss = [4 * b + r for r in range(4)]
        m["c_sbkT"] = np.ascontiguousarray(cache_sb_k[:, ss][:, :, :, h, :].transpose(0, 1, 3, 2).reshape(L * 4 * 64, PAST))
        m["c_sbv"] = np.ascontiguousarray(cache_sb_v[:, ss][:, :, :, h, :].reshape(L * 4 * PAST, 64))
        m["c_dkT"] = np.ascontiguousarray(cache_diff_k[:, ss][:, :, :, h, :].transpose(0, 1, 3, 2).reshape(L * 4 * 128, PAST))
        m["c_dv"] = np.ascontiguousarray(cache_diff_v[:, ss][:, :, :, h, :].reshape(L * 4 * PAST, 128))
        sp = np.zeros((L, 4, 64, 16), np.float32)
        sp[:, :, :, 0:15] = state_pool[:, ss, :, 64 * h:64 * h + 64].transpose(0, 1, 3, 2)
        m["spoolT"] = sp.reshape(L * 4 * 64, 16)
        m["c_memk"] = np.ascontiguousarray(cache_mem_k[:, c].reshape(L * 256, D))
        m["c_memv"] = np.ascontiguousarray(cache_mem_v[:, c].reshape(L * 256, D))
        m["memp"] = np.ascontiguousarray(mem_prompt[b])
        in_maps.append(m)

    if STOP_AFTER is not None and (STOP_AFTER >= 30 or STOP_AFTER in (1, 2, 3)):
        for m in in_maps:
            for k in ("w_out_p", "wq", "wk", "wv", "wo", "w_gate", "w_up", "w_down", "c_memk", "c_memv", "memp", "gmem_b"):
                m[k] = np.zeros((1, 1), np.float32)
    res = run_bass_kernel_spmd(nc, in_maps, core_ids=list(range(8)))
    R = res.results
    y_prompt = np.zeros((2, SEQ, D), np.float32)
    y_sample = np.zeros((8, NS, D), np.float32)
    sbk_p = np.zeros((L, 2, SEQ, 4, 64), np.float32)
    sbv_p = np.zeros((L, 2, SEQ, 4, 64), np.float32)
    dk_p = np.zeros((L, 2, SEQ, 4, 128), np.float32)
    dv_p = np.zeros((L, 2, SEQ, 4, 128), np.float32)
    pool_p = np.zeros((L, 2, 15, 256), np.float32)
    mk_p = np.zeros((L, 2, 256, 4, 256), np.float32)
    mv_p = np.zeros((L, 2, 256, 4, 256), np.float32)
    sbk_s = np.zeros((L, 8, NS, 4, 64), np.float32)
    sbv_s = np.zeros((L, 8, NS, 4, 64), np.float32)
    dk_s = np.zeros((L, 8, NS, 4, 128), np.float32)
    dv_s = np.zeros((L, 8, NS, 4, 128), np.float32)
    pool_s = np.zeros((L, 8, 15, 256), np.float32)
    for c in range(8):
        b, h = c // 4, c % 4
        r = R[c]
        yt = np.asarray(r["yT"])
        y_prompt[b, TOKP * h:TOKP * (h + 1)] = yt[:, :TOKP].T
        y_sample[c] = yt[:, TOKP:].T
        skT = np.asarray(r["skT_o"]).reshape(L, 64, HC)
        dkT = np.asarray(r["dkT_o"]).reshape(L, 128, HC)
        svo = np.asarray(r["sv_o"]).reshape(L, HC, 64)
        dvo = np.asarray(r["dv_o"]).reshape(L, HC, 128)
        po = np.asarray(r["pool_o"]).reshape(L, 64, 75)
        sbk_p[:, b, :, h, :] = skT[:, :, :SEQ].transpose(0, 2, 1)
        dk_p[:, b, :, h, :] = dkT[:, :, :SEQ].transpose(0, 2, 1)
        sbv_p[:, b, :, h, :] = svo[:, :SEQ]
        dv_p[:, b, :, h, :] = dvo[:, :SEQ]
        pool_p[:, b, :, 64 * h:64 * h + 64] = po[:, :, 0:15].transpose(0, 2, 1)
        for rr in range(4):
            s = 4 * b + rr
            sl = slice(SEQ + 32 * rr, SEQ + 32 * rr + 32)
            sbk_s[:, s, :, h, :] = skT[:, :, sl].transpose(0, 2, 1)
            dk_s[:, s, :, h, :] = dkT[:, :, sl].transpose(0, 2, 1)
            sbv_s[:, s, :, h, :] = svo[:, sl]
            dv_s[:, s, :, h, :] = dvo[:, sl]
            pool_s[:, s, :, 64 * h:64 * h + 64] = po[:, :, 15 + 15 * rr:30 + 15 * rr].transpose(0, 2, 1)
        if h == 0:
            mk_p[:, b] = np.asarray(r["memk_o"]).reshape(L, 256, 4, 256)
            mv_p[:, b] = np.asarray(r["memv_o"]).reshape(L, 256, 4, 256)
    return (y_prompt, y_sample, sbk_p, sbv_p, dk_p, dv_p, pool_p, mk_p, mv_p, sbk_s, sbv_s, dk_s, dv_s, pool_s)
```

```python
import math
import contextlib
import numpy as np
import concourse.bass as bass
import concourse.mybir as mybir
from concourse.bass_utils import run_bass_kernel_spmd

F32 = mybir.dt.float32
BF16 = mybir.dt.bfloat16
AF = mybir.ActivationFunctionType
ALU = mybir.AluOpType
AX = mybir.AxisListType

L = 2
D = 1024
KC = 8
SEQ = 8192
NS = 32
PAST = 2048
TOKP = 2048
NT = TOKP + NS
HC = SEQ + 4 * NS
DFF = 2816
FC = 22
EPS = 1e-6
WINDOWS = (2, 4, 8, 16)
GROUPS = [[0, 1, 2, 3], [4, 5, 6, 7]]
NG = 98
WSLOT = 4096
STOP_AFTER = None
DBG = ''


class _Stop(Exception):
    pass


class Buf:
    __slots__ = ("last_w", "readers")

    def __init__(self):
        self.last_w = None
        self.readers = {}


class Sched:
    def __init__(self, nc):
        self.nc = nc
        self.E = {"pe": nc.tensor, "act": nc.scalar, "dve": nc.vector, "pool": nc.gpsimd, "sp": nc.sync}
        self.esem = {}
        self.ecnt = {}
        for e in ("pe", "act", "dve", "pool"):
            self.esem[e] = nc.alloc_semaphore("se_" + e)
            self.ecnt[e] = 0
        self.csem = {}
        self.ccnt = {}
        self.known = {e: {} for e in self.E}
        self.pending = {e: None for e in self.E}
        self.nops = 0

    def buf(self):
        return Buf()

    def _waits(self, eng, reads, writes, skip_waw=False):
        need = {}

        def add(tok, raw):
            if tok is None:
                return
            if tok[0] == "e" and tok[1] == eng:
                if eng == "pe":
                    return
            k = (tok[0], tok[1])
            if need.get(k, 0) < tok[2]:
                need[k] = tok[2]

        for b in reads:
            add(b.last_w, True)
        for b in writes:
            if not skip_waw:
                add(b.last_w, False)
            for k, v in b.readers.items():
                add((k[0], k[1], v), False)
        if self.pending[eng] is not None:
            for tok in self.pending[eng]:
                if tok[0] == "e" and tok[1] == eng:
                    continue
                k = (tok[0], tok[1])
                if need.get(k, 0) < tok[2]:
                    need[k] = tok[2]
            self.pending[eng] = None
        kn = self.known[eng]
        for k, v in need.items():
            if kn.get(k, 0) >= v:
                continue
            kn[k] = v
            sem = self.esem[k[1]] if k[0] == "e" else self.csem[k[1]]
            self.E[eng].wait_ge(sem, v)

    def _record(self, tok, reads, writes):
        k = (tok[0], tok[1])
        for b in reads:
            if b.readers.get(k, 0) < tok[2]:
                b.readers[k] = tok[2]
        for b in writes:
            b.last_w = tok
            b.readers = {}

    def op(self, eng, fn, reads=(), writes=()):
        self._waits(eng, reads, writes)
        ins = fn(self.E[eng])
        self.ecnt[eng] += 1
        ins.then_inc(self.esem[eng], 1)
        self._record(("e", eng, self.ecnt[eng]), reads, writes)
        self.nops += 1

    def dma(self, q, chan, fn, reads=(), writes=(), skip_waw=False, inc=16):
        if chan not in self.csem:
            self.csem[chan] = self.nc.alloc_semaphore("sc_" + chan)
            self.ccnt[chan] = 0
        self._waits(q, reads, writes, skip_waw)
        ins = fn(self.E[q])
        self.ccnt[chan] += inc
        ins.then_inc(self.csem[chan], inc)
        self._record(("c", chan, self.ccnt[chan]), reads, writes)
        self.nops += 1

    def barrier(self, skip_prefix=None):
        toks = [("e", e, n) for e, n in self.ecnt.items() if n > 0]
        toks += [("c", c, n) for c, n in self.ccnt.items() if n > 0 and not (skip_prefix and c.startswith(skip_prefix))]
        for e in self.E:
            self.pending[e] = list(toks)

    def finish(self):
        self.barrier()
        self._waits("sp", (), ())


_SLOT_UID = [0]


class Slots:
    def __init__(self, nc, S, stack, name, n, shape, dt):
        _SLOT_UID[0] += 1
        self.t = [stack.enter_context(nc.sbuf_tensor(f"sl_{name}_{_SLOT_UID[0]}_{i}", list(shape), dt)) for i in range(n)]
        self.b = [S.buf() for _ in range(n)]
        self.i = 0
        self.n = n

    def next(self):
        i = self.i
        self.i = (i + 1) % self.n
        return self.t[i], self.b[i], i


class Banks:
    def __init__(self, nc, S, stack):
        self.t = [stack.enter_context(nc.psum_tensor(f"ps{i}", [128, 512], F32)) for i in range(8)]
        self.b = [S.buf() for _ in range(8)]
        self.pinned = set()
        self.rr = 0

    def get(self, pin=False):
        for _ in range(16):
            i = self.rr
            self.rr = (self.rr + 1) % 8
            if i not in self.pinned:
                if pin:
                    self.pinned.add(i)
                return i
        raise RuntimeError("no psum bank")

    def unpin(self, i):
        self.pinned.discard(i)


def build_program():
    nc = bass.Bass("TRN2", target_bir_lowering=False)
    S = Sched(nc)

    small = STOP_AFTER is not None and (STOP_AFTER >= 30 or STOP_AFTER in (1, 2, 3))
    BIG = ("w_out_p", "wq", "wk", "wv", "wo", "w_gate", "w_up", "w_down", "c_memk", "c_memv", "memp", "gmem_b")

    def din(name, shape):
        if small and name in BIG:
            shape = [1, 1]
        return nc.dram_tensor(name, list(shape), F32, kind="ExternalInput").ap()

    def dout(name, shape):
        return nc.dram_tensor(name, list(shape), F32, kind="ExternalOutput").ap()

    def dint(name, shape):
        return nc.dram_tensor(name, list(shape), F32).ap()

    xT_in = din("xT", [D, NT])
    xT_full = din("xT_full", [D, HC])
    w_in_h = din("w_in_h", [L * D, 640])
    pool_w_h = din("pool_w_h", [L * 64, 64])
    pvec_d = din("pvec", [64, 32])
    w_out_p = din("w_out_p", [L * D, D])
    wq_d = din("wq", [L * D, D])
    wk_d = din("wk", [L * D, D])
    wv_d = din("wv", [L * D, D])
    wo_d = din("wo", [L * D, D])
    wg_d = din("w_gate", [L * D, DFF])
    wu_d = din("w_up", [L * D, DFF])
    wd_d = din("w_down", [L * DFF, D])
    gvec_d = din("gvec", [128, NG])
    gmem_d = din("gmem_b", [L * 128, D])
    lamv_d = din("lamv", [64, 4 * L])
    c_sbkT = din("c_sbkT", [L * 4 * 64, PAST])
    c_sbv = din("c_sbv", [L * 4 * PAST, 64])
    c_dkT = din("c_dkT", [L * 4 * 128, PAST])
    c_dv = din("c_dv", [L * 4 * PAST, 128])
    spoolT = din("spoolT", [L * 4 * 64, 16])
    c_memk = din("c_memk", [L * 256, D])
    c_memv = din("c_memv", [L * 256, D])
    memp_d = din("memp", [256, D])
    ones_d = din("ones_c", [128, 128])
    tri_d = din("tri_c", [128, 128])
    ident_d = din("ident_c", [128, 128])
    ms_d = din("ms_c", [128, 2048])
    md_d = din("md_c", [128, 2048])
    yT = dout("yT", [D, NT])
    skT_o = dout("skT_o", [L * 64, HC])
    dkT_o = dout("dkT_o", [L * 128, HC])
    sv_o = dout("sv_o", [L * HC, 64])
    dv_o = dout("dv_o", [L * HC, 128])
    pool_o = dout("pool_o", [L * 64, 75])
    memk_o = dout("memk_o", [L * 256, D])
    memv_o = dout("memv_o", [L * 256, D])
    hT_src = [(dint(f"hT_src{l}", [8 * D, 256]), dint(f"hT_srcs{l}", [D, NS])) for l in range(L)]
    hT_all = [(dint(f"hT_all{l}", [8 * 4 * D, 256]), dint(f"hT_alls{l}", [4 * D, NS])) for l in range(L)]
    mixT_src = [(dint(f"mixT_src{l}", [8 * 256, 1024]), dint(f"mixT_srcs{l}", [256, 4 * NS])) for l in range(L)]
    mixT_all = [(dint(f"mixT_all{l}", [8 * 1024, 1024]), dint(f"mixT_alls{l}", [1024, 4 * NS])) for l in range(L)]
    xT_scr = dint("xT_scr", [D, NT])

    TILES = [(0, 512, "p"), (512, 512, "p"), (1024, 512, "p"), (1536, 512, "p"), (2048, NS, "s")]

    with contextlib.ExitStack() as g:
        uid = [0]

        def sbt(stack, name, shape, dt=F32):
            uid[0] += 1
            return stack.enter_context(nc.sbuf_tensor(f"sb_{name}_{uid[0]}", list(shape), dt))

        banks = Banks(nc, S, g)
        ps = banks.t
        pb = banks.b

        ones = sbt(g, "ones", [128, 128], BF16)
        ones_f = sbt(g, "ones_f", [128, 128], F32)
        triT = sbt(g, "triT", [128, 128], BF16)
        ident_f = sbt(g, "ident_f", [128, 128], F32)
        ms_sb = sbt(g, "ms_sb", [128, 2048], BF16)
        md_sb = sbt(g, "md_sb", [128, 2048], BF16)
        gvec = sbt(g, "gvec", [128, NG], F32)
        pvec_sb = sbt(g, "pvec_sb", [64, 32], F32)
        lamv = sbt(g, "lamv", [64, 4 * L], F32)
        eps_t = sbt(g, "eps_t", [128, 1], F32)
        one_t = sbt(g, "one_t", [128, 1], F32)
        gd_t = sbt(g, "gd_t", [128, L], F32)
        nlam_t = sbt(g, "nlam_t", [128, L], F32)
        Bc = S.buf()
        S.dma("pool", "const", lambda e: e.dma_start(out=ones[:], in_=ones_d), writes=[Bc], skip_waw=True)
        S.dma("pool", "const", lambda e: e.dma_start(out=triT[:], in_=tri_d), writes=[Bc], skip_waw=True)
        S.dma("pool", "const", lambda e: e.dma_start(out=ms_sb[:], in_=ms_d), writes=[Bc], skip_waw=True)
        S.dma("pool", "const", lambda e: e.dma_start(out=md_sb[:], in_=md_d), writes=[Bc], skip_waw=True)
        S.dma("sp", "const2", lambda e: e.dma_start(out=ones_f[:], in_=ones_d), writes=[Bc], skip_waw=True)
        S.dma("sp", "const2", lambda e: e.dma_start(out=ident_f[:], in_=ident_d), writes=[Bc], skip_waw=True)
        S.dma("sp", "const2", lambda e: e.dma_start(out=gvec[:], in_=gvec_d), writes=[Bc], skip_waw=True)
        S.dma("sp", "const2", lambda e: e.dma_start(out=pvec_sb[:], in_=pvec_d), writes=[Bc], skip_waw=True)
        S.dma("sp", "const2", lambda e: e.dma_start(out=lamv[:], in_=lamv_d), writes=[Bc], skip_waw=True)
        S.op("dve", lambda e: e.memset(eps_t[:], EPS), writes=[Bc])
        S.op("dve", lambda e: e.memset(one_t[:], 1.0), writes=[Bc])
        S.barrier()
        with contextlib.ExitStack() as st0:
            prods = sbt(st0, "prods", [64, 2 * L], F32)
            ev = sbt(st0, "ev", [128, 2 * L], F32)
            Bp = S.buf()
            for l in range(L):
                S.op("dve", lambda e: e.tensor_tensor(out=prods[:, 2 * l:2 * l + 1], in0=lamv[:, 4 * l:4 * l + 1],
                                                      in1=lamv[:, 4 * l + 1:4 * l + 2], op=ALU.mult), writes=[Bp])
                S.op("dve", lambda e: e.tensor_tensor(out=prods[:, 2 * l + 1:2 * l + 2], in0=lamv[:, 4 * l + 2:4 * l + 3],
                                                      in1=lamv[:, 4 * l + 3:4 * l + 4], op=ALU.mult), writes=[Bp])
            bi = banks.get()
            S.op("pe", lambda e: e.matmul(ps[bi][:, 0:2 * L], lhsT=ones_f[0:64, :], rhs=prods[:, :], start=True, stop=True),
                 reads=[Bp], writes=[pb[bi]])
            Be = S.buf()
            S.op("act", lambda e: e.activation(out=ev[:], in_=ps[bi][:, 0:2 * L], func=AF.Exp), reads=[pb[bi]], writes=[Be])
            for l in range(L):
                lam_init = 0.8 - 0.6 * math.exp(-0.3 * l)
                S.op("dve", lambda e: e.tensor_tensor(out=nlam_t[:, l:l + 1], in0=ev[:, 2 * l + 1:2 * l + 2],
                                                      in1=ev[:, 2 * l:2 * l + 1], op=ALU.subtract), reads=[Be], writes=[Bc])
                S.op("dve", lambda e: e.tensor_scalar(out=nlam_t[:, l:l + 1], in0=nlam_t[:, l:l + 1], scalar1=-lam_init,
                                                      scalar2=None, op0=ALU.add), reads=[Bc], writes=[Bc])
                S.op("dve", lambda e: e.tensor_scalar(out=gd_t[:, l:l + 1], in0=gvec[:, 96 + l:97 + l], scalar1=1.0 - lam_init,
                                                      scalar2=None, op0=ALU.mult), writes=[Bc])
            S.barrier()

        pid4 = nc.gpsimd.partition_id() % 4
        dyn_off = {}
        for c0_ in (0, 1024):
            dyn_off[c0_] = nc.gpsimd.snap(pid4 * 2048 + c0_, min_val=c0_, max_val=3 * 2048 + c0_)
        dyn_off[2048] = nc.gpsimd.snap(pid4 * NS, min_val=0, max_val=3 * NS)

        def gpre(l, i):
            return (l * 3 + i) * 8

        def gpost(l, i):
            return 48 + (l * 3 + i) * 8

        ag_bufs = {}
        ag_byq = {}

        def ag_chunk(chan, src_ap, dst_ap, reads):
            b_ = S.buf()
            ag_bufs.setdefault(chan.split("_")[0], []).append(b_)
            ag_byq[chan] = b_
            S.dma("pool", chan, lambda e: e.collective_compute("AllGather", ALU.bypass, replica_groups=GROUPS, ins=[src_ap], outs=[dst_ap]),
                  reads=reads, writes=[b_], inc=1)

        def token_phase(l, x_src, x_dst, do_sub, next_g, hT_dst, mix_all, hT_gat=None, ag_l=0):
            Bhs = S.buf()
            with contextlib.ExitStack() as sc:
                wsl = Slots(nc, S, sc, "wsl", 4, [128, WSLOT], BF16)
                stat = {"bank": None}
                if do_sub:
                    mkT_p = sbt(sc, "mkT_p", [128, 8, 256], BF16)
                    mv_p = sbt(sc, "mv_p", [128, 2, D], BF16)
                    mkT_s = sbt(sc, "mkT_s", [128, 8, 256], BF16)
                    mv_s = sbt(sc, "mv_s", [128, 2, D], BF16)
                    Bmk = S.buf()

                def wload(W, r0, kcn, c0, ncols):
                    t, b, i = wsl.next()
                    view = t[:, 0:kcn * ncols].rearrange("p (k n) -> p k n", k=kcn)
                    src = W[r0:r0 + kcn * 128, c0:c0 + ncols].rearrange("(k p) n -> p k n", p=128)
                    S.dma("pool", f"w{i}", lambda e: e.dma_start(out=view, in_=src), writes=[b])
                    return view, b

                deferred = []

                def run_blocks(blocks, prefetch=3):
                    loaded = []
                    nxt = 0
                    for i, blk in enumerate(blocks):
                        while nxt < len(blocks) and nxt < i + prefetch:
                            W, r0, kcn, c0, ncols, _ = blocks[nxt]
                            loaded.append(wload(W, r0, kcn, c0, ncols))
                            nxt += 1
                        view, b = loaded[i]
                        blk[5](view, b)
                        for d_ in list(deferred):
                            d_[0] -= 1
                            if d_[0] <= 0:
                                deferred.remove(d_)
                                d_[1]()
                    for d_ in list(deferred):
                        deferred.remove(d_)
                        d_[1]()

                if do_sub:
                    with contextlib.ExitStack() as sm:
                        mp = sbt(sm, "mp", [128, 2, D], F32)
                        Bmp = S.buf()
                        sqt = sbt(sm, "sqt", [128, D], F32)
                        ss = sbt(sm, "ss", [128, 2], F32)
                        rm = sbt(sm, "rm", [128, 2], F32)
                        gm = sbt(sm, "gm", [128, D], F32)
                        hm = sbt(sm, "hm", [128, 2, D], F32)
                        Bhm = S.buf()
                        hmT = sbt(sm, "hmT", [128, 8, 256], BF16)
                        BhmT = S.buf()
                        mkn = sbt(sm, "mkn", [128, 2, D], F32)
                        Bmkn = S.buf()
                        mvn = sbt(sm, "mvn", [128, 2, D], F32)
                        Bmvn = S.buf()
                        ck = sbt(sm, "ck", [128, 2, D], F32)
                        Bck = S.buf()
                        S.dma("sp", "me", lambda e: e.dma_start(out=mp[:], in_=memp_d.rearrange("(i p) d -> p i d", p=128)), writes=[Bmp])
                        Bgm = S.buf()
                        S.dma("sp", "me2", lambda e: e.dma_start(out=gm[:], in_=gmem_d[l * 128:(l + 1) * 128, :]), writes=[Bgm])
                        S.dma("sp", "ck", lambda e: e.dma_start(out=ck[:], in_=c_memk[l * 256:(l + 1) * 256, :].rearrange("(i p) d -> p i d", p=128)),
                              writes=[Bck])
                        S.dma("pool", "cv", lambda e: e.dma_start(out=mv_s[:], in_=c_memv[l * 256:(l + 1) * 256, :].rearrange("(i p) d -> p i d", p=128)),
                              writes=[S.buf()])
                        Bss = S.buf()
                        for i in range(2):
                            S.op("dve", lambda e: e.tensor_tensor(out=sqt[:], in0=mp[:, i, :], in1=mp[:, i, :], op=ALU.mult), reads=[Bmp], writes=[Bss])
                            S.op("dve", lambda e: e.reduce_sum(out=ss[:, i:i + 1], in_=sqt[:], axis=AX.X), reads=[Bss], writes=[Bss])
                        S.op("act", lambda e: e.activation(out=rm[:], in_=ss[:], func=AF.Ln, scale=1.0 / D, bias=eps_t[:, 0:1]), reads=[Bss], writes=[Bss])
                        S.op("act", lambda e: e.activation(out=rm[:], in_=rm[:], func=AF.Exp, scale=-0.5), reads=[Bss], writes=[Bss])
                        for i in range(2):
                            S.op("dve", lambda e: e.scalar_tensor_tensor(out=hm[:, i, :], in0=mp[:, i, :], scalar=rm[:, i:i + 1], in1=gm[:],
                                                                         op0=ALU.mult, op1=ALU.mult), reads=[Bmp, Bss, Bgm], writes=[Bhm])

                        def transpose_to(src, srcb, dst, dstb):
                            for c8 in range(8):
                                bi = banks.get()
                                for i in range(2):
                                    S.op("pe", lambda e: e.transpose(ps[bi][:, i * 128:(i + 1) * 128], src[:, i, c8 * 128:(c8 + 1) * 128], ident_f[:]),
                                         reads=[srcb], writes=[pb[bi]])
                                S.op("act", lambda e: e.activation(out=dst[:, c8, :], in_=ps[bi][:, 0:256], func=AF.Copy), reads=[pb[bi]], writes=[dstb])

                        transpose_to(hm, Bhm, hmT, BhmT)
                        transpose_to(ck, Bck, mkT_s, Bmk)
                        blocks = []

                        def mk_block(dstn, dstb, cb):
                            def f(view, wb):
                                for i in range(2):
                                    bi = banks.get()
                                    for kc in range(8):
                                        S.op("pe", lambda e: e.matmul(ps[bi][:, :], lhsT=hmT[:, kc, i * 128:(i + 1) * 128], rhs=view[:, kc, :],
                                                                      start=(kc == 0), stop=(kc == 7)), reads=[BhmT, wb], writes=[pb[bi]])
                                    S.op("act", lambda e: e.activation(out=dstn[:, i, cb * 512:(cb + 1) * 512], in_=ps[bi][:, :], func=AF.Copy),
                                         reads=[pb[bi]], writes=[dstb])
                            return f
                        for cb in range(2):
                            blocks.append((wk_d, l * D, 8, cb * 512, 512, mk_block(mkn, Bmkn, cb)))
                        for cb in range(2):
                            blocks.append((wv_d, l * D, 8, cb * 512, 512, mk_block(mvn, Bmvn, cb)))
                        run_blocks(blocks)
                        S.dma("sp", "mo0", lambda e: e.dma_start(out=memk_o[l * 256:(l + 1) * 256, :].rearrange("(i p) d -> p i d", p=128), in_=mkn[:]),
                              reads=[Bmkn])
                        S.dma("sp", "mo1", lambda e: e.dma_start(out=memv_o[l * 256:(l + 1) * 256, :].rearrange("(i p) d -> p i d", p=128), in_=mvn[:]),
                              reads=[Bmvn])
                        transpose_to(mkn, Bmkn, mkT_p, Bmk)
                        for i in range(2):
                            S.op("dve", lambda e: e.tensor_copy(out=mv_p[:, i, :], in_=mvn[:, i, :]), reads=[Bmvn], writes=[Bmk])
                        S.barrier(skip_prefix="ag")

                def make_ctx(W, tiles_, tag):
                    stat = {"bank": None}
                    sg_bufs = {}
                    Bhs = S.buf()
                    xT = sbt(sc, "xTt", [128, 8, W], F32)
                    Bx = [S.buf() for _ in range(8)]
                    osb = sbt(sc, "osb", [128, 8, W], F32)
                    Bo = [S.buf() for _ in range(8)]
                    hT = sbt(sc, "hTt", [128, 8, W], BF16)
                    Bh = [S.buf() for _ in range(8)]
                    sqs = Slots(nc, S, sc, "sqs", 2, [128, W], BF16)
                    lnt = sbt(sc, "lnt", [128, W], F32)
                    Bln = S.buf()
                    rstd = sbt(sc, "rstd", [128, W], F32)
                    Brs = S.buf()
                    if do_sub:
                        mixT = sbt(sc, "mixTt", [128, 8, W], BF16)
                        Bm = S.buf()
                        qT = sbt(sc, "qTt", [128, 8, W], BF16)
                        Bq = [S.buf() for _ in range(8)]
                        oT = sbt(sc, "oTt", [128, 8, W], BF16)
                        Boo = [S.buf() for _ in range(8)]
                        actT = sbt(sc, "actT", [128, FC, W], BF16)
                        Ba = [S.buf() for _ in range(FC)]
                        Psl = Slots(nc, S, sc, "Psl", 2, [128, 2, W], BF16)
                        rdn = Slots(nc, S, sc, "rdn", 2, [128, W], F32)
                        sgs = Slots(nc, S, sc, "sgs", 2, [128, 4, W], F32)
                    def stats_add(src_ap, tt, n, src_bufs):
                        sq, sqb, _ = sqs.next()
                        S.op("act", lambda e: e.activation(out=sq[:, :tt], in_=src_ap, func=AF.Square), reads=src_bufs, writes=[sqb])
                        if n == 0:
                            stat["bank"] = banks.get(pin=True)
                        bi = stat["bank"]
                        S.op("pe", lambda e: e.matmul(ps[bi][:, :tt], lhsT=ones[:, :], rhs=sq[:, :tt], start=(n == 0), stop=(n == 7)),
                             reads=[sqb], writes=[pb[bi]])

                    def rstd_compute(tt):
                        bi = stat["bank"]
                        S.op("act", lambda e: e.activation(out=lnt[:, :tt], in_=ps[bi][:, :tt], func=AF.Ln, scale=1.0 / D, bias=eps_t[:, 0:1]),
                             reads=[pb[bi]], writes=[Bln])
                        S.op("act", lambda e: e.activation(out=rstd[:, :tt], in_=lnt[:, :tt], func=AF.Exp, scale=-0.5),
                             reads=[Bln], writes=[Brs])
                        banks.unpin(bi)
                        stat["bank"] = None

                    def sub_out(n, bi, tt):
                        S.op("act", lambda e: e.activation(out=osb[:, n, :tt], in_=ps[bi][:, :tt], func=AF.Copy), reads=[pb[bi]], writes=[Bo[n]])
                        stats_add(ps[bi][:, :tt], tt, n, [pb[bi]])

                    def post_norm(gb, tt):
                        rstd_compute(tt)
                        for n in range(8):
                            S.op("dve", lambda e: e.tensor_tensor(out=osb[:, n, :tt], in0=osb[:, n, :tt], in1=rstd[:, :tt], op=ALU.mult),
                                 reads=[Bo[n], Brs], writes=[Bo[n]])
                        for n in range(8):
                            S.op("dve", lambda e: e.scalar_tensor_tensor(out=xT[:, n, :tt], in0=osb[:, n, :tt], scalar=gvec[:, gb + n:gb + n + 1],
                                                                         in1=xT[:, n, :tt], op0=ALU.mult, op1=ALU.add),
                                 reads=[Bo[n], Bx[n]], writes=[Bx[n]])

                    def pre_norm(gb, tt, dst, dstb):
                        for n in range(8):
                            stats_add(xT[:, n, :tt], tt, n, [Bx[n]])
                        rstd_compute(tt)
                        for n in range(8):
                            S.op("dve", lambda e: e.scalar_tensor_tensor(out=dst[:, n, :tt], in0=xT[:, n, :tt], scalar=gvec[:, gb + n:gb + n + 1],
                                                                         in1=rstd[:, :tt], op0=ALU.mult, op1=ALU.mult),
                                 reads=[Bx[n], Brs], writes=[dstb[n]])

                    blocks = []
                    for (c0, tt, kind) in tiles_:
                        def load_tile(c0=c0, tt=tt, kind=kind):
                            S.dma("sp", "xT" + tag, lambda e: e.dma_start(out=xT[:, :, :tt], in_=x_src[:, c0:c0 + tt].rearrange("(k p) n -> p k n", p=128)),
                                  writes=Bx)
                            if do_sub:
                                if kind == "p":
                                    off = dyn_off[(c0 // 1024) * 1024]
                                    cc = c0 % 1024
                                    src = mix_all[0][bass.ds(off, 1024), cc:cc + tt].rearrange("(c p) n -> p c n", p=128)
                                else:
                                    src = mix_all[1].rearrange("(c p) n -> p c n", p=128)[:, :, bass.ds(dyn_off[2048], tt)]
                                S.dma("pool", "mx" + tag, lambda e: e.dma_start(out=mixT[:, :, :tt], in_=src), reads=ag_bufs.get(f"agm{l}", []), writes=[Bm])

                        def end_tile(c0=c0, tt=tt):
                            if next_g is not None:
                                pre_norm(next_g, tt, osb, Bo)
                                if tt == 512:
                                    for hf in range(2):
                                        q = c0 // 256 + hf
                                        S.dma("sp", "hs" + tag, lambda e: e.dma_start(out=hT_dst[0][q * D:(q + 1) * D, :].rearrange("(k p) n -> p k n", p=128),
                                                                                in_=osb[:, :, 256 * hf:256 * (hf + 1)]), reads=Bo, writes=[Bhs], skip_waw=True)
                                    for hf in range(2):
                                        q = c0 // 256 + hf
                                        deferred.append([3, lambda q=q: ag_chunk(f"agh{ag_l}_{q}", hT_dst[0][q * D:(q + 1) * D, :], hT_gat[0][q * 4 * D:(q + 1) * 4 * D, :], [Bhs])])
                                else:
                                    S.dma("sp", "hs" + tag, lambda e: e.dma_start(out=hT_dst[1].rearrange("(k p) n -> p k n", p=128), in_=osb[:, :, :tt]),
                                          reads=Bo, writes=[Bhs], skip_waw=True)
                                    deferred.append([3, lambda: ag_chunk(f"agh{ag_l}_8", hT_dst[1], hT_gat[1], [Bhs])])
                            if x_dst is not None:
                                S.dma("sp", "xs" + tag, lambda e: e.dma_start(out=x_dst[:, c0:c0 + tt].rearrange("(k p) n -> p k n", p=128), in_=xT[:, :, :tt]),
                                      reads=Bx)

                        if not do_sub:
                            load_tile()
                            end_tile()
                            continue

                        first = [True]

                        def proj_block(rhsT, rhsb, cb, tt, consume, after=None, pre=None):
                            def f(view, wb):
                                if pre is not None:
                                    pre()
                                for nl in range(4):
                                    n = cb * 4 + nl
                                    bi = banks.get()
                                    for kc in range(8):
                                        S.op("pe", lambda e: e.matmul(ps[bi][:, :tt], lhsT=view[:, kc, nl * 128:(nl + 1) * 128], rhs=rhsT[:, kc, :tt],
                                                                      start=(kc == 0), stop=(kc == 7)), reads=[rhsb[kc], wb], writes=[pb[bi]])
                                    consume(n, bi)
                                if after is not None:
                                    after()
                            return f

                        def attn_core(tt=tt, kind=kind):
                            mkT = mkT_p if kind == "p" else mkT_s
                            mvv = mv_p if kind == "p" else mv_s
                            for hm_ in range(4):
                                sc_ = [banks.get(), banks.get()]
                                for mc in range(2):
                                    for dc in range(2):
                                        S.op("pe", lambda e: e.matmul(ps[sc_[mc]][:, :tt], lhsT=mkT[:, 2 * hm_ + dc, mc * 128:(mc + 1) * 128],
                                                                      rhs=qT[:, 2 * hm_ + dc, :tt], start=(dc == 0), stop=(dc == 1)),
                                             reads=[Bmk, Bq[2 * hm_ + dc]], writes=[pb[sc_[mc]]])
                                P, Pb, _ = Psl.next()
                                for mc in range(2):
                                    S.op("act", lambda e: e.activation(out=P[:, mc, :tt], in_=ps[sc_[mc]][:, :tt], func=AF.Exp, scale=1.0 / 16.0),
                                         reads=[pb[sc_[mc]]], writes=[Pb])
                                dn = banks.get()
                                for mc in range(2):
                                    S.op("pe", lambda e: e.matmul(ps[dn][:, :tt], lhsT=ones[:, :], rhs=P[:, mc, :tt], start=(mc == 0), stop=(mc == 1)),
                                         reads=[Pb], writes=[pb[dn]])
                                ob = [banks.get(), banks.get()]
                                for dc in range(2):
                                    for mc in range(2):
                                        S.op("pe", lambda e: e.matmul(ps[ob[dc]][:, :tt], lhsT=mvv[:, mc, hm_ * 256 + dc * 128:hm_ * 256 + (dc + 1) * 128],
                                                                      rhs=P[:, mc, :tt], start=(mc == 0), stop=(mc == 1)),
                                             reads=[Bmk, Pb], writes=[pb[ob[dc]]])
                                rd, rdb, _ = rdn.next()
                                S.op("dve", lambda e: e.reciprocal(out=rd[:, :tt], in_=ps[dn][:, :tt]), reads=[pb[dn]], writes=[rdb])
                                for dc in range(2):
                                    S.op("dve", lambda e: e.tensor_tensor(out=oT[:, 2 * hm_ + dc, :tt], in0=ps[ob[dc]][:, :tt], in1=rd[:, :tt], op=ALU.mult),
                                         reads=[pb[ob[dc]], rdb], writes=[Boo[2 * hm_ + dc]])

                        for cb in range(2):
                            blocks.append((w_out_p, l * D, 8, cb * 512, 512,
                                           proj_block(mixT, [Bm] * 8, cb, tt, lambda n, bi, tt=tt: sub_out(n, bi, tt),
                                                      after=(lambda tt=tt: (post_norm(gpost(l, 0), tt), pre_norm(gpre(l, 1), tt, hT, Bh))) if cb == 1 else None,
                                                      pre=load_tile if cb == 0 else None)))
                        def q_consume(n, bi, tt=tt):
                            S.op("act", lambda e: e.activation(out=qT[:, n, :tt], in_=ps[bi][:, :tt], func=AF.Copy), reads=[pb[bi]], writes=[Bq[n]])
                        for cb in range(2):
                            blocks.append((wq_d, l * D, 8, cb * 512, 512,
                                           proj_block(hT, Bh, cb, tt, q_consume, after=attn_core if cb == 1 else None)))
                        for cb in range(2):
                            blocks.append((wo_d, l * D, 8, cb * 512, 512,
                                           proj_block(oT, Boo, cb, tt, lambda n, bi, tt=tt: sub_out(n, bi, tt),
                                                      after=(lambda tt=tt: (post_norm(gpost(l, 1), tt), pre_norm(gpre(l, 2), tt, hT, Bh))) if cb == 1 else None)))
                        gslot = {}
                        for cb in range(6):
                            ncols = 512 if cb < 5 else 256

                            def gate_f(view, wb, cb=cb, ncols=ncols, tt=tt):
                                sg, _sgb0, si_ = sgs.next()
                                sgb = sg_bufs.setdefault(si_, [S.buf() for _ in range(4)])
                                gslot[cb] = (sg, sgb)
                                for jl in range(ncols // 128):
                                    bi = banks.get()
                                    for kc in range(8):
                                        S.op("pe", lambda e: e.matmul(ps[bi][:, :tt], lhsT=view[:, kc, jl * 128:(jl + 1) * 128], rhs=hT[:, kc, :tt],
                                                                      start=(kc == 0), stop=(kc == 7)), reads=[Bh[kc], wb], writes=[pb[bi]])
                                    S.op("act", lambda e: e.activation(out=sg[:, jl, :tt], in_=ps[bi][:, :tt], func=AF.Silu), reads=[pb[bi]], writes=[sgb[jl]])

                            def up_f(view, wb, cb=cb, ncols=ncols, tt=tt):
                                sg, sgb = gslot[cb]
                                for jl in range(ncols // 128):
                                    j = cb * 4 + jl
                                    bi = banks.get()
                                    for kc in range(8):
                                        S.op("pe", lambda e: e.matmul(ps[bi][:, :tt], lhsT=view[:, kc, jl * 128:(jl + 1) * 128], rhs=hT[:, kc, :tt],
                                                                      start=(kc == 0), stop=(kc == 7)), reads=[Bh[kc], wb], writes=[pb[bi]])
                                    S.op("dve", lambda e: e.tensor_tensor(out=actT[:, j, :tt], in0=ps[bi][:, :tt], in1=sg[:, jl, :tt], op=ALU.mult),
                                         reads=[pb[bi], sgb[jl]], writes=[Ba[j]])
                            blocks.append((wg_d, l * D, 8, cb * 512, ncols, gate_f))
                            blocks.append((wu_d, l * D, 8, cb * 512, ncols, up_f))
                        for n in range(8):
                            def down_f(view, wb, n=n, tt=tt, end_tile=end_tile):
                                bi = banks.get()
                                for j in range(FC):
                                    S.op("pe", lambda e: e.matmul(ps[bi][:, :tt], lhsT=view[:, j, :], rhs=actT[:, j, :tt],
                                                                  start=(j == 0), stop=(j == FC - 1)), reads=[Ba[j], wb], writes=[pb[bi]])
                                sub_out(n, bi, tt)
                                if n == 7:
                                    post_norm(gpost(l, 2), tt)
                                    end_tile()
                            blocks.append((wd_d, l * DFF, FC, n * 128, 128, down_f))
                    return blocks

                if do_sub:
                    bm = make_ctx(512, TILES[:4], "m")
                    bs = make_ctx(NS, TILES[4:], "s")
                    nz = len(bs)
                    merged = bm[:len(bm) - nz]
                    for k_ in range(nz):
                        a_, b_ = bm[len(bm) - nz + k_], bs[k_]
                        assert a_[:5] == b_[:5]
                        merged.append(a_[:5] + ((lambda view, wb, fa=a_[5], fb=b_[5]: (fa(view, wb), fb(view, wb))),))
                    run_blocks(merged)
                else:
                    make_ctx(512, TILES, "m")
                S.barrier(skip_prefix="ag")

        def head_phase(l):
            hall = hT_all[l]

            def mix_dst(row0, nrows, col, n):
                if col < SEQ:
                    q, cc = col // 1024, col % 1024
                    return mixT_src[l][0][q * 256 + row0:q * 256 + row0 + nrows, cc:cc + n]
                return mixT_src[l][1][row0:row0 + nrows, col - SEQ:col - SEQ + n]
            with contextlib.ExitStack() as sc:
                sqT = sbt(sc, "sqT", [64, HC], BF16)
                skTn = sbt(sc, "skTn", [64, HC], BF16)
                dqT = sbt(sc, "dqT", [128, HC], BF16)
                dkT = sbt(sc, "dkT", [128, HC], BF16)
                sv = sbt(sc, "svr", [128, 68, 64], BF16)
                dv = sbt(sc, "dvr", [128, 68, 128], BF16)
                with contextlib.ExitStack() as sp:
                    w_sb = sbt(sp, "w_sb", [128, 8, 640], BF16)
                    Bw = S.buf()
                    pw_sb = sbt(sp, "pw_sb", [64, 64], BF16)
                    hTs = Slots(nc, S, sp, "hTs", 2, [128, 8, 512], BF16)
                    kst = Slots(nc, S, sp, "kst", 2, [128, 512], F32)
                    vst = Slots(nc, S, sp, "vst", 2, [128, 4, 192], F32)
                    uts = Slots(nc, S, sp, "uts", 2, [64, 528], F32)
                    us = sbt(sp, "us", [64, 4, 48], F32)
                    Bus = S.buf()
                    s1 = sbt(sp, "s1", [64, 528], F32)
                    s2 = sbt(sp, "s2", [64, 528], F32)
                    s3 = sbt(sp, "s3", [64, 528], F32)
                    s4 = sbt(sp, "s4", [64, 528], F32)
                    acc = sbt(sp, "acc", [64, 512], F32)
                    dT = sbt(sp, "dT", [64, 512], BF16)
                    Bsc = S.buf()
                    mps = Slots(nc, S, sp, "mps", 2, [64, 512], F32)
                    S.dma("pool", "wi", lambda e: e.dma_start(out=w_sb[:], in_=w_in_h[l * D:(l + 1) * D, :].rearrange("(k p) n -> p k n", p=128)),
                          writes=[Bw])
                    Bpw = S.buf()
                    S.dma("pool", "wi2", lambda e: e.dma_start(out=pw_sb[:], in_=pool_w_h[l * 64:(l + 1) * 64, :]), writes=[Bpw])
                    S.op("dve", lambda e: e.memset(us[:], 0.0), writes=[Bus])
                    if 'c' not in DBG:
                      S.dma("sp", "spl", lambda e: e.dma_start(out=us[:, :, 1:16],
                                                             in_=spoolT[l * 256:(l + 1) * 256, 0:15].rearrange("(r p) n -> p r n", p=64)),
                            writes=[Bus])

                    if l == 0:
                        xfs = Slots(nc, S, sp, "xfs", 2, [128, 8, 512], F32)
                        nsq = Slots(nc, S, sp, "nsq", 2, [128, 512], BF16)
                        nln = sbt(sp, "nln", [128, 512], F32)
                        nrs = sbt(sp, "nrs", [128, 512], F32)
                        Bnr = S.buf()

                    def load_x(i):
                        xf, xb_, si = xfs.next()
                        nt_ = 512 if i < 16 else 128
                        S.dma("sp", f"xf{si}", lambda e: e.dma_start(out=xf[:, :, :nt_], in_=xT_full[:, 512 * i:512 * i + nt_].rearrange("(k p) n -> p k n", p=128)),
                              writes=[xb_])
                        return xf, xb_, nt_

                    def norm_x(xl):
                        xf, xb_, nt_ = xl
                        t, b, si = hTs.next()
                        bi = banks.get(pin=True)
                        for n in range(8):
                            sq, sqb, _ = nsq.next()
                            S.op("act", lambda e: e.activation(out=sq[:, :nt_], in_=xf[:, n, :nt_], func=AF.Square), reads=[xb_], writes=[sqb])
                            S.op("pe", lambda e: e.matmul(ps[bi][:, :nt_], lhsT=ones[:, :], rhs=sq[:, :nt_], start=(n == 0), stop=(n == 7)),
                                 reads=[sqb], writes=[pb[bi]])
                        S.op("act", lambda e: e.activation(out=nln[:, :nt_], in_=ps[bi][:, :nt_], func=AF.Ln, scale=1.0 / D, bias=eps_t[:, 0:1]),
                             reads=[pb[bi]], writes=[Bnr])
                        banks.unpin(bi)
                        S.op("act", lambda e: e.activation(out=nrs[:, :nt_], in_=nln[:, :nt_], func=AF.Exp, scale=-0.5), reads=[Bnr], writes=[Bnr])
                        gb = gpre(0, 0)
                        for n in range(8):
                            S.op("dve", lambda e: e.scalar_tensor_tensor(out=t[:, n, :nt_], in0=xf[:, n, :nt_], scalar=gvec[:, gb + n:gb + n + 1],
                                                                         in1=nrs[:, :nt_], op0=ALU.mult, op1=ALU.mult),
                                 reads=[xb_, Bnr], writes=[b])
                        return t, b

                    def load_h(i):
                        t, b, si = hTs.next()
                        if i < 16:
                            r, c0 = i // 4, (i % 4) * 512
                            for hf in range(2):
                                q = c0 // 256 + hf
                                row = q * 4 * D + r * D
                                S.dma("pool", f"hT{si}", lambda e: e.dma_start(
                                    out=t[:, :, 256 * hf:256 * (hf + 1)], in_=hall[0][row:row + D, :].rearrange("(k p) n -> p k n", p=128)),
                                    reads=[ag_byq[f"agh{l}_{q}"]], writes=[b], skip_waw=True)
                        else:
                            for r in range(4):
                                S.dma("pool", f"hT{si}", lambda e: e.dma_start(
                                    out=t[:, :, 32 * r:32 * r + 32], in_=hall[1][r * D:(r + 1) * D, :].rearrange("(k p) n -> p k n", p=128)),
                                    reads=[ag_byq[f"agh{l}_8"]], writes=[b], skip_waw=True)
                        return t, b

                    def pool_scan(u_ap, nt, first, out_cols, ub):
                        n = 16 + nt
                        S.op("dve", lambda e: e.tensor_tensor(out=s1[:, 1:n], in0=u_ap[:, 1:n], in1=u_ap[:, 0:n - 1], op=ALU.add), reads=[ub], writes=[Bsc])
                        S.op("dve", lambda e: e.tensor_tensor(out=s2[:, 3:n], in0=s1[:, 3:n], in1=s1[:, 1:n - 2], op=ALU.add), reads=[Bsc], writes=[Bsc])
                        S.op("dve", lambda e: e.tensor_tensor(out=s3[:, 7:n], in0=s2[:, 7:n], in1=s2[:, 3:n - 4], op=ALU.add), reads=[Bsc], writes=[Bsc])
                        S.op("dve", lambda e: e.tensor_tensor(out=s4[:, 15:n], in0=s3[:, 15:n], in1=s3[:, 7:n - 8], op=ALU.add), reads=[Bsc], writes=[Bsc])
                        S.op("dve", lambda e: e.tensor_scalar(out=acc[:, :nt], in0=s1[:, 16:n], scalar1=pvec_sb[:, 2:3], scalar2=None, op0=ALU.mult),
                             reads=[Bsc], writes=[Bsc])
                        for k, sk_ in enumerate((s2, s3, s4)):
                            S.op("dve", lambda e: e.scalar_tensor_tensor(out=acc[:, :nt], in0=sk_[:, 16:n], scalar=pvec_sb[:, 3 + k:4 + k], in1=acc[:, :nt],
                                                                         op0=ALU.mult, op1=ALU.add), reads=[Bsc], writes=[Bsc])
                        if first:
                            S.op("dve", lambda e: e.tensor_tensor(out=acc[:, 0:16], in0=acc[:, 0:16], in1=pvec_sb[:, 8:24], op=ALU.mult), reads=[Bsc], writes=[Bsc])
                        S.op("dve", lambda e: e.tensor_tensor(out=dT[:, :nt], in0=acc[:, :nt], in1=u_ap[:, 16:n], op=ALU.subtract), reads=[Bsc, ub], writes=[Bsc])
                        bi = banks.get()
                        S.op("pe", lambda e: e.matmul(ps[bi][0:64, :nt], lhsT=pw_sb[:, :], rhs=dT[:, :nt], start=True, stop=True),
                             reads=[Bsc, Bpw], writes=[pb[bi]])
                        m, mb, mi = mps.next()
                        S.op("dve", lambda e: e.tensor_scalar(out=m[:, :nt], in0=ps[bi][0:64, :nt], scalar1=pvec_sb[:, l:l + 1], scalar2=None, op0=ALU.mult),
                             reads=[pb[bi]], writes=[mb])
                        S.dma("sp", f"mp{mi}", lambda e: e.dma_start(out=mix_dst(0, 64, out_cols, nt), in_=m[:, :nt]), reads=[mb])

                    prev_u = None
                    if l == 0:
                        xq = [load_x(0), load_x(1)]
                        hcur = norm_x(xq[0])
                    else:
                        nxt = load_h(0)
                    for i in range(17):
                        if l == 0:
                            hT, hb = hcur
                            if i < 16:
                                hcur = norm_x(xq[(i + 1) % 2])
                                if i + 2 <= 16:
                                    xq[i % 2] = load_x(i + 2)
                        else:
                            hT, hb = nxt
                            if i < 16:
                                nxt = load_h(i + 1)
                        nt = 512 if i < 16 else 128
                        t0 = 512 * i
                        u_t, u_b, _ = uts.next()
                        for (nm, wc0, M) in (("sq", 0, 64), ("sk", 64, 64), ("u", 128, 64), ("dq", 192, 128), ("dk", 320, 128)):
                            if 'e' in DBG:
                                continue
                            if 'd' in DBG and nm in ("sk", "dk"):
                                continue
                            bi = banks.get()
                            for kc in range(8):
                                S.op("pe", lambda e: e.matmul(ps[bi][:M, :nt], lhsT=w_sb[:, kc, wc0:wc0 + M], rhs=hT[:, kc, :nt],
                                                              start=(kc == 0), stop=(kc == 7)), reads=[Bw, hb], writes=[pb[bi]])
                            if nm == "sq":
                                S.op("act", lambda e: e.activation(out=sqT[:, t0:t0 + nt], in_=ps[bi][:64, :nt], func=AF.Copy), reads=[pb[bi]])
                            elif nm == "dq":
                                S.op("act", lambda e: e.activation(out=dqT[:, t0:t0 + nt], in_=ps[bi][:, :nt], func=AF.Copy), reads=[pb[bi]])
                            elif nm == "sk":
                                k_, kb_, ki = kst.next()
                                S.op("act", lambda e: e.activation(out=k_[:64, :nt], in_=ps[bi][:64, :nt], func=AF.Copy), reads=[pb[bi]], writes=[kb_])
                                S.op("dve", lambda e: e.tensor_scalar(out=skTn[:, t0:t0 + nt], in0=k_[:64, :nt], scalar1=-0.125, scalar2=None, op0=ALU.mult),
                                     reads=[kb_])
                                if 'h' not in DBG:
                                    S.dma("sp", f"ks{ki}", lambda e: e.dma_start(out=skT_o[l * 64:(l + 1) * 64, t0:t0 + nt], in_=k_[:64, :nt]), reads=[kb_])
                            elif nm == "dk":
                                k_, kb_, ki = kst.next()
                                S.op("act", lambda e: e.activation(out=k_[:, :nt], in_=ps[bi][:, :nt], func=AF.Copy), reads=[pb[bi]], writes=[kb_])
                                S.op("dve", lambda e: e.tensor_copy(out=dkT[:, t0:t0 + nt], in_=k_[:, :nt]), reads=[kb_])
                                if 'h' not in DBG:
                                    S.dma("sp", f"ks{ki}", lambda e: e.dma_start(out=dkT_o[l * 128:(l + 1) * 128, t0:t0 + nt], in_=k_[:, :nt]), reads=[kb_])
                            else:
                                if i < 16:
                                    S.op("act", lambda e: e.activation(out=u_t[:, 16:16 + nt], in_=ps[bi][:64, :nt], func=AF.Copy), reads=[pb[bi]], writes=[u_b])
                                else:
                                    for r in range(4):
                                        S.op("act", lambda e: e.activation(out=us[:, r, 16:48], in_=ps[bi][:64, 32 * r:32 * r + 32], func=AF.Copy),
                                             reads=[pb[bi]], writes=[Bus])
                        v_, vb_, vi = vst.next()
                        if 'b' in DBG:
                            pass
                        elif i < 16:
                            for half in range(2):
                                bi = banks.get()
                                for s_ in range(2):
                                    sub = half * 2 + s_
                                    for kc in range(8):
                                        S.op("pe", lambda e: e.matmul(ps[bi][:, s_ * 192:(s_ + 1) * 192], lhsT=hT[:, kc, sub * 128:(sub + 1) * 128],
                                                                      rhs=w_sb[:, kc, 448:640], start=(kc == 0), stop=(kc == 7)), reads=[Bw, hb], writes=[pb[bi]])
                                for s_ in range(2):
                                    sub = half * 2 + s_
                                    blk = 4 * i + sub
                                    S.op("dve", lambda e: e.tensor_copy(out=v_[:, sub, :], in_=ps[bi][:, s_ * 192:(s_ + 1) * 192]), reads=[pb[bi]], writes=[vb_])
                                    S.op("pool", lambda e: e.tensor_copy(out=sv[:, blk, :], in_=v_[:, sub, 0:64]), reads=[vb_])
                                    S.op("pool", lambda e: e.tensor_copy(out=dv[:, blk, :], in_=v_[:, sub, 64:192]), reads=[vb_])
                            S.dma("sp", f"vs{vi}", lambda e: e.dma_start(
                                out=sv_o[l * HC + t0:l * HC + t0 + 512, :].rearrange("(s p) d -> p s d", p=128), in_=v_[:, :, 0:64]), reads=[vb_])
                            S.dma("sp", f"vs{vi}", lambda e: e.dma_start(
                                out=dv_o[l * HC + t0:l * HC + t0 + 512, :].rearrange("(s p) d -> p s d", p=128), in_=v_[:, :, 64:192]), reads=[vb_])
                        else:
                            for r in range(4):
                                bi = banks.get()
                                for kc in range(8):
                                    S.op("pe", lambda e: e.matmul(ps[bi][:32, 0:192], lhsT=hT[:, kc, 32 * r:32 * r + 32], rhs=w_sb[:, kc, 448:640],
                                                                  start=(kc == 0), stop=(kc == 7)), reads=[Bw, hb], writes=[pb[bi]])
                                S.op("dve", lambda e: e.tensor_copy(out=v_[:32, r, :], in_=ps[bi][:32, 0:192]), reads=[pb[bi]], writes=[vb_])
                                S.op("pool", lambda e: e.tensor_copy(out=sv[:32, 64 + r, :], in_=v_[:32, r, 0:64]), reads=[vb_])
                                S.op("pool", lambda e: e.tensor_copy(out=dv[:32, 64 + r, :], in_=v_[:32, r, 64:192]), reads=[vb_])
                            S.dma("sp", f"vs{vi}", lambda e: e.dma_start(
                                out=sv_o[l * HC + SEQ:l * HC + SEQ + 128, :].rearrange("(r p) d -> p r d", p=32), in_=v_[:32, :, 0:64]), reads=[vb_])
                            S.dma("sp", f"vs{vi}", lambda e: e.dma_start(
                                out=dv_o[l * HC + SEQ:l * HC + SEQ + 128, :].rearrange("(r p) d -> p r d", p=32), in_=v_[:32, :, 64:192]), reads=[vb_])
                        if 'a' in DBG:
                            pass
                        elif i < 16:
                            if i == 0:
                                S.op("dve", lambda e: e.memset(u_t[:, 0:16], 0.0), writes=[u_b])
                            else:
                                pu, pub = prev_u
                                S.op("dve", lambda e: e.tensor_copy(out=u_t[:, 0:16], in_=pu[:, 512:528]), reads=[pub], writes=[u_b])
                            pool_scan(u_t, 512, i == 0, t0, u_b)
                            if i == 15:
                                S.dma("sp", "po", lambda e: e.dma_start(out=pool_o[l * 64:(l + 1) * 64, 0:15], in_=u_t[:, 16 + 497:16 + 512]), reads=[u_b])
                            prev_u = (u_t, u_b)
                        else:
                            for r in range(4):
                                pool_scan(us[:, r, :], 32, False, SEQ + 32 * r, Bus)
                                S.dma("sp", "po", lambda e: e.dma_start(out=pool_o[l * 64:(l + 1) * 64, 15 + 15 * r:30 + 15 * r], in_=us[:, r, 33:48]), reads=[Bus])
                        if STOP_AFTER == 31 and i == 0:
                            raise _Stop()
                    S.barrier(skip_prefix="ag")
                    if STOP_AFTER == 32:
                        raise _Stop()
                with contextlib.ExitStack() as sa:
                    Es = Slots(nc, S, sa, "Es", 3, [128, 512], F32)
                    Zs = Slots(nc, S, sa, "Zs", 6, [128, 512], F32)
                    Ars = Slots(nc, S, sa, "Ars", 3, [128, 512], F32)
                    Ls = Slots(nc, S, sa, "Ls", 4, [128, 512], BF16)
                    As = Slots(nc, S, sa, "As", 3, [128, 512], BF16)
                    P1s = Slots(nc, S, sa, "P1s", 3, [128, 512], BF16)
                    P2s = Slots(nc, S, sa, "P2s", 3, [128, 512], BF16)
                    Lsum = sbt(sa, "Lsum", [128, 512], BF16)
                    BLs = S.buf()
                    pac = sbt(sa, "pac", [128, 1024], BF16)
                    pa1 = pac[:, 0:512]
                    pa2 = pac[:, 512:1024]
                    Pc = Slots(nc, S, sa, "Pc", 3, [128, 1024], BF16)
                    P2bufs = [S.buf() for _ in range(3)]
                    Bpa = S.buf()
                    msb = Slots(nc, S, sa, "msb", 2, [64, 512], F32)
                    mdf = Slots(nc, S, sa, "mdf", 2, [128, 512], F32)
                    f1 = sbt(sa, "f1", [128, 512], F32)
                    f2 = sbt(sa, "f2", [128, 512], F32)
                    f3 = sbt(sa, "f3", [128, 512], F32)
                    f4 = sbt(sa, "f4", [128, 512], F32)
                    fsq = sbt(sa, "fsq", [128, 512], BF16)
                    Bf = S.buf()
                    skTp = sbt(sa, "skTp", [64, PAST], BF16)
                    svp = sbt(sa, "svp", [128, 16, 64], BF16)
                    dkTp = sbt(sa, "dkTp", [128, PAST], BF16)
                    dvp = sbt(sa, "dvp", [128, 16, 128], BF16)
                    Bca = [S.buf() for _ in range(4)]
                    nb_sb = sbt(sa, "nb_sb", [128, 2048], BF16)
                    S.op("dve", lambda e: e.tensor_scalar(out=nb_sb[:], in0=ms_sb[:], scalar1=-1.0, scalar2=-30000.0, op0=ALU.add, op1=ALU.mult))
                    Bmsw = [S.buf(), S.buf()]
                    Bmdw = [S.buf(), S.buf()]

                    def run_pipelined(gens, width):
                        active = []
                        it = iter(gens)
                        more = True
                        while True:
                            if more and len(active) < width:
                                try:
                                    active.append(next(it))
                                except StopIteration:
                                    more = False
                            if not active:
                                break
                            for g_ in list(active):
                                try:
                                    next(g_)
                                except StopIteration:
                                    active.remove(g_)

                    def sb_attend(q_ap, nq, blocks, out_cols):
                        Oi = banks.get(pin=True)
                        nb = len(blocks)

                        def unit(idx, blk):
                            kT, vv, hs, mask, rb = blk
                            zi = banks.get()
                            S.op("pe", lambda e: e.matmul(ps[zi][:hs, :nq], lhsT=kT, rhs=q_ap, start=True, stop=True), reads=rb, writes=[pb[zi]])
                            yield
                            zs, zsb, _ = Zs.next()
                            if mask is None:
                                S.op("dve", lambda e: e.tensor_copy(out=zs[:hs, :nq], in_=ps[zi][:hs, :nq]), reads=[pb[zi]], writes=[zsb])
                            else:
                                S.op("dve", lambda e: e.tensor_tensor(out=zs[:hs, :nq], in0=ps[zi][:hs, :nq], in1=mask, op=ALU.add), reads=[pb[zi]], writes=[zsb])
                            yield
                            E, Eb, _ = Es.next()
                            S.op("act", lambda e: e.activation(out=E[:hs, :nq], in_=zs[:hs, :nq], func=AF.Exp, scale=-1.0), reads=[zsb], writes=[Eb])
                            yield
                            Lp, Lb, _ = Ls.next()
                            S.op("act", lambda e: e.activation(out=Lp[:hs, :nq], in_=E[:hs, :nq], func=AF.Ln, bias=one_t[:hs, 0:1]), reads=[Eb], writes=[Lb])
                            yield
                            si = banks.get()
                            S.op("pe", lambda e: e.matmul(ps[si][:hs, :nq], lhsT=triT[:hs, :hs], rhs=Lp[:hs, :nq], start=True, stop=(idx == 0)),
                                 reads=[Lb], writes=[pb[si]])
                            if idx > 0:
                                S.op("pe", lambda e: e.matmul(ps[si][:hs, :nq], lhsT=ones[:, :hs], rhs=Lsum[:, :nq], start=False, stop=True),
                                     reads=[BLs], writes=[pb[si]])
                            if idx < nb - 1:
                                if idx == 0:
                                    if hs < 128:
                                        S.op("dve", lambda e: e.memset(Lsum[:, :nq], 0.0), writes=[BLs])
                                    S.op("dve", lambda e: e.tensor_copy(out=Lsum[:hs, :nq], in_=Lp[:hs, :nq]), reads=[Lb], writes=[BLs])
                                else:
                                    S.op("dve", lambda e: e.tensor_tensor(out=Lsum[:hs, :nq], in0=Lsum[:hs, :nq], in1=Lp[:hs, :nq], op=ALU.add),
                                         reads=[Lb, BLs], writes=[BLs])
                            yield
                            ar, arb, _ = Ars.next()
                            S.op("dve", lambda e: e.tensor_tensor(out=ar[:hs, :nq], in0=ps[si][:hs, :nq], in1=zs[:hs, :nq], op=ALU.add),
                                 reads=[pb[si], zsb], writes=[arb])
                            yield
                            A, Ab, _ = As.next()
                            S.op("act", lambda e: e.activation(out=A[:hs, :nq], in_=ar[:hs, :nq], func=AF.Exp, scale=-1.0), reads=[arb], writes=[Ab])
                            yield
                            S.op("pe", lambda e: e.matmul(ps[Oi][:64, :nq], lhsT=vv, rhs=A[:hs, :nq], start=(idx == 0), stop=(idx == nb - 1)),
                                 reads=[Ab] + list(rb), writes=[pb[Oi]])

                        run_pipelined((unit(i_, b_) for i_, b_ in enumerate(blocks)), 8)
                        m, mb, mi = msb.next()
                        S.op("act", lambda e: e.activation(out=m[:, :nq], in_=ps[Oi][:64, :nq], func=AF.Copy), reads=[pb[Oi]], writes=[mb])
                        banks.unpin(Oi)
                        S.dma("sp", f"ms{mi}", lambda e: e.dma_start(out=mix_dst(64, 64, out_cols, nq), in_=m[:, :nq]), reads=[mb], writes=[Bmsw[mi]])

                    def diff_attend(q1, q2, nq, blocks, out_cols):
                        O1 = banks.get(pin=True)
                        O2 = banks.get(pin=True)
                        nb = len(blocks)

                        def unit(idx, blk):
                            k1T, k2T, vv, hs, mask, rb = blk
                            a1 = banks.get()
                            S.op("pe", lambda e: e.matmul(ps[a1][:hs, :nq], lhsT=k1T, rhs=q1, start=True, stop=True), reads=rb, writes=[pb[a1]])
                            a2 = banks.get()
                            S.op("pe", lambda e: e.matmul(ps[a2][:hs, :nq], lhsT=k2T, rhs=q2, start=True, stop=True), reads=rb, writes=[pb[a2]])
                            yield
                            Pt, P1b, pi_ = Pc.next()
                            P2b = P2bufs[pi_]
                            P1 = Pt[:, 0:512]
                            P2 = Pt[:, 512:1024]
                            S.op("act", lambda e: e.activation(out=P1[:hs, :nq], in_=ps[a1][:hs, :nq], func=AF.Exp, scale=0.125), reads=[pb[a1]], writes=[P1b])
                            S.op("act", lambda e: e.activation(out=P2[:hs, :nq], in_=ps[a2][:hs, :nq], func=AF.Exp, scale=0.125), reads=[pb[a2]], writes=[P2b])
                            if mask is not None:
                                S.op("pool", lambda e: e.tensor_tensor(out=P1[:hs, :nq], in0=P1[:hs, :nq], in1=mask, op=ALU.mult), reads=[P1b], writes=[P1b])
                                S.op("dve", lambda e: e.tensor_tensor(out=P2[:hs, :nq], in0=P2[:hs, :nq], in1=mask, op=ALU.mult), reads=[P2b], writes=[P2b])
                            yield
                            st_, sp_ = (idx == 0), (idx == nb - 1)
                            S.op("pe", lambda e: e.matmul(ps[O1][:, :nq], lhsT=vv, rhs=P1[:hs, :nq], start=st_, stop=sp_), reads=[P1b] + list(rb), writes=[pb[O1]])
                            S.op("pe", lambda e: e.matmul(ps[O2][:, :nq], lhsT=vv, rhs=P2[:hs, :nq], start=st_, stop=sp_), reads=[P2b] + list(rb), writes=[pb[O2]])
                            if nq == 512 and idx == 0:
                                S.op("dve", lambda e: e.tensor_copy(out=pac[:hs, :], in_=Pt[:hs, :]), reads=[P1b, P2b], writes=[Bpa])
                            elif nq == 512:
                                S.op("dve", lambda e: e.tensor_tensor(out=pac[:hs, :], in0=pac[:hs, :], in1=Pt[:hs, :], op=ALU.add),
                                     reads=[P1b, P2b, Bpa], writes=[Bpa])
                            elif idx == 0:
                                S.op("dve", lambda e: e.tensor_copy(out=pa1[:hs, :nq], in_=P1[:hs, :nq]), reads=[P1b], writes=[Bpa])
                                S.op("dve", lambda e: e.tensor_copy(out=pa2[:hs, :nq], in_=P2[:hs, :nq]), reads=[P2b], writes=[Bpa])
                            else:
                                S.op("dve", lambda e: e.tensor_tensor(out=pa1[:hs, :nq], in0=pa1[:hs, :nq], in1=P1[:hs, :nq], op=ALU.add), reads=[P1b, Bpa], writes=[Bpa])
                                S.op("dve", lambda e: e.tensor_tensor(out=pa2[:hs, :nq], in0=pa2[:hs, :nq], in1=P2[:hs, :nq], op=ALU.add), reads=[P2b, Bpa], writes=[Bpa])

                        run_pipelined((unit(i_, b_) for i_, b_ in enumerate(blocks)), 3)
                        D1 = banks.get()
                        S.op("pe", lambda e: e.matmul(ps[D1][:, :nq], lhsT=ones[:, :], rhs=pa1[:, :nq], start=True, stop=True), reads=[Bpa], writes=[pb[D1]])
                        D2 = banks.get()
                        S.op("pe", lambda e: e.matmul(ps[D2][:, :nq], lhsT=ones[:, :], rhs=pa2[:, :nq], start=True, stop=True), reads=[Bpa], writes=[pb[D2]])
                        S.op("dve", lambda e: e.reciprocal(out=f1[:, :nq], in_=ps[D1][:, :nq]), reads=[pb[D1]], writes=[Bf])
                        S.op("dve", lambda e: e.reciprocal(out=f2[:, :nq], in_=ps[D2][:, :nq]), reads=[pb[D2]], writes=[Bf])
                        S.op("dve", lambda e: e.tensor_tensor(out=f3[:, :nq], in0=ps[O1][:, :nq], in1=f1[:, :nq], op=ALU.mult), reads=[pb[O1], Bf], writes=[Bf])
                        S.op("dve", lambda e: e.tensor_tensor(out=f4[:, :nq], in0=ps[O2][:, :nq], in1=f2[:, :nq], op=ALU.mult), reads=[pb[O2], Bf], writes=[Bf])
                        for bnk in (O1, O2):
                            banks.unpin(bnk)
                        S.op("dve", lambda e: e.scalar_tensor_tensor(out=f3[:, :nq], in0=f4[:, :nq], scalar=nlam_t[:, l:l + 1], in1=f3[:, :nq],
                                                                     op0=ALU.mult, op1=ALU.add), reads=[Bf], writes=[Bf])
                        S.op("act", lambda e: e.activation(out=fsq[:, :nq], in_=f3[:, :nq], func=AF.Square), reads=[Bf], writes=[Bf])
                        ssb = banks.get()
                        S.op("pe", lambda e: e.matmul(ps[ssb][:, :nq], lhsT=ones[:, :], rhs=fsq[:, :nq], start=True, stop=True), reads=[Bf], writes=[pb[ssb]])
                        S.op("act", lambda e: e.activation(out=f1[:, :nq], in_=ps[ssb][:, :nq], func=AF.Ln, scale=1.0 / 128.0, bias=eps_t[:, 0:1]),
                             reads=[pb[ssb], Bf], writes=[Bf])
                        S.op("act", lambda e: e.activation(out=f2[:, :nq], in_=f1[:, :nq], func=AF.Exp, scale=-0.5), reads=[Bf], writes=[Bf])
                        m, mb, mi = mdf.next()
                        S.op("dve", lambda e: e.scalar_tensor_tensor(out=m[:, :nq], in0=f3[:, :nq], scalar=gd_t[:, l:l + 1], in1=f2[:, :nq],
                                                                     op0=ALU.mult, op1=ALU.mult), reads=[Bf], writes=[mb])
                        S.dma("sp", f"md{mi}", lambda e: e.dma_start(out=mix_dst(128, 128, out_cols, nq), in_=m[:, :nq]), reads=[mb], writes=[Bmdw[mi]])

                    for qt in range(16):
                        c0 = qt * 512
                        blocks = []
                        for kb in range(4 * qt + 3, -1, -1):
                            o = kb - 4 * qt
                            mask = nb_sb[:, o * 512:(o + 1) * 512] if o >= 0 else None
                            blocks.append((skTn[:, kb * 128:(kb + 1) * 128], sv[:, kb, :], 128, mask, ()))
                        sb_attend(sqT[:, c0:c0 + 512], 512, blocks, c0)
                        if STOP_AFTER == 33:
                            raise _Stop()
                        blocks = []
                        for kb in range(0, 4 * qt + 4):
                            o = kb - 4 * qt
                            mask = md_sb[:, o * 512:(o + 1) * 512] if o >= 0 else None
                            blocks.append((dkT[0:64, kb * 128:(kb + 1) * 128], dkT[64:128, kb * 128:(kb + 1) * 128], dv[:, kb, :], 128, mask, ()))
                        diff_attend(dqT[0:64, c0:c0 + 512], dqT[64:128, c0:c0 + 512], 512, blocks, c0)
                        if qt % 2 == 1:
                            q = qt // 2
                            ag_chunk(f"agm{l}_{q}", mixT_src[l][0][q * 256:(q + 1) * 256, :], mixT_all[l][0][q * 1024:(q + 1) * 1024, :], Bmsw + Bmdw)
                        if STOP_AFTER == 34:
                            raise _Stop()
                    if STOP_AFTER == 35:
                        raise _Stop()
                    for r in range(4):
                        base = (l * 4 + r)
                        S.dma("pool", "ca0", lambda e: e.dma_start(out=skTp[:], in_=c_sbkT[base * 64:(base + 1) * 64, :]), writes=[Bca[0]])
                        S.op("act", lambda e: e.activation(out=skTp[:], in_=skTp[:], func=AF.Identity, scale=-0.125), reads=[Bca[0]], writes=[Bca[0]])
                        S.dma("pool", "ca1", lambda e: e.dma_start(out=svp[:], in_=c_sbv[base * PAST:(base + 1) * PAST, :].rearrange("(k p) d -> p k d", p=128)),
                              writes=[Bca[1]])
                        S.dma("pool", "ca2", lambda e: e.dma_start(out=dkTp[:], in_=c_dkT[base * 128:(base + 1) * 128, :]), writes=[Bca[2]])
                        S.dma("pool", "ca3", lambda e: e.dma_start(out=dvp[:], in_=c_dv[base * PAST:(base + 1) * PAST, :].rearrange("(k p) d -> p k d", p=128)),
                              writes=[Bca[3]])
                        c0 = SEQ + 32 * r
                        blocks = [(skTn[:, c0:c0 + 32], sv[0:32, 64 + r, :], 32, nb_sb[0:32, 0:32], ())]
                        for kb in range(15, -1, -1):
                            blocks.append((skTp[:, kb * 128:(kb + 1) * 128], svp[:, kb, :], 128, None, (Bca[0], Bca[1])))
                        sb_attend(sqT[:, c0:c0 + 32], 32, blocks, c0)
                        blocks = []
                        for kb in range(16):
                            blocks.append((dkTp[0:64, kb * 128:(kb + 1) * 128], dkTp[64:128, kb * 128:(kb + 1) * 128], dvp[:, kb, :], 128, None,
                                           (Bca[2], Bca[3])))
                        blocks.append((dkT[0:64, c0:c0 + 32], dkT[64:128, c0:c0 + 32], dv[0:32, 64 + r, :], 32, None, ()))
                        diff_attend(dqT[0:64, c0:c0 + 32], dqT[64:128, c0:c0 + 32], 32, blocks, c0)
                    ag_chunk(f"agm{l}_8", mixT_src[l][1], mixT_all[l][1], Bmsw + Bmdw)
                    S.barrier(skip_prefix="ag")

        def schedule():
            if STOP_AFTER == 0:
                return
            for l in range(L):
                head_phase(l)
                if STOP_AFTER == 3:
                    return
                if l == 0:
                    token_phase(0, xT_in, xT_scr, True, gpre(1, 0), hT_src[1], mixT_all[0], hT_all[1], 1)
                else:
                    token_phase(1, xT_scr, yT, True, None, None, mixT_all[1])
                if STOP_AFTER == 5:
                    return
        try:
            schedule()
        except _Stop:
            g.pop_all()
            S.finish()
            return nc, S
        S.finish()
    return nc, S


_CACHE = {}


def _consts():
    j = np.arange(128)[:, None]
    s = np.arange(128)[None, :]
    tri = (j >= s).astype(np.float32)
    t = np.arange(512)[None, :]
    ms = np.zeros((128, 2048), np.float32)
    md = np.zeros((128, 2048), np.float32)
    for o in range(4):
        ks = 128 * o + np.arange(128)[:, None]
        ms[:, o * 512:(o + 1) * 512] = (ks < t)
        md[:, o * 512:(o + 1) * 512] = ((ks // 64) <= (t // 64))
    return {"ones_c": np.ones((128, 128), np.float32), "tri_c": tri, "ident_c": np.eye(128, dtype=np.float32),
            "ms_c": ms, "md_c": md}


def kernel(x_prompt, x_sample, cache_sb_k, cache_sb_v, cache_diff_k, cache_diff_v,
           cache_mem_k, cache_mem_v, state_pool, mem_prompt,
           g_pre, g_post, g_mem, w_in, w_out, pool_w, pool_scale,
           lam_q1, lam_k1, lam_q2, lam_k2, diff_g, wq_m, wk_m, wv_m, wo_m,
           w_gate, w_up, w_down):
    f = lambda a: np.ascontiguousarray(np.asarray(a, dtype=np.float32))
    x_prompt, x_sample = f(x_prompt), f(x_sample)
    cache_sb_k, cache_sb_v, cache_diff_k, cache_diff_v = f(cache_sb_k), f(cache_sb_v), f(cache_diff_k), f(cache_diff_v)
    cache_mem_k, cache_mem_v, state_pool, mem_prompt = f(cache_mem_k), f(cache_mem_v), f(state_pool), f(mem_prompt)
    g_pre, g_post, g_mem, w_in, w_out, pool_w, pool_scale = f(g_pre), f(g_post), f(g_mem), f(w_in), f(w_out), f(pool_w), f(pool_scale)
    lam_q1, lam_k1, lam_q2, lam_k2, diff_g = f(lam_q1), f(lam_k1), f(lam_q2), f(lam_k2), f(diff_g)
    wq_m, wk_m, wv_m, wo_m, w_gate, w_up, w_down = f(wq_m), f(wk_m), f(wv_m), f(wo_m), f(w_gate), f(w_up), f(w_down)

    if "nc" not in _CACHE:
        _CACHE["nc"] = build_program()[0]
    nc = _CACHE["nc"]
    consts = _consts()
    gvec = np.zeros((128, NG), np.float32)
    for l in range(L):
        for i in range(3):
            gvec[:, (l * 3 + i) * 8:(l * 3 + i) * 8 + 8] = g_pre[l, i].reshape(8, 128).T
            gvec[:, 48 + (l * 3 + i) * 8:48 + (l * 3 + i) * 8 + 8] = g_post[l, i].reshape(8, 128).T
        gvec[:, 96 + l] = diff_g[l]
    gmem_b = np.ascontiguousarray(np.broadcast_to(g_mem[:, None, :], (L, 128, D)).reshape(L * 128, D))
    lamv = np.zeros((64, 4 * L), np.float32)
    for l in range(L):
        lamv[:, 4 * l + 0] = lam_q1[l]
        lamv[:, 4 * l + 1] = lam_k1[l]
        lamv[:, 4 * l + 2] = lam_q2[l]
        lamv[:, 4 * l + 3] = lam_k2[l]
    perm = []
    for h in range(4):
        perm += list(range(64 * h, 64 * h + 64)) + list(range(256 + 64 * h, 256 + 64 * h + 64)) + list(range(512 + 128 * h, 512 + 128 * h + 128))
    perm = np.array(perm)
    shared = dict(consts)
    shared.update({
        "w_out_p": w_out[:, perm, :].reshape(L * D, D), "wq": wq_m.reshape(L * D, D), "wk": wk_m.reshape(L * D, D),
        "wv": wv_m.reshape(L * D, D), "wo": wo_m.reshape(L * D, D), "w_gate": w_gate.reshape(L * D, DFF),
        "w_up": w_up.reshape(L * D, DFF), "w_down": w_down.reshape(L * DFF, D), "gvec": gvec, "gmem_b": gmem_b, "lamv": lamv,
    })
    shared = {k: np.ascontiguousarray(v, dtype=np.float32) for k, v in shared.items()}
    xfull = [np.ascontiguousarray(np.concatenate([x_prompt[b].T] + [x_sample[4 * b + r].T for r in range(4)], axis=1)) for b in range(2)]
    in_maps = []
    for c in range(8):
        b, h = c // 4, c % 4
        m = dict(shared)
        m["xT"] = np.ascontiguousarray(np.concatenate([x_prompt[b, TOKP * h:TOKP * (h + 1)].T, x_sample[c].T], axis=1))
        m["xT_full"] = xfull[b]
        cols = (list(range(256 + 64 * h, 256 + 64 * h + 64)) + list(range(512 + 64 * h, 512 + 64 * h + 64)) + list(range(64 * h, 64 * h + 64))
                + list(range(1024 + 128 * h, 1024 + 128 * h + 128)) + list(range(1536 + 128 * h, 1536 + 128 * h + 128))
                + list(range(768 + 64 * h, 768 + 64 * h + 64)) + list(range(2048 + 128 * h, 2048 + 128 * h + 128)))
        m["w_in_h"] = np.ascontiguousarray(w_in[:, :, cols].reshape(L * D, 640))
        m["pool_w_h"] = np.ascontiguousarray(pool_w[:, h].reshape(L * 64, 64))
        pv = np.zeros((64, 32), np.float32)
        for l in range(L):
            pv[:, l] = pool_scale[l, 64 * h:64 * h + 64]
        w = WINDOWS[h]
        pv[:, 2 + h] = 1.0 / w
        tt = np.arange(16)
        pv[:, 8:24] = (w / np.minimum(w, tt + 1))[None, :]
        m["pvec"] = pv
        ss = [4 * b + r for r in range(4)]
        m["c_sbkT"] = np.ascontiguousarray(cache_sb_k[:, ss][:, :, :, h, :].transpose(0, 1, 3, 2).reshape(L * 4 * 64, PAST))
        m["c_sbv"] = np.ascontiguousarray(cache_sb_v[:, ss][:, :, :, h, :].reshape(L * 4 * PAST, 64))
        m["c_dkT"] = np.ascontiguousarray(cache_diff_k[:, ss][:, :, :, h, :].transpose(0, 1, 3, 2).reshape(L * 4 * 128, PAST))
        m["c_dv"] = np.ascontiguousarray(cache_diff_v[:, ss][:, :, :, h, :].reshape(L * 4 * PAST, 128))
        sp = np.zeros((L, 4, 64, 16), np.float32)
        sp[:, :, :, 0:15] = state_pool[:, ss, :, 64 * h:64 * h + 64].transpose(0, 1, 3, 2)
        m["spoolT"] = sp.reshape(L * 4 * 64, 16)
        m["c_memk"] = np.ascontiguousarray(cache_mem_k[:, c].reshape(L * 256, D))
        m["c_memv"] = np.ascontiguousarray(cache_mem_v[:, c].reshape(L * 256, D))
        m["memp"] = np.ascontiguousarray(mem_prompt[b])
        in_maps.append(m)

    if STOP_AFTER is not None and (STOP_AFTER >= 30 or STOP_AFTER in (1, 2, 3)):
        for m in in_maps:
            for k in ("w_out_p", "wq", "wk", "wv", "wo", "w_gate", "w_up", "w_down", "c_memk", "c_memv", "memp", "gmem_b"):
                m[k] = np.zeros((1, 1), np.float32)
    res = run_bass_kernel_spmd(nc, in_maps, core_ids=list(range(8)))
    R = res.results
    y_prompt = np.zeros((2, SEQ, D), np.float32)
    y_sample = np.zeros((8, NS, D), np.float32)
    sbk_p = np.zeros((L, 2, SEQ, 4, 64), np.float32)
    sbv_p = np.zeros((L, 2, SEQ, 4, 64), np.float32)
    dk_p = np.zeros((L, 2, SEQ, 4, 128), np.float32)
    dv_p = np.zeros((L, 2, SEQ, 4, 128), np.float32)
    pool_p = np.zeros((L, 2, 15, 256), np.float32)
    mk_p = np.zeros((L, 2, 256, 4, 256), np.float32)
    mv_p = np.zeros((L, 2, 256, 4, 256), np.float32)
    sbk_s = np.zeros((L, 8, NS, 4, 64), np.float32)
    sbv_s = np.zeros((L, 8, NS, 4, 64), np.float32)
    dk_s = np.zeros((L, 8, NS, 4, 128), np.float32)
    dv_s = np.zeros((L, 8, NS, 4, 128), np.float32)
    pool_s = np.zeros((L, 8, 15, 256), np.float32)
    for c in range(8):
        b, h = c // 4, c % 4
        r = R[c]
        yt = np.asarray(r["yT"])
        y_prompt[b, TOKP * h:TOKP * (h + 1)] = yt[:, :TOKP].T
        y_sample[c] = yt[:, TOKP:].T
        skT = np.asarray(r["skT_o"]).reshape(L, 64, HC)
        dkT = np.asarray(r["dkT_o"]).reshape(L, 128, HC)
        svo = np.asarray(r["sv_o"]).reshape(L, HC, 64)
        dvo = np.asarray(r["dv_o"]).reshape(L, HC, 128)
        po = np.asarray(r["pool_o"]).reshape(L, 64, 75)
        sbk_p[:, b, :, h, :] = skT[:, :, :SEQ].transpose(0, 2, 1)
        dk_p[:, b, :, h, :] = dkT[:, :, :SEQ].transpose(0, 2, 1)
        sbv_p[:, b, :, h, :] = svo[:, :SEQ]
        dv_p[:, b, :, h, :] = dvo[:, :SEQ]
        pool_p[:, b, :, 64 * h:64 * h + 64] = po[:, :, 0:15].transpose(0, 2, 1)
        for rr in range(4):
            s = 4 * b + rr
            sl = slice(SEQ + 32 * rr, SEQ + 32 * rr + 32)
            sbk_s[:, s, :, h, :] = skT[:, :, sl].transpose(0, 2, 1)
            dk_s[:, s, :, h, :] = dkT[:, :, sl].transpose(0, 2, 1)
            sbv_s[:, s, :, h, :] = svo[:, sl]
            dv_s[:, s, :, h, :] = dvo[:, sl]
            pool_s[:, s, :, 64 * h:64 * h + 64] = po[:, :, 15 + 15 * rr:30 + 15 * rr].transpose(0, 2, 1)
        if h == 0:
            mk_p[:, b] = np.asarray(r["memk_o"]).reshape(L, 256, 4, 256)
            mv_p[:, b] = np.asarray(r["memv_o"]).reshape(L, 256, 4, 256)
    return (y_prompt, y_sample, sbk_p, sbv_p, dk_p, dv_p, pool_p, mk_p, mv_p, sbk_s, sbv_s, dk_s, dv_s, pool_s)
```
